# Optimizing a Trainium2 kernel written in Bass

```python
import math
import jax, jax.numpy as jnp
from jax import lax
import numpy as np


D_MODEL = 1024
BATCH = 8
SEQ = 4096
DEPTH = 1

CHUNK = 64
Q_BLOCK = 128
MEM_LEN = 256

DA_HEADS = 4
DA_HEAD_DIM = 64
DA_QK_WIDTH = DA_HEADS * 2 * DA_HEAD_DIM
DA_V_WIDTH = DA_HEADS * 2 * DA_HEAD_DIM

GLA_HEADS = 4
GLA_DK = 64
GLA_DV = 128
GLA_K_WIDTH = GLA_HEADS * GLA_DK
GLA_V_WIDTH = GLA_HEADS * GLA_DV
GLA_GATE_RANK = 16
GLA_GATE_NORM = 16.0

IN_SIZES = (DA_QK_WIDTH, DA_QK_WIDTH, DA_V_WIDTH,
            GLA_K_WIDTH, GLA_K_WIDTH, GLA_V_WIDTH, GLA_V_WIDTH, GLA_GATE_RANK)
IN_WIDTH = sum(IN_SIZES)
MIX_WIDTH = DA_V_WIDTH + GLA_V_WIDTH

FFN_HIDDEN = 2816
CROSS_HEADS = 4
CROSS_HEAD_DIM = D_MODEL // CROSS_HEADS

ALPHA = (2.0 * DEPTH) ** 0.25
BETA = (8.0 * DEPTH) ** -0.25
EPS = 1e-5

kernel_name = 'hymba_diffattn_gla_macaron_deepnorm'


def layer_norm(x, g, b):
    xf = x.astype(jnp.float32)
    mu = jnp.mean(xf, axis=-1, keepdims=True)
    var = jnp.mean(jnp.square(xf - mu), axis=-1, keepdims=True)
    y = (xf - mu) * lax.rsqrt(var + EPS)
    return (y * g.astype(jnp.float32) + b.astype(jnp.float32)).astype(x.dtype)


def rms_norm(x, g):
    xf = x.astype(jnp.float32)
    y = xf * lax.rsqrt(jnp.mean(jnp.square(xf), axis=-1, keepdims=True) + EPS)
    return (y * g.astype(jnp.float32)).astype(x.dtype)


def swiglu(x, w_gate, w_up, w_down):
    return (jax.nn.silu(x @ w_gate) * (x @ w_up)) @ w_down


def diff_attention(q, k, v, lam):
    B, S, H, _, d = q.shape
    n_qb = S // Q_BLOCK
    scale = DA_HEAD_DIM ** -0.5
    q_blocks = (q * scale).reshape(B, n_qb, Q_BLOCK, H, 2, d).transpose(1, 0, 3, 4, 2, 5)
    k_t = k.transpose(0, 2, 3, 1, 4)
    v_t = v.transpose(0, 2, 1, 3)
    key_chunk = jnp.arange(S) // CHUNK
    starts = jnp.arange(n_qb) * Q_BLOCK

    def one_block(args):
        q_blk, start = args
        s = jnp.einsum('bhcqd,bhckd->bhcqk', q_blk, k_t).astype(jnp.float32)
        q_chunk = (start + jnp.arange(Q_BLOCK)) // CHUNK
        mask = key_chunk[None, :] <= q_chunk[:, None]
        p = jax.nn.softmax(jnp.where(mask, s, -jnp.inf), axis=-1)
        w = p[:, :, 0] - lam * p[:, :, 1]
        return jnp.einsum('bhqk,bhkv->bhqv', w.astype(v_t.dtype), v_t)

    o = lax.map(one_block, (q_blocks, starts))
    return o.transpose(1, 0, 3, 2, 4).reshape(B, S, H, 2 * d)


def gla_chunk_causal(q, k, v, log_a):
    B, S, H, dk = q.shape
    dv = v.shape[-1]
    n_c = S // CHUNK
    qc = q.reshape(B, n_c, CHUNK, H, dk).astype(jnp.float32)
    kc = k.reshape(B, n_c, CHUNK, H, dk).astype(jnp.float32)
    vc = v.reshape(B, n_c, CHUNK, H, dv).astype(jnp.float32)
    cum = jnp.cumsum(log_a.reshape(B, n_c, CHUNK, H, dk).astype(jnp.float32), axis=2)
    total = cum[:, :, -1]
    k_end = kc * jnp.exp(total[:, :, None] - cum)
    d_state = jnp.einsum('bnchk,bnchv->nbhkv', k_end, vc)
    decay = jnp.exp(total).transpose(1, 0, 2, 3)

    def step(s_prev, inp):
        dec, ds = inp
        s_new = dec[..., None] * s_prev + ds
        return s_new, s_new

    _, states = lax.scan(step, jnp.zeros((B, H, dk, dv), jnp.float32), (decay, d_state))
    o = jnp.einsum('bnchk,nbhkv->bnchv', qc * (GLA_DK ** -0.5), states)
    return o.reshape(B, S, H, dv)


def memory_cross_attention(h, mem, w_q, w_kv, w_o):
    B, S, _ = h.shape
    M = mem.shape[1]
    q = (h @ w_q).reshape(B, S, CROSS_HEADS, CROSS_HEAD_DIM)
    k, v = jnp.split(mem @ w_kv, 2, axis=-1)
    k = k.reshape(B, M, CROSS_HEADS, CROSS_HEAD_DIM)
    v = v.reshape(B, M, CROSS_HEADS, CROSS_HEAD_DIM)
    s = jnp.einsum('bshd,bmhd->bhsm', q, k).astype(jnp.float32) * (CROSS_HEAD_DIM ** -0.5)
    p = jax.nn.softmax(s, axis=-1).astype(v.dtype)
    o = jnp.einsum('bhsm,bmhd->bshd', p, v).reshape(B, S, D_MODEL)
    return o @ w_o


def setup_inputs(seed: int = 0) -> dict:
    key = jax.random.key(seed)
    ks = iter(jax.random.split(key, 32))

    def nrm(shape, scale):
        return jax.random.normal(next(ks), shape, jnp.float32) * scale

    def gain(n):
        return 1.0 + nrm((DEPTH, n), 0.02)

    L, D, F = DEPTH, D_MODEL, FFN_HIDDEN
    return {
        'x': nrm((BATCH, SEQ, D), 1.0),
        'mem': nrm((BATCH, MEM_LEN, D), 1.0),
        'ffn1_w_gate': nrm((L, D, F), D ** -0.5),
        'ffn1_w_up': nrm((L, D, F), D ** -0.5),
        'ffn1_w_down': nrm((L, F, D), BETA * F ** -0.5),
        'ln1_g': gain(D),
        'ln1_b': nrm((L, D), 0.02),
        'w_in': nrm((L, D, IN_WIDTH), D ** -0.5),
        'da_lambda_q1': nrm((L, DA_HEAD_DIM), 0.1),
        'da_lambda_k1': nrm((L, DA_HEAD_DIM), 0.1),
        'da_lambda_q2': nrm((L, DA_HEAD_DIM), 0.1),
        'da_lambda_k2': nrm((L, DA_HEAD_DIM), 0.1),
        'da_norm_g': gain(2 * DA_HEAD_DIM),
        'gla_w_gate2': nrm((L, GLA_GATE_RANK, GLA_K_WIDTH), GLA_GATE_RANK ** -0.5),
        'gla_b_gate': nrm((L, GLA_K_WIDTH), 0.02),
        'gla_norm_g': gain(GLA_DV),
        'w_out': nrm((L, MIX_WIDTH, D), BETA * MIX_WIDTH ** -0.5),
        'ln2_g': gain(D),
        'ln2_b': nrm((L, D), 0.02),
        'cross_wq': nrm((L, D, D), D ** -0.5),
        'cross_wkv': nrm((L, D, 2 * D), D ** -0.5),
        'cross_wo': nrm((L, D, D), BETA * D ** -0.5),
        'ln3_g': gain(D),
        'ln3_b': nrm((L, D), 0.02),
        'ffn2_w_gate': nrm((L, D, F), D ** -0.5),
        'ffn2_w_up': nrm((L, D, F), D ** -0.5),
        'ffn2_w_down': nrm((L, F, D), BETA * F ** -0.5),
        'ln4_g': gain(D),
        'ln4_b': nrm((L, D), 0.02),
    }


def reference(x, mem, ffn1_w_gate, ffn1_w_up, ffn1_w_down, ln1_g, ln1_b, w_in,
              da_lambda_q1, da_lambda_k1, da_lambda_q2, da_lambda_k2, da_norm_g,
              gla_w_gate2, gla_b_gate, gla_norm_g, w_out, ln2_g, ln2_b,
              cross_wq, cross_wkv, cross_wo, ln3_g, ln3_b,
              ffn2_w_gate, ffn2_w_up, ffn2_w_down, ln4_g, ln4_b):
    B, S, _ = x.shape
    split_points = [int(i) for i in np.cumsum(IN_SIZES)[:-1]]
    h = x
    for l in range(DEPTH):
        h = layer_norm(ALPHA * h + 0.5 * swiglu(h, ffn1_w_gate[l], ffn1_w_up[l], ffn1_w_down[l]),
                       ln1_g[l], ln1_b[l])

        q_da, k_da, v_da, q_g, k_g, v_g, r_g, g_lr = jnp.split(h @ w_in[l], split_points, axis=-1)

        lambda_init = 0.8 - 0.6 * math.exp(-0.3 * l)
        lam = (jnp.exp(jnp.sum(da_lambda_q1[l].astype(jnp.float32) * da_lambda_k1[l].astype(jnp.float32)))
               - jnp.exp(jnp.sum(da_lambda_q2[l].astype(jnp.float32) * da_lambda_k2[l].astype(jnp.float32)))
               + lambda_init)
        o_da = diff_attention(q_da.reshape(B, S, DA_HEADS, 2, DA_HEAD_DIM),
                              k_da.reshape(B, S, DA_HEADS, 2, DA_HEAD_DIM),
                              v_da.reshape(B, S, DA_HEADS, 2 * DA_HEAD_DIM), lam)
        o_da = rms_norm(o_da, da_norm_g[l]) * (1.0 - lambda_init)

        log_a = jax.nn.log_sigmoid((g_lr @ gla_w_gate2[l] + gla_b_gate[l]).astype(jnp.float32)) / GLA_GATE_NORM
        o_g = gla_chunk_causal(q_g.reshape(B, S, GLA_HEADS, GLA_DK),
                               k_g.reshape(B, S, GLA_HEADS, GLA_DK),
                               v_g.reshape(B, S, GLA_HEADS, GLA_DV),
                               log_a.reshape(B, S, GLA_HEADS, GLA_DK))
        o_g = rms_norm(o_g.astype(x.dtype), gla_norm_g[l]) * jax.nn.silu(r_g).reshape(B, S, GLA_HEADS, GLA_DV)

        mix = jnp.concatenate([o_da.reshape(B, S, DA_V_WIDTH).astype(x.dtype),
                               o_g.reshape(B, S, GLA_V_WIDTH).astype(x.dtype)], axis=-1) @ w_out[l]
        h = layer_norm(ALPHA * h + mix, ln2_g[l], ln2_b[l])

        c = memory_cross_attention(h, mem, cross_wq[l], cross_wkv[l], cross_wo[l])
        h = layer_norm(ALPHA * h + c, ln3_g[l], ln3_b[l])

        h = layer_norm(ALPHA * h + 0.5 * swiglu(h, ffn2_w_gate[l], ffn2_w_up[l], ffn2_w_down[l]),
                       ln4_g[l], ln4_b[l])
    return h
```

```python
import contextlib
import numpy as np
import ml_dtypes
import concourse.bass as bass
import concourse.mybir as mybir
from concourse.bass_utils import run_bass_kernel_spmd

F32 = mybir.dt.float32
BF16 = mybir.dt.bfloat16
AF = mybir.ActivationFunctionType
ALU = mybir.AluOpType
AX = mybir.AxisListType

NT = 8
TT = 512
SEQ = 4096
D = 1024
FH = 2816
NFC = 22
INW = 3088
ALPHA = 2.0 ** 0.25
EPS = 1e-5
LAMBDA_INIT = 0.8 - 0.6 * 1.0
NR = 4
SLOT = 4096


class Res:
    __slots__ = ("name", "w", "rs", "excl")

    def __init__(self, name, excl=False):
        self.name = name
        self.w = None
        self.rs = []
        self.excl = excl


class Op:
    __slots__ = ("eng", "fn", "deps", "dma", "lane", "needs", "idx", "laneval", "waits")

    def __init__(self, eng, fn, dma, lane):
        self.eng = eng
        self.fn = fn
        self.dma = dma
        self.lane = lane
        self.deps = []
        self.needs = False
        self.idx = None
        self.laneval = None
        self.waits = []


ENGS = ("pe", "act", "dve", "pool", "sp")
NROT = 4


class Sched:
    def __init__(self):
        self.ops = {e: [] for e in ENGS}
        self.lane_tok = {}

    def add(self, eng, fn, reads=(), writes=(), lane=None):
        dma = lane is not None
        op = Op(eng, fn, dma, lane)
        reads = list(reads)
        writes = list(writes)
        if dma:
            if lane not in self.lane_tok:
                self.lane_tok[lane] = Res("lane_" + str(lane))
            writes.append(self.lane_tok[lane])
        raw = set()
        other = set()
        for r in reads:
            if r.w is not None:
                raw.add(r.w)
            if r.excl:
                for q in r.rs:
                    if q.eng != eng:
                        other.add(q)
        for w in writes:
            if w.w is not None:
                other.add(w.w)
            for q in w.rs:
                other.add(q)
        deps = set()
        for d in raw | other:
            if d is op:
                continue
            if (not d.dma) and (not dma) and d.eng == eng:
                if eng == "pe":
                    continue
                if d not in raw:
                    continue
            deps.add(d)
        op.deps = list(deps)
        for d in deps:
            d.needs = True
        for r in reads:
            if r.excl:
                r.rs = [op]
                continue
            if not dma:
                r.rs = [q for q in r.rs if q.dma or q.eng != eng]
            r.rs.append(op)
        for w in writes:
            w.w = op
            w.rs = []
        self.ops[eng].append(op)
        return op

    def finalize(self):
        lanecnt = {}
        for e in ENGS:
            c = 0
            for op in self.ops[e]:
                if op.dma:
                    lanecnt[op.lane] = lanecnt.get(op.lane, 0) + 1
                    op.laneval = 16 * lanecnt[op.lane]
                elif op.needs:
                    c += 1
                    op.idx = c
        self.lanecnt = lanecnt
        for e in ENGS:
            waited = {}
            for op in self.ops[e]:
                best = {}
                for d in op.deps:
                    if d.dma:
                        key = ("lane", d.lane)
                        val = d.laneval
                    else:
                        key = ("eng", d.eng)
                        val = d.idx
                    if waited.get(key, 0) >= val:
                        continue
                    if best.get(key, 0) < val:
                        best[key] = val
                for key, val in best.items():
                    waited[key] = val
                    op.waits.append((key, val))

    def emit(self, nc, stack, final_lanes=()):
        sems = {}
        for e in ENGS:
            for r in range(NROT):
                sems[("eng", e, r)] = stack.enter_context(nc.semaphore("s_%s%d" % (e, r)))
        for lane in self.lanecnt:
            sems[("lane", lane)] = stack.enter_context(nc.semaphore("l_%s" % str(lane)))
        block = stack.enter_context(nc.Block())
        names = {"pe": "tensor", "act": "scalar", "dve": "vector", "pool": "gpsimd", "sp": "sync"}

        def run(e, eng):
            for op in self.ops[e]:
                for key, val in op.waits:
                    if key[0] == "lane":
                        eng.wait_ge(sems[key], val)
                    else:
                        j = val - 1
                        eng.wait_ge(sems[("eng", key[1], j % NROT)], j // NROT + 1)
                inst = op.fn(eng)
                if op.dma:
                    inst.then_inc(sems[("lane", op.lane)], 16)
                elif op.idx is not None:
                    j = op.idx - 1
                    inst.then_inc(sems[("eng", e, j % NROT)], 1)
            if e == "sp":
                for lane in self.lanecnt:
                    eng.wait_ge(sems[("lane", lane)], 16 * self.lanecnt[lane])

        for e in ENGS:
            getattr(block, names[e])(lambda eng, e=e: run(e, eng))


def build_program(nt=NT, dbg=False, stop_after=None, unit_limit=None, skip_lastx=False, xmode=0):
    nc = bass.Bass("TRN2", target_bir_lowering=False, dynamic_dma_scratch_size=8192)
    S = Sched()

    def din(name, shape, dt=F32):
        return nc.dram_tensor(name, list(shape), dt, kind="ExternalInput").ap()

    x_d = din("x", [SEQ, D])
    mem_d = din("mem", [256, D])
    wsrc = {
        "wg1": din("ffn1_w_gate", [D, FH]), "wu1": din("ffn1_w_up", [D, FH]), "wd1": din("ffn1_w_down", [FH, D]),
        "win": din("w_in", [D, INW]), "wout": din("w_out", [D, D]), "wq": din("cross_wq", [D, D]),
        "wkv": din("cross_wkv", [D, 2 * D]), "wo": din("cross_wo", [D, D]),
        "wg2": din("ffn2_w_gate", [D, FH]), "wu2": din("ffn2_w_up", [D, FH]), "wd2": din("ffn2_w_down", [FH, D]),
    }
    lnp_d = din("lnp", [128, 64])
    lamv_d = din("lamv", [128, 256])
    dng_d = din("dng4", [128, 512])
    gng_d = din("gng4", [128, 512])
    w2a_d = din("w2aug", [32, 256])
    identf_d = din("identf", [128, 128])
    identb_d = din("identb", [128, 128], BF16)
    m1_d = din("m1", [128, 128])
    ind_d = din("ind", [128, 2])
    onesb_d = din("onesb", [128, 128], BF16)
    out_d = nc.dram_tensor("out", [SEQ, D], F32, kind="ExternalOutput").ap()
    wbf = {k: nc.dram_tensor("bf_" + k, list(v.shape), BF16, kind="Internal").ap() for k, v in wsrc.items()}
    if dbg:
        dbg_d = nc.dram_tensor("dbg", [128, 8, 512], F32, kind="ExternalOutput").ap()
        dbg2_d = nc.dram_tensor("dbg2", [128, 8, 512], BF16, kind="ExternalOutput").ap()

    with contextlib.ExitStack() as st:
        def sb(n, s, d):
            return st.enter_context(nc.sbuf_tensor(n, list(s), d))

        Kc = sb("Kc", [128, 4, SEQ], BF16)
        Vc = sb("Vc", [128, 32, 4, 129], BF16)
        ring = [sb("ring%d" % i, [128, SLOT], BF16) for i in range(NR)]
        RT = sb("RT", [128, 8, TT], F32)
        hT = sb("hT", [128, 8, TT], BF16)
        HT = sb("HT", [128, NFC, TT], BF16)
        tokbuf = sb("tokbuf", [128, 2048], F32)
        sg = [sb("sg%d" % i, [128, 512], F32) for i in range(2)]
        PT = [sb("PT%d" % i, [128, 512], BF16) for i in range(3)]
        zsq = [sb("zsq%d" % i, [128, 512], BF16) for i in range(2)]
        lnt = sb("lnt", [128, 512], F32)
        rstd = sb("rstd", [128, 512], F32)
        Sst = sb("Sst", [64, 512], F32)
        Sbf = [sb("Sbf%d" % i, [64, 512], BF16) for i in range(2)]
        glr = sb("glr", [32, 512], F32)
        w2a = sb("w2a", [32, 256], F32)
        winglr = sb("winglr", [128, 8, 16], BF16)
        identf = sb("identf_s", [128, 128], F32)
        identb = sb("identb_s", [128, 128], BF16)
        m1 = sb("m1_s", [128, 128], F32)
        ind = sb("ind_s", [128, 2], F32)
        onesb = sb("onesb_s", [128, 128], BF16)
        lnp = sb("lnp_s", [128, 64], F32)
        lnpa = sb("lnpa_s", [128, 64], F32)
        lamv = sb("lamv_s", [128, 256], F32)
        lamt = sb("lamt", [128, 64], F32)
        lams = sb("lams", [128, 8], F32)
        gda4 = sb("gda4", [128, 512], F32)
        gng4 = sb("gng4_s", [128, 512], F32)
        kmT = sb("kmT", [128, 8, 256], BF16)
        Vm = sb("Vm", [128, 2, 4, 257], BF16)
        la = [sb("la%d" % i, [128, 256], F32) for i in range(2)]
        ed = [sb("ed%d" % i, [128, 256], F32) for i in range(2)]
        dec = sb("dec", [64, 4, 8], F32)
        o0 = sb("o0", [128, 4, 128], F32)
        od = sb("od", [128, 4, 128], F32)
        sm = sb("sm", [128, 32], F32)
        banks = [st.enter_context(nc.psum_tensor("pb%d" % i, [128, 512], F32)) for i in range(8)]
        bres = [Res("pb%d" % i, excl=True) for i in range(8)]
        free_banks = list(range(8))

        def take():
            return free_banks.pop(0)

        def free(b):
            free_banks.append(b)

        scr = HT[:].rearrange("p a b -> p (a b)")
        def scr_bf(g0, ng, shape_str=None, **kw):
            v = scr[:, g0 * 512:(g0 + ng) * 512]
            return v.rearrange(shape_str, **kw) if shape_str else v

        def scr_f32(g0, ng, shape_str=None, **kw):
            v = scr[:, g0 * 512:(g0 + ng) * 512].bitcast(F32)
            return v.rearrange(shape_str, **kw) if shape_str else v

        qdT = scr_bf(0, 4, "p (a b) -> p a b", b=512)
        qgT = scr_bf(4, 4, "p (a b) -> p a b", b=512)
        kg_sb = scr_f32(8, 4, "p (a b) -> p a b", b=256)
        vg_sb = scr_bf(12, 4, "p (a b) -> p a b", b=512)
        gate = scr_bf(16, 4, "p (a b) -> p a b", b=512)
        kend = scr_bf(20, 2, "p (a b) -> p a b", b=256)
        qcT = scr_bf(0, 8, "p (a b) -> p a b", b=512)
        PTc = scr_bf(8, 8, "p (a b) -> p a b", b=512)
        tok = [tokbuf[:, i * 1024:(i + 1) * 1024] for i in range(2)]
        ocat = tokbuf[:].bitcast(BF16).rearrange("p (a b) -> p a b", b=1024)

        R = {}
        def res(name):
            if name not in R:
                R[name] = Res(name)
            return R[name]
        rRT = [res("RT%d" % k) for k in range(8)]
        rhT = [res("hT%d" % k) for k in range(8)]
        rG = [res("G%d" % g) for g in range(NFC)]
        rring = [res("ring%d" % i) for i in range(NR)]
        rK = [[res("K%d_%d" % (h, t)) for t in range(NT)] for h in range(4)]
        rV = [res("V%d" % k) for k in range(32)]
        rocat = [res("ocat%d" % s) for s in range(4)]
        rtok = [[rocat[0], rocat[1]], [rocat[2], rocat[3]]]
        rsg = [res("sg%d" % i) for i in range(2)]
        rPT = [res("PT%d" % i) for i in range(3)]
        rzsq = [res("zsq%d" % i) for i in range(2)]
        rSbf = [res("Sbf%d" % i) for i in range(2)]
        rla = [res("la%d" % i) for i in range(2)]
        red = [res("ed%d" % i) for i in range(2)]

        def gr(g0, ng):
            return rG[g0:g0 + ng]

        def MM(out, lhsT, rhs, start, stop, r, w, skip=False):
            if skip:
                return S.add("pe", lambda e: e.matmul(out, lhsT=lhsT, rhs=rhs, start=start, stop=stop, skip_group_check=True), r, w)
            return S.add("pe", lambda e: e.matmul(out, lhsT=lhsT, rhs=rhs, start=start, stop=stop), r, w)

        def TR(out, in_, ident, r, w):
            return S.add("pe", lambda e: e.transpose(out=out, in_=in_, identity=ident), r, w)

        def A(out, in_, func, r, w, scale=None, bias=None):
            kw = {}
            if scale is not None:
                kw["scale"] = scale
            if bias is not None:
                kw["bias"] = bias
            return S.add("act", lambda e: e.activation(out=out, in_=in_, func=func, **kw), r, w)

        def TT_(eng, out, in0, in1, op, r, w):
            return S.add(eng, lambda e: e.tensor_tensor(out=out, in0=in0, in1=in1, op=op), r, w)

        def TS(eng, out, in0, s1, s2, op0, op1, r, w):
            if op1 is None:
                return S.add(eng, lambda e: e.tensor_scalar(out=out, in0=in0, scalar1=s1, scalar2=None, op0=op0), r, w)
            return S.add(eng, lambda e: e.tensor_scalar(out=out, in0=in0, scalar1=s1, scalar2=s2, op0=op0, op1=op1), r, w)

        def STT(out, in0, scalar, in1, op0, op1, r, w):
            return S.add("dve", lambda e: e.scalar_tensor_tensor(out=out, in0=in0, scalar=scalar, in1=in1, op0=op0, op1=op1), r, w)

        def CP(eng, out, in_, r, w):
            return S.add(eng, lambda e: e.tensor_copy(out=out, in_=in_), r, w)

        def RECIP(out, in_, r, w):
            return S.add("dve", lambda e: e.reciprocal(out=out, in_=in_), r, w)

        def RED(out, in_, r, w):
            return S.add("dve", lambda e: e.tensor_reduce(out=out, in_=in_, axis=AX.X, op=ALU.add), r, w)

        def MEMSET(eng, ap, val, r, w):
            return S.add(eng, lambda e: e.memset(ap, val), r, w)

        def DMA(eng, out, in_, r, w, lane):
            return S.add(eng, lambda e: e.dma_start(out=out, in_=in_), r, w, lane=lane)

        cnt = {"lane": 0, "cast": 0}

        def const_load(dst, src, r):
            cnt["lane"] += 1
            DMA("act", dst, src, [], [r], "k%d" % cnt["lane"])

        const_load(identf[:], identf_d, res("identf"))
        const_load(identb[:], identb_d, res("identb"))
        const_load(onesb[:], onesb_d, res("onesb"))
        const_load(lnp[:], lnp_d, res("lnp"))
        const_load(lamv[:], lamv_d, res("lamv"))
        const_load(gda4[:], dng_d, res("gda4"))
        const_load(gng4[:], gng_d, res("gng4"))
        const_load(w2a[:], w2a_d, res("w2a"))
        const_load(m1[:], m1_d, res("m1"))
        const_load(ind[:], ind_d, res("ind"))

        rW = {}

        def cast(name, c0, c1):
            key = (name, c0)
            rW[key] = Res("w_%s_%d" % key)
            lane = "cast%d" % (cnt["cast"] % 8)
            cnt["cast"] += 1
            DMA("pool", wbf[name][:, c0:c1], wsrc[name][:, c0:c1], [], [rW[key]], lane)

        def cast_cols(name, ncols, step=512):
            for c0 in range(0, ncols, step):
                cast(name, c0, min(ncols, c0 + step))

        def cast_ffn(i):
            for c0 in range(0, FH, 512):
                cast("wg%d" % i, c0, min(FH, c0 + 512))
                cast("wu%d" % i, c0, min(FH, c0 + 512))
            cast_cols("wd%d" % i, D)

        cast_cols("wkv", 2 * D)
        cast_ffn(1)
        cast_cols("win", 3072)
        cast("win", 3072, 3088)
        cast_cols("wout", D)
        cast_cols("wq", D)
        cast_cols("wo", D)
        cast_ffn(2)

        units = []

        def u_k8(name, c0, ncols):
            units.append((name, c0, ncols, 0, 8))

        def u_ffn(i):
            for c0 in range(0, FH, 512):
                u_k8("wg%d" % i, c0, min(512, FH - c0))
                u_k8("wu%d" % i, c0, min(512, FH - c0))
            for grp in range(2):
                for part in range(3):
                    nf = 8 if part < 2 else 6
                    units.append(("wd%d" % i, grp * 512, 512, part * 8, nf))

        for c0 in range(0, 2 * D, 512):
            u_k8("wkv", c0, 512)
        for t in range(nt):
            u_ffn(1)
            for c0 in range(0, 3072, 512):
                u_k8("win", c0, 512)
            for nm in ("wout", "wq", "wo"):
                u_k8(nm, 0, 512)
                u_k8(nm, 512, 512)
            u_ffn(2)

        ws = {"issued": 0, "acq": 0, "rel": 0}

        def ws_issue():
            i = ws["issued"]
            if i >= len(units) or (unit_limit is not None and i >= unit_limit):
                return
            name, c0, ncols, f0, nf = units[i]
            slot = i % NR
            src = wbf[name].rearrange("(kc p) n -> p kc n", p=128)[:, f0:f0 + nf, c0:c0 + ncols]
            dst = ring[slot][:, 0:nf * ncols].rearrange("p (a b) -> p a b", b=ncols)
            rr = [rW[(name, c)] for c in range((c0 // 512) * 512, c0 + ncols, 512)]
            DMA("sp", dst, src, rr, [rring[slot]], "ring%d" % slot)
            ws["issued"] += 1

        def acquire():
            i = ws["acq"]
            assert i < ws["issued"], "ring underflow"
            name, c0, ncols, f0, nf = units[i]
            slot = i % NR
            ws["acq"] += 1
            view = ring[slot][:, 0:nf * ncols].rearrange("p (a b) -> p a b", b=ncols)
            return view, rring[slot], units[i]

        def release():
            ws["rel"] += 1
            ws_issue()

        for _ in range(NR):
            ws_issue()

        DMA("sp", winglr[:], wbf["win"].rearrange("(kc p) n -> p kc n", p=128)[:, :, 3072:3088],
            [rW[("win", 3072)]], [res("winglr")], "k_winglr")

        rlnpa = res("lnpa")
        TS("dve", lnpa[:, 0:48], lnp[:, 0:48], ALPHA, None, ALU.mult, None, [res("lnp")], [rlnpa])
        CP("dve", lnpa[:, 48:64], lnp[:, 48:64], [res("lnp")], [rlnpa])
        rlam = res("lam")
        TT_("dve", lamt[:, 0:64], lamv[:, 0:64], lamv[:, 64:128], ALU.mult, [res("lamv")], [res("lamt")])
        RED(lams[:, 0:1], lamt[:, 0:64], [res("lamt")], [res("lams0")])
        TT_("dve", lamt[:, 0:64], lamv[:, 128:192], lamv[:, 192:256], ALU.mult, [res("lamv"), res("lams0")], [res("lamt")])
        RED(lams[:, 1:2], lamt[:, 0:64], [res("lamt")], [res("lams1")])
        A(lams[:, 2:4], lams[:, 0:2], AF.Exp, [res("lams0"), res("lams1")], [res("lams2")])
        TT_("dve", lams[:, 4:5], lams[:, 3:4], lams[:, 2:3], ALU.subtract, [res("lams2")], [res("lams4")])
        TS("dve", lams[:, 5:6], lams[:, 4:5], -LAMBDA_INIT, None, ALU.add, None, [res("lams4")], [rlam])
        neglam = lams[:, 5:6]
        TS("dve", gda4[:], gda4[:], 1.0 - LAMBDA_INIT, None, ALU.mult, None, [res("gda4")], [res("gda4")])
        MEMSET("pool", Vc[:, :, :, 128:129], 1.0, [], rV)
        MEMSET("pool", Vm[:, :, :, 256:257], 1.0, [], [res("Vm")])
        MEMSET("pool", glr[:], 1.0, [], [res("glr")])
        rSst = [res("Sst%d" % h) for h in range(4)]
        MEMSET("pool", Sst[:], 0.0, [], rSst)

        def transpose_in(src_tok, r_src, sub, scale_rt, xm=0):
            cs = slice(sub * 128, (sub + 1) * 128)
            for g in range(2):
                b = take()
                for j in range(4):
                    kc = g * 4 + j
                    TR(banks[b][:, j * 128:(j + 1) * 128], src_tok[:, kc * 128:(kc + 1) * 128], identf[:],
                       r_src + [res("identf")], [bres[b]])
                bv = banks[b][:].rearrange("p (a c) -> p a c", c=128)
                if scale_rt is not None:
                    A(RT[:, g * 4:(g + 1) * 4, cs], bv, AF.Copy, [bres[b]], rRT[g * 4:(g + 1) * 4], scale=scale_rt)
                    TS("pool", hT[:, g * 4:(g + 1) * 4, cs], RT[:, g * 4:(g + 1) * 4, cs], 1.0 / scale_rt, None, ALU.mult, None,
                       rRT[g * 4:(g + 1) * 4], rhT[g * 4:(g + 1) * 4])
                else:
                    CP("dve", hT[:, g * 4:(g + 1) * 4, cs], bv, [bres[b]], rhT[g * 4:(g + 1) * 4])
                free(b)

        def transpose_ocat():
            for sub in range(4):
                cs = slice(sub * 128, (sub + 1) * 128)
                for g in range(2):
                    b = take()
                    bvb = banks[b][:].bitcast(BF16)
                    for j in range(4):
                        kc = g * 4 + j
                        TR(bvb[:, j * 128:(j + 1) * 128], ocat[:, sub, kc * 128:(kc + 1) * 128], identb[:],
                           [rocat[sub], res("identb")], [bres[b]])
                    bv = bvb[:, 0:512].rearrange("p (a c) -> p a c", c=128)
                    if (sub + g) % 2 == 0:
                        CP("dve", hT[:, g * 4:(g + 1) * 4, cs], bv, [bres[b]], rhT[g * 4:(g + 1) * 4])
                    else:
                        A(hT[:, g * 4:(g + 1) * 4, cs], bv, AF.Copy, [bres[b]], rhT[g * 4:(g + 1) * 4])
                    free(b)

        st_ = {"zi": 0}

        def evac_residual(b, kc, factor):
            STT(RT[:, kc, :], banks[b][:], factor, RT[:, kc, :], ALU.mult, ALU.add, [bres[b], rRT[kc]], [rRT[kc]])
            CP("pool", hT[:, kc, :], RT[:, kc, :], [rRT[kc]], [rhT[kc]])
            zi = st_["zi"] % 2
            st_["zi"] += 1
            A(zsq[zi][:], RT[:, kc, :], AF.Square, [rRT[kc]], [rzsq[zi]])
            return zi

        def stats_mm(kc, zi, bM, bQ):
            MM(banks[bM][:], onesb[:], hT[:, kc, :], kc == 0, kc == 7, [res("onesb"), rhT[kc]], [bres[bM]])
            MM(banks[bQ][:], onesb[:], zsq[zi][:], kc == 0, kc == 7, [res("onesb"), rzsq[zi]], [bres[bQ]])

        def layer_norm(li, bM, bQ, final=False):
            rl = res("lnt")
            rr = res("rstd")
            A(lnt[:], banks[bM][:], AF.Square, [bres[bM]], [rl])
            TT_("dve", lnt[:], banks[bQ][:], lnt[:], ALU.subtract, [bres[bQ], rl], [rl])
            A(lnt[:], lnt[:], AF.Ln, [rl], [rl], bias=EPS)
            A(rstd[:], lnt[:], AF.Exp, [rl], [rr], scale=-0.5)
            free(bQ)
            gi, bi = 2 * li, 2 * li + 1
            for kc in range(8):
                TT_("dve", RT[:, kc, :], RT[:, kc, :], banks[bM][:], ALU.subtract, [rRT[kc], bres[bM]], [rRT[kc]])
                TT_("pool", RT[:, kc, :], RT[:, kc, :], rstd[:], ALU.mult, [rRT[kc], rr], [rRT[kc]])
                if not final:
                    A(hT[:, kc, :], RT[:, kc, :], AF.Identity, [rRT[kc], res("lnp")], [rhT[kc]],
                      scale=lnp[:, gi * 8 + kc:gi * 8 + kc + 1], bias=lnp[:, bi * 8 + kc:bi * 8 + kc + 1])
                TS("pool", RT[:, kc, :], RT[:, kc, :], lnpa[:, gi * 8 + kc:gi * 8 + kc + 1], lnpa[:, bi * 8 + kc:bi * 8 + kc + 1],
                   ALU.mult, ALU.add, [rRT[kc], rlnpa], [rRT[kc]])
            free(bM)

        def proj_residual_ln(n_k, rhs_fn, r_rhs_fn, factor, li, final=False, wd=False):
            grp_banks = []
            for grp in range(2):
                bs = [take() for _ in range(4)]
                grp_banks.append(bs)
                nparts = 3 if wd else 1
                for part in range(nparts):
                    view, rs, u = acquire()
                    nf = u[4]
                    for dmc in range(4):
                        for fl in range(nf):
                            f = part * 8 + fl
                            MM(banks[bs[dmc]][:], view[:, fl, dmc * 128:(dmc + 1) * 128], rhs_fn(f), f == 0, f == n_k - 1,
                               [rs] + r_rhs_fn(f), [bres[bs[dmc]]])
                    release()
            bM = bQ = None
            pend = []
            for grp in range(2):
                for dmc in range(4):
                    kc = grp * 4 + dmc
                    zi = evac_residual(grp_banks[grp][dmc], kc, factor)
                    free(grp_banks[grp][dmc])
                    pend.append((kc, zi))
                    if kc == 0:
                        continue
                    if bM is None:
                        bM = take()
                        bQ = take()
                    for (k2, z2) in pend:
                        stats_mm(k2, z2, bM, bQ)
                    pend = []
            layer_norm(li, bM, bQ, final)

        def ffn(li):
            si = 0
            for fg in range(6):
                vg_, rg_, ug_ = acquire()
                vu_, ru_, uu_ = acquire()
                ncols = ug_[2]
                for j in range(ncols // 128):
                    fc = fg * 4 + j
                    bG = take()
                    bU = take()
                    for kc in range(8):
                        MM(banks[bG][:], vg_[:, kc, j * 128:(j + 1) * 128], hT[:, kc, :], kc == 0, kc == 7, [rg_, rhT[kc]], [bres[bG]])
                    for kc in range(8):
                        MM(banks[bU][:], vu_[:, kc, j * 128:(j + 1) * 128], hT[:, kc, :], kc == 0, kc == 7, [ru_, rhT[kc]], [bres[bU]])
                    i = si % 2
                    si += 1
                    A(sg[i][:], banks[bG][:], AF.Silu, [bres[bG]], [rsg[i]])
                    TT_("dve", HT[:, fc, :], sg[i][:], banks[bU][:], ALU.mult, [rsg[i], bres[bU]], [rG[fc]])
                    free(bG)
                    free(bU)
                release()
                release()
            proj_residual_ln(NFC, lambda f: HT[:, f, :], lambda f: [rG[f]], 0.5, li, final=(li == 3), wd=True)

        def proj_fm(n_chunks, lhs_cols, evac):
            view, rs, u = acquire()
            for c in range(n_chunks):
                b = take()
                for kc in range(8):
                    MM(banks[b][0:lhs_cols, :], view[:, kc, c * lhs_cols:(c + 1) * lhs_cols], hT[:, kc, :], kc == 0, kc == 7,
                       [rs, rhT[kc]], [bres[b]])
                evac(c, b)
                free(b)
            return view, rs

        def proj_tm(view, rs, c0, ncols, evac):
            for sub in range(4):
                b = take()
                for kc in range(8):
                    MM(banks[b][:, 0:ncols], hT[:, kc, sub * 128:(sub + 1) * 128], view[:, kc, c0:c0 + ncols], kc == 0, kc == 7,
                       [rs, rhT[kc]], [bres[b]])
                evac(sub, b)
                free(b)

        def rms_rs(ss_ap, r_in, n):
            A(sm[:, 16:20], ss_ap, AF.Ln, r_in, [res("sm_ln")], scale=1.0 / n, bias=EPS)
            A(sm[:, 20:24], sm[:, 16:20], AF.Exp, [res("sm_ln")], [res("sm_rs")], scale=-0.5)

        def mem_kv():
            for ms in range(2):
                DMA("sp", tok[ms], mem_d[ms * 128:(ms + 1) * 128, :], [], rtok[ms], "tok%d" % ms)
            for ms in range(2):
                transpose_in(tok[ms], rtok[ms], ms, None)
            for uu in range(2):
                view, rs, u = acquire()
                for c in range(4):
                    ch = uu * 4 + c
                    b = take()
                    for kc in range(8):
                        MM(banks[b][:, 0:256], view[:, kc, c * 128:(c + 1) * 128], hT[:, kc, 0:256], kc == 0, kc == 7, [rs, rhT[kc]], [bres[b]])
                    A(kmT[:, ch, :], banks[b][:, 0:256], AF.Copy, [bres[b]], [res("kmT")])
                    free(b)
                release()
            for uu in range(2):
                view, rs, u = acquire()
                for ms in range(2):
                    b = take()
                    for kc in range(8):
                        MM(banks[b][:], hT[:, kc, ms * 128:(ms + 1) * 128], view[:, kc, :], kc == 0, kc == 7, [rs, rhT[kc]], [bres[b]])
                    CP("dve", Vm[:, ms, uu * 2:(uu + 1) * 2, 0:256], banks[b][:].rearrange("p (a c) -> p a c", c=256), [bres[b]], [res("Vm")])
                    free(b)
                release()

        def load_x(t):
            for sub in range(4):
                i = sub % 2
                r0 = t * TT + sub * 128
                DMA("sp", tok[i], x_d[r0:r0 + 128, :], [], rtok[i], "tok%d" % i)
                if xmode == 1 and t == nt - 1:
                    continue
                transpose_in(tok[i], rtok[i], sub, ALPHA, xm=(xmode if t == nt - 1 else 0))

        def w_in_phase(t):
            proj_fm(4, 128, lambda h, b: A(qdT[:, h, :], banks[b][:], AF.Copy, [bres[b]], [rG[h]]))
            release()
            proj_fm(4, 128, lambda h, b: CP("dve", Kc[:, h, t * TT:(t + 1) * TT], banks[b][:], [bres[b]], [rK[h][t]]))
            release()
            view, rs, u = acquire()
            proj_tm(view, rs, 0, 512, lambda sub, b: A(Vc[:, 4 * t + sub, :, 0:128], banks[b][:].rearrange("p (a c) -> p a c", c=128),
                                                       AF.Copy, [bres[b]], [rV[4 * t + sub]]))
            release()
            view, rs = proj_fm(4, 64, lambda h, b: A(qgT[0:64, h, :], banks[b][0:64, :], AF.Copy, [bres[b]], [rG[4 + h]], scale=0.125))
            proj_tm(view, rs, 256, 256, lambda sub, b: CP("dve", kg_sb[:, sub, :], banks[b][:, 0:256], [bres[b]], [rG[8 + sub]]))
            b = take()
            for kc in range(8):
                MM(banks[b][0:16, :], winglr[:, kc, :], hT[:, kc, :], kc == 0, kc == 7, [res("winglr"), rhT[kc]], [bres[b]])
            CP("dve", glr[0:16, :], banks[b][0:16, :], [bres[b]], [res("glr")])
            free(b)
            release()
            view, rs, u = acquire()
            proj_tm(view, rs, 0, 512, lambda sub, b: CP("dve", vg_sb[:, sub, :], banks[b][:], [bres[b]], [rG[12 + sub]]))
            release()
            view, rs, u = acquire()

            def ev_r(sub, b):
                i = sub % 2
                A(sg[i][:], banks[b][:], AF.Silu, [bres[b]], [rsg[i]])
                TT_("pool", gate[:, sub, :], sg[i][:], gng4[:], ALU.mult, [rsg[i], res("gng4")], [rG[16 + sub]])
            proj_tm(view, rs, 0, 512, ev_r)
            release()

        pti = [0]

        def da_head_half(t, h, c):
            nkt = 4 * t + 4
            ps = slice(c * 64, (c + 1) * 64)
            accb = [take(), take()]

            def acc(s):
                return banks[accb[s // 2]][:, (s % 2) * 256:(s % 2) * 256 + 129]

            def qk(kt):
                j = kt - 4 * t
                q0 = 128 * j if j >= 0 else 0
                b = take()
                MM(banks[b][:, q0:512], Kc[ps, h, kt * 128:(kt + 1) * 128], qdT[ps, h, q0:512], True, True,
                   [rK[h][kt // 4], rG[h]], [bres[b]])
                i = pti[0] % 3
                pti[0] += 1
                A(PT[i][:, q0:512], banks[b][:, q0:512], AF.Exp, [bres[b]], [rPT[i]], scale=0.125)
                if j >= 0:
                    MEMSET("pool", PT[i][64:128, q0:q0 + 64], 0.0, [], [rPT[i]])
                free(b)
                return i, j

            def pv(kt, i, j):
                for s in range(max(j, 0), 4):
                    MM(acc(s), PT[i][:, s * 128:(s + 1) * 128], Vc[:, kt, h, :], (kt == 0 and s % 2 == 0), kt == 4 * t + s,
                       [rPT[i], rV[kt]], [bres[accb[s // 2]]], skip=True)

            nxt = qk(0)
            for kt in range(nkt):
                cur = nxt
                if kt + 1 < nkt:
                    nxt = qk(kt + 1)
                pv(kt, cur[0], cur[1])
            rb = [bres[accb[0]], bres[accb[1]]]
            if c == 0:
                for s in range(4):
                    RECIP(sm[:, s:s + 1], acc(s)[:, 128:129], [rb[s // 2]], [res("sm_r0_%d" % s)])
                    TS("dve", o0[:, s, :], acc(s)[:, 0:128], sm[:, s:s + 1], None, ALU.mult, None,
                       [rb[s // 2], res("sm_r0_%d" % s)], [res("o0")])
            else:
                for s in range(4):
                    RECIP(sm[:, 4 + s:5 + s], acc(s)[:, 128:129], [rb[s // 2]], [res("sm_r1")])
                TS("dve", sm[:, 8:12], sm[:, 4:8], neglam, None, ALU.mult, None, [res("sm_r1"), rlam], [res("sm_r1l")])
                for s in range(4):
                    STT(od[:, s, :], acc(s)[:, 0:128], sm[:, 8 + s:9 + s], o0[:, s, :], ALU.mult, ALU.add,
                        [rb[s // 2], res("sm_r1l"), res("o0")], [res("od")])
                A(sg[0][:], od[:].rearrange("p a b -> p (a b)"), AF.Square, [res("od")], [rsg[0]])
                RED(sm[:, 12:16], sg[0][:].rearrange("p (a b) -> p a b", b=128), [rsg[0]], [res("sm_ss")])
                rms_rs(sm[:, 12:16], [res("sm_ss")], 128.0)
                TT_("dve", od[:], od[:], sm[:, 20:24].unsqueeze(2).to_broadcast([128, 4, 128]), ALU.mult,
                    [res("od"), res("sm_rs")], [res("od")])
                TT_("pool", ocat[:, :, h * 128:(h + 1) * 128], od[:], gda4[:].rearrange("p (a b) -> p a b", b=128), ALU.mult,
                    [res("od"), res("gda4")], rocat)
            free(accb[0])
            free(accb[1])

        def da_phase(t):
            for h in range(4):
                for c in range(2):
                    da_head_half(t, h, c)

        sbi = [0]

        def gla_sub(t, sub):
            i = sub % 2
            cs = slice(sub * 128, (sub + 1) * 128)
            bz = take()
            MM(banks[bz][:, 0:256], glr[0:32, cs], w2a[0:32, :], True, True, [res("glr"), res("w2a")], [bres[bz]])
            A(la[i][:], banks[bz][:, 0:256], AF.Exp, [bres[bz]], [rla[i]], scale=-1.0)
            A(la[i][:], la[i][:], AF.Ln, [rla[i]], [rla[i]], bias=1.0)
            free(bz)
            bD = take()
            MM(banks[bD][:, 0:256], m1[:], la[i][:], True, True, [res("m1"), rla[i]], [bres[bD]])
            for h in range(4):
                MM(banks[bD][0:64, 256 + 2 * h:258 + 2 * h], la[i][:, h * 64:(h + 1) * 64], ind[:], True, True,
                   [res("ind"), rla[i]], [bres[bD]], skip=True)
            A(ed[i][:], banks[bD][:, 0:256], AF.Exp, [bres[bD]], [red[i]])
            A(dec[:, sub, :], banks[bD][0:64, 256:264], AF.Exp, [bres[bD]], [res("dec")])
            TT_("dve", kend[:, sub, :], kg_sb[:, sub, :], ed[i][:], ALU.mult, [rG[8 + sub], red[i]], [rG[20 + sub // 2]])
            free(bD)
            bO = take()
            for c in range(2):
                ps = slice(c * 64, (c + 1) * 64)
                bS = take()
                for h in range(4):
                    MM(banks[bS][0:64, h * 128:(h + 1) * 128], kend[ps, sub, h * 64:(h + 1) * 64], vg_sb[ps, sub, h * 128:(h + 1) * 128],
                       True, True, [rG[20 + sub // 2], rG[12 + sub]], [bres[bS]], skip=True)
                for h in range(4):
                    STT(Sst[:, h * 128:(h + 1) * 128], Sst[:, h * 128:(h + 1) * 128], dec[:, sub, 2 * h + c:2 * h + c + 1],
                        banks[bS][0:64, h * 128:(h + 1) * 128], ALU.mult, ALU.add, [rSst[h], res("dec"), bres[bS]], [rSst[h]])
                k = sbi[0] % 2
                sbi[0] += 1
                CP("pool", Sbf[k][:], Sst[:], rSst, [rSbf[k]])
                free(bS)
                for h in range(4):
                    MM(banks[bO][ps, h * 128:(h + 1) * 128], qgT[0:64, h, sub * 128 + c * 64:sub * 128 + (c + 1) * 64],
                       Sbf[k][:, h * 128:(h + 1) * 128], True, True, [rG[4 + h], rSbf[k]], [bres[bO]], skip=True)
            A(sg[1][:], banks[bO][:], AF.Square, [bres[bO]], [rsg[1]])
            RED(sm[:, 12:16], sg[1][:].rearrange("p (a b) -> p a b", b=128), [rsg[1]], [res("sm_ss")])
            rms_rs(sm[:, 12:16], [res("sm_ss")], 128.0)
            TT_("dve", sg[1][:].rearrange("p (a b) -> p a b", b=128), banks[bO][:].rearrange("p (a b) -> p a b", b=128),
                sm[:, 20:24].unsqueeze(2).to_broadcast([128, 4, 128]), ALU.mult, [bres[bO], res("sm_rs"), rsg[1]], [rsg[1]])
            TT_("pool", ocat[:, sub, 512:1024], sg[1][:], gate[:, sub, :], ALU.mult, [rsg[1], rG[16 + sub]], [rocat[sub]])
            free(bO)

        def gla_phase(t):
            for sub in range(4):
                gla_sub(t, sub)

        rci = [0]

        def cross_phase():
            for uu in range(2):
                proj_fm(4, 128, lambda c, b, uu=uu: A(qcT[:, uu * 4 + c, :], banks[b][:], AF.Copy, [bres[b]], [rG[uu * 4 + c]]))
                release()
            for hd in range(4):
                for ms in range(2):
                    b = take()
                    for j in range(2):
                        MM(banks[b][:], kmT[:, 2 * hd + j, ms * 128:(ms + 1) * 128], qcT[:, 2 * hd + j, :], j == 0, j == 1,
                           [res("kmT"), rG[2 * hd + j]], [bres[b]])
                    A(PTc[:, hd * 2 + ms, :], banks[b][:], AF.Exp, [bres[b]], [rG[8 + hd * 2 + ms]], scale=1.0 / 16.0)
                    free(b)
            for hd in range(4):
                for sub in range(4):
                    b = take()
                    for ms in range(2):
                        MM(banks[b][:, 0:257], PTc[:, hd * 2 + ms, sub * 128:(sub + 1) * 128], Vm[:, ms, hd, :], ms == 0, ms == 1,
                           [rG[8 + hd * 2 + ms], res("Vm")], [bres[b]])
                    k = 24 + (rci[0] % 8)
                    rci[0] += 1
                    RECIP(sm[:, k:k + 1], banks[b][:, 256:257], [bres[b]], [res("sm_rc%d" % k)])
                    A(ocat[:, sub, hd * 256:(hd + 1) * 256], banks[b][:, 0:256], AF.Identity, [bres[b], res("sm_rc%d" % k)], [rocat[sub]],
                      scale=sm[:, k:k + 1])
                    free(b)

        def store_out(t):
            for sub in range(4):
                i = sub % 2
                for g in range(2):
                    b = take()
                    for j in range(4):
                        kc = g * 4 + j
                        TR(banks[b][:, j * 128:(j + 1) * 128], RT[:, kc, sub * 128:(sub + 1) * 128], identf[:], [rRT[kc], res("identf")], [bres[b]])
                    if g == 0:
                        A(tok[i][:, g * 512:(g + 1) * 512], banks[b][:], AF.Copy, [bres[b]], rtok[i])
                    else:
                        CP("dve", tok[i][:, g * 512:(g + 1) * 512], banks[b][:], [bres[b]], rtok[i])
                    free(b)
                r0 = t * TT + sub * 128
                DMA("sp", out_d[r0:r0 + 128, :], tok[i], rtok[i], [], "tok%d" % i)

        mem_kv()
        for t in range(nt):
            if not (skip_lastx and t == nt - 1):
                load_x(t)
            if stop_after == "x" and t == nt - 1:
                break
            ffn(0)
            if stop_after == "ln1" and t == nt - 1:
                break
            w_in_phase(t)
            if stop_after == "win" and t == nt - 1:
                break
            da_phase(t)
            if stop_after == "da" and t == nt - 1:
                break
            gla_phase(t)
            if stop_after == "gla" and t == nt - 1:
                break
            transpose_ocat()
            proj_residual_ln(8, lambda f: hT[:, f, :], lambda f: [rhT[f]], 1.0, 1)
            if stop_after == "ln2" and t == nt - 1:
                break
            cross_phase()
            transpose_ocat()
            proj_residual_ln(8, lambda f: hT[:, f, :], lambda f: [rhT[f]], 1.0, 2)
            if stop_after == "ln3" and t == nt - 1:
                break
            ffn(3)
            store_out(t)
        if dbg:
            DMA("sp", dbg_d, RT[:], rRT, [], "dbg")
            DMA("sp", dbg2_d, hT[:], rhT, [], "dbg2")
        if stop_after is None:
            assert ws["acq"] == len(units) == ws["rel"], (ws, len(units))
        S.finalize()
        S.emit(nc, st, final_lanes=["tok0", "tok1", "dbg", "dbg2"])
    return nc


def make_in_maps(inputs, cores):
    f = lambda a: np.ascontiguousarray(np.asarray(a, dtype=np.float32))
    g = {k: f(v) for k, v in inputs.items()}
    lnp = np.zeros((128, 64), np.float32)
    for i, nm in enumerate(["ln1_g", "ln1_b", "ln2_g", "ln2_b", "ln3_g", "ln3_b", "ln4_g", "ln4_b"]):
        lnp[:, i * 8:(i + 1) * 8] = g[nm][0].reshape(8, 128).T
    lamv = np.concatenate([g["da_lambda_q1"][0], g["da_lambda_k1"][0], g["da_lambda_q2"][0], g["da_lambda_k2"][0]])
    lamv = np.ascontiguousarray(np.broadcast_to(lamv[None, :], (128, 256)))
    dng4 = np.ascontiguousarray(np.broadcast_to(np.tile(g["da_norm_g"][0], 4)[None, :], (128, 512)))
    gng4 = np.ascontiguousarray(np.broadcast_to(np.tile(g["gla_norm_g"][0], 4)[None, :], (128, 512)))
    w2aug = np.zeros((32, 256), np.float32)
    w2aug[0:16] = g["gla_w_gate2"][0]
    w2aug[16] = g["gla_b_gate"][0]
    s_idx = np.arange(128)
    same = (s_idx[:, None] // 64) == (s_idx[None, :] // 64)
    m1 = np.where(same & (s_idx[:, None] > s_idx[None, :]), -1.0 / 16.0, 0.0).astype(np.float32)
    ind = np.zeros((128, 2), np.float32)
    ind[0:64, 0] = -1.0 / 16.0
    ind[64:128, 1] = -1.0 / 16.0
    common = {
        "ffn1_w_gate": g["ffn1_w_gate"][0], "ffn1_w_up": g["ffn1_w_up"][0], "ffn1_w_down": g["ffn1_w_down"][0],
        "w_in": g["w_in"][0], "w_out": g["w_out"][0], "cross_wq": g["cross_wq"][0], "cross_wkv": g["cross_wkv"][0],
        "cross_wo": g["cross_wo"][0], "ffn2_w_gate": g["ffn2_w_gate"][0], "ffn2_w_up": g["ffn2_w_up"][0],
        "ffn2_w_down": g["ffn2_w_down"][0],
        "lnp": lnp, "lamv": lamv, "dng4": dng4, "gng4": gng4, "w2aug": w2aug,
        "identf": np.eye(128, dtype=np.float32), "identb": np.eye(128, dtype=np.float32).astype(ml_dtypes.bfloat16),
        "m1": m1, "ind": ind, "onesb": np.full((128, 128), 1.0 / 1024.0, np.float32).astype(ml_dtypes.bfloat16),
    }
    maps = []
    for c in cores:
        m = dict(common)
        m["x"] = g["x"][c]
        m["mem"] = g["mem"][c]
        maps.append(m)
    return maps


_CACHE = {}


def kernel(**inputs):
    if "nc" not in _CACHE:
        _CACHE["nc"] = build_program()
    nc = _CACHE["nc"]
    cores = list(range(8))
    in_maps = make_in_maps(inputs, cores)
    res = run_bass_kernel_spmd(nc, in_maps, core_ids=cores)
    out = np.stack([np.asarray(r["out"], dtype=np.float32) for r in res.results], axis=0)
    return out
```

```python
import contextlib
import numpy as np
import ml_dtypes
import concourse.bass as bass
import concourse.mybir as mybir
from concourse.bass_utils import run_bass_kernel_spmd

F32 = mybir.dt.float32
BF16 = mybir.dt.bfloat16
AF = mybir.ActivationFunctionType
ALU = mybir.AluOpType
AX = mybir.AxisListType

NT = 8
TT = 512
SEQ = 4096
D = 1024
FH = 2816
NFC = 22
INW = 3088
ALPHA = 2.0 ** 0.25
EPS = 1e-5
LAMBDA_INIT = 0.8 - 0.6 * 1.0
NR = 4
SLOT = 4096


class Res:
    __slots__ = ("name", "w", "rs", "excl")

    def __init__(self, name, excl=False):
        self.name = name
        self.w = None
        self.rs = []
        self.excl = excl


class Op:
    __slots__ = ("eng", "fn", "deps", "dma", "lane", "needs", "idx", "laneval", "waits", "phase")

    def __init__(self, eng, fn, dma, lane):
        self.phase = None
        self.eng = eng
        self.fn = fn
        self.dma = dma
        self.lane = lane
        self.deps = []
        self.needs = False
        self.idx = None
        self.laneval = None
        self.waits = []


ENGS = ("pe", "act", "dve", "pool", "sp")
NROT = 4


class Sched:
    def __init__(self):
        self.ops = {e: [] for e in ENGS}
        self.lane_tok = {}
        self.phase = ""

    def add(self, eng, fn, reads=(), writes=(), lane=None):
        dma = lane is not None
        op = Op(eng, fn, dma, lane)
        op.phase = self.phase
        reads = list(reads)
        writes = list(writes)
        if dma:
            if lane not in self.lane_tok:
                self.lane_tok[lane] = Res("lane_" + str(lane))
            writes.append(self.lane_tok[lane])
        raw = set()
        other = set()
        for r in reads:
            if r.w is not None:
                raw.add(r.w)
            if r.excl:
                for q in r.rs:
                    if q.eng != eng:
                        other.add(q)
        for w in writes:
            if w.w is not None:
                other.add(w.w)
            for q in w.rs:
                other.add(q)
        deps = set()
        for d in raw | other:
            if d is op:
                continue
            if (not d.dma) and (not dma) and d.eng == eng:
                if eng == "pe":
                    continue
                if d not in raw:
                    continue
            deps.add(d)
        op.deps = list(deps)
        for d in deps:
            d.needs = True
        for r in reads:
            if r.excl:
                r.rs = [op]
                continue
            if not dma:
                r.rs = [q for q in r.rs if q.dma or q.eng != eng]
            r.rs.append(op)
        for w in writes:
            w.w = op
            w.rs = []
        self.ops[eng].append(op)
        return op

    def finalize(self):
        lanecnt = {}
        for e in ENGS:
            c = 0
            for op in self.ops[e]:
                if op.dma:
                    lanecnt[op.lane] = lanecnt.get(op.lane, 0) + 1
                    op.laneval = 16 * lanecnt[op.lane]
                elif op.needs:
                    c += 1
                    op.idx = c
        self.lanecnt = lanecnt
        for e in ENGS:
            waited = {}
            for op in self.ops[e]:
                best = {}
                for d in op.deps:
                    if d.dma:
                        key = ("lane", d.lane)
                        val = d.laneval
                    else:
                        key = ("eng", d.eng)
                        val = d.idx
                    if waited.get(key, 0) >= val:
                        continue
                    if best.get(key, 0) < val:
                        best[key] = val
                for key, val in best.items():
                    waited[key] = val
                    op.waits.append((key, val))

    def emit(self, nc, stack, final_lanes=()):
        sems = {}
        for e in ENGS:
            for r in range(NROT):
                sems[("eng", e, r)] = stack.enter_context(nc.semaphore("s_%s%d" % (e, r)))
        for lane in self.lanecnt:
            sems[("lane", lane)] = stack.enter_context(nc.semaphore("l_%s" % str(lane)))
        block = stack.enter_context(nc.Block())
        names = {"pe": "tensor", "act": "scalar", "dve": "vector", "pool": "gpsimd", "sp": "sync"}

        def run(e, eng):
            for op in self.ops[e]:
                for key, val in op.waits:
                    if key[0] == "lane":
                        eng.wait_ge(sems[key], val)
                    else:
                        j = val - 1
                        eng.wait_ge(sems[("eng", key[1], j % NROT)], j // NROT + 1)
                inst = op.fn(eng)
                if op.dma:
                    inst.then_inc(sems[("lane", op.lane)], 16)
                elif op.idx is not None:
                    j = op.idx - 1
                    inst.then_inc(sems[("eng", e, j % NROT)], 1)
            if e == "sp":
                for lane in self.lanecnt:
                    eng.wait_ge(sems[("lane", lane)], 16 * self.lanecnt[lane])

        for e in ENGS:
            getattr(block, names[e])(lambda eng, e=e: run(e, eng))


def build_program(nt=NT, dbg=False, stop_after=None, unit_limit=None, skip_lastx=False, xmode=0):
    nc = bass.Bass("TRN2", target_bir_lowering=False, dynamic_dma_scratch_size=8192)
    S = Sched()

    def din(name, shape, dt=F32):
        return nc.dram_tensor(name, list(shape), dt, kind="ExternalInput").ap()

    x_d = din("x", [SEQ, D])
    mem_d = din("mem", [256, D])
    wsrc = {
        "wg1": din("ffn1_w_gate", [D, FH]), "wu1": din("ffn1_w_up", [D, FH]), "wd1": din("ffn1_w_down", [FH, D]),
        "win": din("w_in", [D, INW]), "wout": din("w_out", [D, D]), "wq": din("cross_wq", [D, D]),
        "wkv": din("cross_wkv", [D, 2 * D]), "wo": din("cross_wo", [D, D]),
        "wg2": din("ffn2_w_gate", [D, FH]), "wu2": din("ffn2_w_up", [D, FH]), "wd2": din("ffn2_w_down", [FH, D]),
    }
    lnp_d = din("lnp", [128, 64])
    lamv_d = din("lamv", [128, 256])
    dng_d = din("dng4", [128, 512])
    gng_d = din("gng4", [128, 512])
    w2a_d = din("w2aug", [32, 256])
    identf_d = din("identf", [128, 128])
    identb_d = din("identb", [128, 128], BF16)
    m1_d = din("m1", [128, 128])
    ind_d = din("ind", [128, 2])
    onesb_d = din("onesb", [128, 128], BF16)
    out_d = nc.dram_tensor("out", [SEQ, D], F32, kind="ExternalOutput").ap()
    wbf = {k: nc.dram_tensor("bf_" + k, list(v.shape), BF16, kind="Internal").ap() for k, v in wsrc.items()}
    if dbg:
        dbg_d = nc.dram_tensor("dbg", [128, 8, 512], F32, kind="ExternalOutput").ap()
        dbg2_d = nc.dram_tensor("dbg2", [128, 8, 512], BF16, kind="ExternalOutput").ap()

    with contextlib.ExitStack() as st:
        def sb(n, s, d):
            return st.enter_context(nc.sbuf_tensor(n, list(s), d))

        Kc = sb("Kc", [128, 4, SEQ], BF16)
        Vc = sb("Vc", [128, 32, 4, 129], BF16)
        ring = [sb("ring%d" % i, [128, SLOT], BF16) for i in range(NR)]
        RT = sb("RT", [128, 8, TT], F32)
        hT = sb("hT", [128, 8, TT], BF16)
        HT = sb("HT", [128, NFC, TT], BF16)
        tokbuf = sb("tokbuf", [128, 2048], F32)
        qdz = [sb("qdz%d" % i, [128, 4, 512], BF16) for i in range(2)]
        xst = [sb("xst%d" % i, [128, 1024], F32) for i in range(2)]
        sg = [sb("sg%d" % i, [128, 512], F32) for i in range(2)]
        PT = [sb("PT%d" % i, [128, 512], BF16) for i in range(3)]
        zsq = [sb("zsq%d" % i, [128, 512], BF16) for i in range(2)]
        lnt = sb("lnt", [128, 512], F32)
        Sst = sb("Sst", [64, 512], F32)
        Sbf = [sb("Sbf%d" % i, [64, 512], BF16) for i in range(2)]
        glr = sb("glr", [32, 512], F32)
        w2a = sb("w2a", [32, 256], F32)
        winglr = sb("winglr", [128, 8, 16], BF16)
        identf = sb("identf_s", [128, 128], F32)
        identb = sb("identb_s", [128, 128], BF16)
        m1 = sb("m1_s", [128, 128], F32)
        ind = sb("ind_s", [128, 2], F32)
        onesb = sb("onesb_s", [128, 128], BF16)
        lnp = sb("lnp_s", [128, 64], F32)
        lnpa = sb("lnpa_s", [128, 64], F32)
        lamv = sb("lamv_s", [128, 256], F32)
        lamt = sb("lamt", [128, 64], F32)
        lams = sb("lams", [128, 8], F32)
        gda4 = sb("gda4", [128, 512], F32)
        gng4 = sb("gng4_s", [128, 512], F32)
        kmT = sb("kmT", [128, 8, 256], BF16)
        Vm = sb("Vm", [128, 2, 4, 257], BF16)
        la = [sb("la%d" % i, [128, 256], F32) for i in range(2)]
        ed = [sb("ed%d" % i, [128, 256], F32) for i in range(2)]
        dec = sb("dec", [64, 4, 8], F32)
        o0 = sb("o0", [128, 4, 128], F32)
        od = sb("od", [128, 4, 128], F32)
        sm = sb("sm", [128, 32], F32)
        banks = [st.enter_context(nc.psum_tensor("pb%d" % i, [128, 512], F32)) for i in range(8)]
        bres = [Res("pb%d" % i, excl=True) for i in range(8)]
        free_banks = list(range(8))

        def take():
            return free_banks.pop(0)

        def free(b):
            free_banks.append(b)

        scr = HT[:].rearrange("p a b -> p (a b)")
        def scr_bf(g0, ng, shape_str=None, **kw):
            v = scr[:, g0 * 512:(g0 + ng) * 512]
            return v.rearrange(shape_str, **kw) if shape_str else v

        def scr_f32(g0, ng, shape_str=None, **kw):
            v = scr[:, g0 * 512:(g0 + ng) * 512].bitcast(F32)
            return v.rearrange(shape_str, **kw) if shape_str else v

        qdT = scr_bf(0, 4, "p (a b) -> p a b", b=512)
        qgT = scr_bf(4, 4, "p (a b) -> p a b", b=512)
        kg_sb = scr_f32(8, 4, "p (a b) -> p a b", b=256)
        vg_sb = scr_bf(12, 4, "p (a b) -> p a b", b=512)
        gate = scr_bf(16, 4, "p (a b) -> p a b", b=512)
        kend = scr_bf(20, 2, "p (a b) -> p a b", b=256)
        qcT = scr_bf(0, 8, "p (a b) -> p a b", b=512)
        PTc = scr_bf(8, 8, "p (a b) -> p a b", b=512)
        tok = [tokbuf[:, i * 1024:(i + 1) * 1024] for i in range(2)]
        ocat = tokbuf[:].bitcast(BF16).rearrange("p (a b) -> p a b", b=1024)

        R = {}
        def res(name):
            if name not in R:
                R[name] = Res(name)
            return R[name]
        rRT = [res("RT%d" % k) for k in range(8)]
        rhT = [res("hT%d" % k) for k in range(8)]
        rG = [res("G%d" % g) for g in range(NFC)]
        rring = [res("ring%d" % i) for i in range(NR)]
        rK = [[res("K%d_%d" % (h, t)) for t in range(NT)] for h in range(4)]
        rV = [res("V%d" % k) for k in range(32)]
        rocat = [res("ocat%d" % s) for s in range(4)]
        rtok = [[rocat[0], rocat[1]], [rocat[2], rocat[3]]]
        rsg = [res("sg%d" % i) for i in range(2)]
        rqd = [[res("qd%d_%d" % (c, h)) for h in range(4)] for c in range(2)]
        rxst = [res("xst%d" % i) for i in range(2)]
        rPT = [res("PT%d" % i) for i in range(3)]
        rzsq = [res("zsq%d" % i) for i in range(2)]
        rSbf = [res("Sbf%d" % i) for i in range(2)]
        rla = [res("la%d" % i) for i in range(2)]
        red = [res("ed%d" % i) for i in range(2)]

        def gr(g0, ng):
            return rG[g0:g0 + ng]

        def MM(out, lhsT, rhs, start, stop, r, w, skip=False):
            if skip:
                return S.add("pe", lambda e: e.matmul(out, lhsT=lhsT, rhs=rhs, start=start, stop=stop, skip_group_check=True), r, w)
            return S.add("pe", lambda e: e.matmul(out, lhsT=lhsT, rhs=rhs, start=start, stop=stop), r, w)

        def TR(out, in_, ident, r, w):
            return S.add("pe", lambda e: e.transpose(out=out, in_=in_, identity=ident), r, w)

        def A(out, in_, func, r, w, scale=None, bias=None):
            kw = {}
            if scale is not None:
                kw["scale"] = scale
            if bias is not None:
                kw["bias"] = bias
            return S.add("act", lambda e: e.activation(out=out, in_=in_, func=func, **kw), r, w)

        def TT_(eng, out, in0, in1, op, r, w):
            return S.add(eng, lambda e: e.tensor_tensor(out=out, in0=in0, in1=in1, op=op), r, w)

        def TS(eng, out, in0, s1, s2, op0, op1, r, w):
            if op1 is None:
                return S.add(eng, lambda e: e.tensor_scalar(out=out, in0=in0, scalar1=s1, scalar2=None, op0=op0), r, w)
            return S.add(eng, lambda e: e.tensor_scalar(out=out, in0=in0, scalar1=s1, scalar2=s2, op0=op0, op1=op1), r, w)

        def STT(out, in0, scalar, in1, op0, op1, r, w):
            return S.add("dve", lambda e: e.scalar_tensor_tensor(out=out, in0=in0, scalar=scalar, in1=in1, op0=op0, op1=op1), r, w)

        def CP(eng, out, in_, r, w):
            return S.add(eng, lambda e: e.tensor_copy(out=out, in_=in_), r, w)

        def RECIP(out, in_, r, w):
            return S.add("dve", lambda e: e.reciprocal(out=out, in_=in_), r, w)

        def RED(out, in_, r, w):
            return S.add("dve", lambda e: e.tensor_reduce(out=out, in_=in_, axis=AX.X, op=ALU.add), r, w)

        def MEMSET(eng, ap, val, r, w):
            return S.add(eng, lambda e: e.memset(ap, val), r, w)

        def DMA(eng, out, in_, r, w, lane):
            return S.add(eng, lambda e: e.dma_start(out=out, in_=in_), r, w, lane=lane)

        cnt = {"lane": 0, "cast": 0}

        def const_load(dst, src, r):
            cnt["lane"] += 1
            DMA("act", dst, src, [], [r], "k%d" % cnt["lane"])

        const_load(identf[:], identf_d, res("identf"))
        const_load(identb[:], identb_d, res("identb"))
        const_load(onesb[:], onesb_d, res("onesb"))
        const_load(lnp[:], lnp_d, res("lnp"))
        const_load(lamv[:], lamv_d, res("lamv"))
        const_load(gda4[:], dng_d, res("gda4"))
        const_load(gng4[:], gng_d, res("gng4"))
        const_load(w2a[:], w2a_d, res("w2a"))
        const_load(m1[:], m1_d, res("m1"))
        const_load(ind[:], ind_d, res("ind"))

        rW = {}

        def cast(name, c0, c1):
            key = (name, c0)
            rW[key] = Res("w_%s_%d" % key)
            lane = "cast%d" % (cnt["cast"] % 8)
            cnt["cast"] += 1
            DMA("pool", wbf[name][:, c0:c1], wsrc[name][:, c0:c1], [], [rW[key]], lane)

        def cast_cols(name, ncols, step=512):
            for c0 in range(0, ncols, step):
                cast(name, c0, min(ncols, c0 + step))

        def cast_ffn(i):
            for c0 in range(0, FH, 512):
                cast("wg%d" % i, c0, min(FH, c0 + 512))
                cast("wu%d" % i, c0, min(FH, c0 + 512))
            cast_cols("wd%d" % i, D)

        cast_cols("wkv", 2 * D)
        cast_ffn(1)
        cast_cols("win", 3072)
        cast("win", 3072, 3088)
        cast_cols("wout", D)
        cast_cols("wq", D)
        cast_cols("wo", D)
        cast_ffn(2)

        units = []

        def u_k8(name, c0, ncols):
            units.append((name, c0, ncols, 0, 8))

        def u_ffn(i):
            for c0 in range(0, FH, 512):
                u_k8("wg%d" % i, c0, min(512, FH - c0))
                u_k8("wu%d" % i, c0, min(512, FH - c0))
            for grp in range(2):
                for part in range(3):
                    nf = 8 if part < 2 else 6
                    units.append(("wd%d" % i, grp * 512, 512, part * 8, nf))

        for c0 in range(0, 2 * D, 512):
            u_k8("wkv", c0, 512)
        for t in range(nt):
            u_ffn(1)
            for c0 in range(0, 3072, 512):
                u_k8("win", c0, 512)
            for nm in ("wout", "wq", "wo"):
                u_k8(nm, 0, 512)
                u_k8(nm, 512, 512)
            u_ffn(2)

        ws = {"issued": 0, "acq": 0, "rel": 0}

        def ws_issue():
            i = ws["issued"]
            if i >= len(units) or (unit_limit is not None and i >= unit_limit):
                return
            name, c0, ncols, f0, nf = units[i]
            slot = i % NR
            src = wbf[name].rearrange("(kc p) n -> p kc n", p=128)[:, f0:f0 + nf, c0:c0 + ncols]
            dst = ring[slot][:, 0:nf * ncols].rearrange("p (a b) -> p a b", b=ncols)
            rr = [rW[(name, c)] for c in range((c0 // 512) * 512, c0 + ncols, 512)]
            DMA("sp", dst, src, rr, [rring[slot]], "ring%d" % slot)
            ws["issued"] += 1

        def acquire():
            i = ws["acq"]
            assert i < ws["issued"], "ring underflow"
            name, c0, ncols, f0, nf = units[i]
            slot = i % NR
            ws["acq"] += 1
            view = ring[slot][:, 0:nf * ncols].rearrange("p (a b) -> p a b", b=ncols)
            return view, rring[slot], units[i]

        def release():
            ws["rel"] += 1
            ws_issue()

        for _ in range(NR):
            ws_issue()

        DMA("sp", winglr[:], wbf["win"].rearrange("(kc p) n -> p kc n", p=128)[:, :, 3072:3088],
            [rW[("win", 3072)]], [res("winglr")], "k_winglr")

        rlnpa = res("lnpa")
        TS("dve", lnpa[:, 0:48], lnp[:, 0:48], ALPHA, None, ALU.mult, None, [res("lnp")], [rlnpa])
        CP("dve", lnpa[:, 48:64], lnp[:, 48:64], [res("lnp")], [rlnpa])
        rlam = res("lam")
        TT_("dve", lamt[:, 0:64], lamv[:, 0:64], lamv[:, 64:128], ALU.mult, [res("lamv")], [res("lamt")])
        RED(lams[:, 0:1], lamt[:, 0:64], [res("lamt")], [res("lams0")])
        TT_("dve", lamt[:, 0:64], lamv[:, 128:192], lamv[:, 192:256], ALU.mult, [res("lamv"), res("lams0")], [res("lamt")])
        RED(lams[:, 1:2], lamt[:, 0:64], [res("lamt")], [res("lams1")])
        A(lams[:, 2:4], lams[:, 0:2], AF.Exp, [res("lams0"), res("lams1")], [res("lams2")])
        TT_("dve", lams[:, 4:5], lams[:, 3:4], lams[:, 2:3], ALU.subtract, [res("lams2")], [res("lams4")])
        TS("dve", lams[:, 5:6], lams[:, 4:5], -LAMBDA_INIT, None, ALU.add, None, [res("lams4")], [rlam])
        neglam = lams[:, 5:6]
        TS("dve", gda4[:], gda4[:], 1.0 - LAMBDA_INIT, None, ALU.mult, None, [res("gda4")], [res("gda4")])
        MEMSET("pool", Vc[:, :, :, 128:129], 1.0, [], rV)
        MEMSET("pool", Vm[:, :, :, 256:257], 1.0, [], [res("Vm")])
        MEMSET("pool", glr[:], 1.0, [], [res("glr")])
        MEMSET("pool", qdz[0][64:128, :, :], 0.0, [], rqd[0])
        MEMSET("pool", qdz[1][0:64, :, :], 0.0, [], rqd[1])
        rSst = [res("Sst%d" % h) for h in range(4)]
        MEMSET("pool", Sst[:], 0.0, [], rSst)

        def transpose_in(src_tok, r_src, sub, scale_rt, xm=0):
            cs = slice(sub * 128, (sub + 1) * 128)
            for g in range(2):
                b = take()
                for j in range(4):
                    kc = g * 4 + j
                    TR(banks[b][:, j * 128:(j + 1) * 128], src_tok[:, kc * 128:(kc + 1) * 128], identf[:],
                       r_src + [res("identf")], [bres[b]])
                bv = banks[b][:].rearrange("p (a c) -> p a c", c=128)
                if scale_rt is not None:
                    A(RT[:, g * 4:(g + 1) * 4, cs], bv, AF.Copy, [bres[b]], rRT[g * 4:(g + 1) * 4], scale=scale_rt)
                    TS("dve", hT[:, g * 4:(g + 1) * 4, cs], RT[:, g * 4:(g + 1) * 4, cs], 1.0 / scale_rt, None, ALU.mult, None,
                       rRT[g * 4:(g + 1) * 4], rhT[g * 4:(g + 1) * 4])
                else:
                    CP("dve", hT[:, g * 4:(g + 1) * 4, cs], bv, [bres[b]], rhT[g * 4:(g + 1) * 4])
                free(b)

        def transpose_ocat():
            for sub in range(4):
                cs = slice(sub * 128, (sub + 1) * 128)
                for g in range(2):
                    b = take()
                    bvb = banks[b][:].bitcast(BF16)
                    for j in range(4):
                        kc = g * 4 + j
                        TR(bvb[:, j * 128:(j + 1) * 128], ocat[:, sub, kc * 128:(kc + 1) * 128], identb[:],
                           [rocat[sub], res("identb")], [bres[b]])
                    bv = bvb[:, 0:512].rearrange("p (a c) -> p a c", c=128)
                    if (sub + g) % 2 == 0:
                        CP("dve", hT[:, g * 4:(g + 1) * 4, cs], bv, [bres[b]], rhT[g * 4:(g + 1) * 4])
                    else:
                        A(hT[:, g * 4:(g + 1) * 4, cs], bv, AF.Copy, [bres[b]], rhT[g * 4:(g + 1) * 4])
                    free(b)

        st_ = {"zi": 0}

        def evac_residual(b, kc, factor):
            STT(RT[:, kc, :], banks[b][:], factor, RT[:, kc, :], ALU.mult, ALU.add, [bres[b], rRT[kc]], [rRT[kc]])
            A(hT[:, kc, :], RT[:, kc, :], AF.Copy, [rRT[kc]], [rhT[kc]])
            zi = st_["zi"] % 2
            st_["zi"] += 1
            A(zsq[zi][:], RT[:, kc, :], AF.Square, [rRT[kc]], [rzsq[zi]])
            return zi

        def stats_mm(kc, zi, bM, bQ):
            MM(banks[bM][:], onesb[:], hT[:, kc, :], kc == 0, kc == 7, [res("onesb"), rhT[kc]], [bres[bM]])
            MM(banks[bQ][:], onesb[:], zsq[zi][:], kc == 0, kc == 7, [res("onesb"), rzsq[zi]], [bres[bQ]])

        def layer_norm(li, bM, bQ, final=False):
            rl = res("lnt")
            A(lnt[:], banks[bM][:], AF.Square, [bres[bM]], [rl])
            TT_("dve", lnt[:], banks[bQ][:], lnt[:], ALU.subtract, [bres[bQ], rl], [rl])
            A(lnt[:], lnt[:], AF.Ln, [rl], [rl], bias=EPS)
            A(banks[bQ][:], lnt[:], AF.Exp, [rl], [bres[bQ]], scale=-0.5)
            gi, bi = 2 * li, 2 * li + 1
            for kc in range(8):
                TT_("dve", RT[:, kc, :], RT[:, kc, :], banks[bM][:], ALU.subtract, [rRT[kc], bres[bM]], [rRT[kc]])
                TT_("dve", RT[:, kc, :], RT[:, kc, :], banks[bQ][:], ALU.mult, [rRT[kc], bres[bQ]], [rRT[kc]])
                if not final:
                    A(hT[:, kc, :], RT[:, kc, :], AF.Identity, [rRT[kc], res("lnp")], [rhT[kc]],
                      scale=lnp[:, gi * 8 + kc:gi * 8 + kc + 1], bias=lnp[:, bi * 8 + kc:bi * 8 + kc + 1])
                A(RT[:, kc, :], RT[:, kc, :], AF.Identity, [rRT[kc], rlnpa], [rRT[kc]],
                  scale=lnpa[:, gi * 8 + kc:gi * 8 + kc + 1], bias=lnpa[:, bi * 8 + kc:bi * 8 + kc + 1])
            free(bQ)
            free(bM)

        def proj_residual_ln(n_k, rhs_fn, r_rhs_fn, factor, li, final=False, wd=False):
            grp_banks = []
            for grp in range(2):
                bs = [take() for _ in range(4)]
                grp_banks.append(bs)
                nparts = 3 if wd else 1
                for part in range(nparts):
                    view, rs, u = acquire()
                    nf = u[4]
                    for dmc in range(4):
                        for fl in range(nf):
                            f = part * 8 + fl
                            MM(banks[bs[dmc]][:], view[:, fl, dmc * 128:(dmc + 1) * 128], rhs_fn(f), f == 0, f == n_k - 1,
                               [rs] + r_rhs_fn(f), [bres[bs[dmc]]])
                    release()
            bM = bQ = None
            pend = []
            for grp in range(2):
                for dmc in range(4):
                    kc = grp * 4 + dmc
                    zi = evac_residual(grp_banks[grp][dmc], kc, factor)
                    free(grp_banks[grp][dmc])
                    pend.append((kc, zi))
                    if kc == 0:
                        continue
                    if bM is None:
                        bM = take()
                        bQ = take()
                    for (k2, z2) in pend:
                        stats_mm(k2, z2, bM, bQ)
                    pend = []
            layer_norm(li, bM, bQ, final)

        def ffn(li):
            si = 0
            for fg in range(6):
                vg_, rg_, ug_ = acquire()
                vu_, ru_, uu_ = acquire()
                ncols = ug_[2]
                for j in range(ncols // 128):
                    fc = fg * 4 + j
                    bG = take()
                    bU = take()
                    for kc in range(8):
                        MM(banks[bG][:], vg_[:, kc, j * 128:(j + 1) * 128], hT[:, kc, :], kc == 0, kc == 7, [rg_, rhT[kc]], [bres[bG]])
                    for kc in range(8):
                        MM(banks[bU][:], vu_[:, kc, j * 128:(j + 1) * 128], hT[:, kc, :], kc == 0, kc == 7, [ru_, rhT[kc]], [bres[bU]])
                    i = si % 2
                    si += 1
                    A(sg[i][:], banks[bG][:], AF.Silu, [bres[bG]], [rsg[i]])
                    TT_("dve", HT[:, fc, :], sg[i][:], banks[bU][:], ALU.mult, [rsg[i], bres[bU]], [rG[fc]])
                    free(bG)
                    free(bU)
                release()
                release()
            proj_residual_ln(NFC, lambda f: HT[:, f, :], lambda f: [rG[f]], 0.5, li, final=(li == 3), wd=True)

        def proj_fm(n_chunks, lhs_cols, evac):
            view, rs, u = acquire()
            for c in range(n_chunks):
                b = take()
                for kc in range(8):
                    MM(banks[b][0:lhs_cols, :], view[:, kc, c * lhs_cols:(c + 1) * lhs_cols], hT[:, kc, :], kc == 0, kc == 7,
                       [rs, rhT[kc]], [bres[b]])
                evac(c, b)
                free(b)
            return view, rs

        def proj_tm(view, rs, c0, ncols, evac):
            for sub in range(4):
                b = take()
                for kc in range(8):
                    MM(banks[b][:, 0:ncols], hT[:, kc, sub * 128:(sub + 1) * 128], view[:, kc, c0:c0 + ncols], kc == 0, kc == 7,
                       [rs, rhT[kc]], [bres[b]])
                evac(sub, b)
                free(b)

        def rms_rs(ss_ap, r_in, n):
            A(sm[:, 16:20], ss_ap, AF.Ln, r_in, [res("sm_ln")], scale=1.0 / n, bias=EPS)
            A(sm[:, 20:24], sm[:, 16:20], AF.Exp, [res("sm_ln")], [res("sm_rs")], scale=-0.5)

        def mem_kv():
            for ms in range(2):
                DMA("sp", tok[ms], mem_d[ms * 128:(ms + 1) * 128, :], [], rtok[ms], "tok%d" % ms)
            for ms in range(2):
                transpose_in(tok[ms], rtok[ms], ms, None)
            for uu in range(2):
                view, rs, u = acquire()
                for c in range(4):
                    ch = uu * 4 + c
                    b = take()
                    for kc in range(8):
                        MM(banks[b][:, 0:256], view[:, kc, c * 128:(c + 1) * 128], hT[:, kc, 0:256], kc == 0, kc == 7, [rs, rhT[kc]], [bres[b]])
                    A(kmT[:, ch, :], banks[b][:, 0:256], AF.Copy, [bres[b]], [res("kmT")])
                    free(b)
                release()
            for uu in range(2):
                view, rs, u = acquire()
                for ms in range(2):
                    b = take()
                    for kc in range(8):
                        MM(banks[b][:], hT[:, kc, ms * 128:(ms + 1) * 128], view[:, kc, :], kc == 0, kc == 7, [rs, rhT[kc]], [bres[b]])
                    CP("dve", Vm[:, ms, uu * 2:(uu + 1) * 2, 0:256], banks[b][:].rearrange("p (a c) -> p a c", c=256), [bres[b]], [res("Vm")])
                    free(b)
                release()

        def x_dma(t, sub):
            i = sub % 2
            r0 = t * TT + sub * 128
            DMA("sp", xst[i][:], x_d[r0:r0 + 128, :], [], [rxst[i]], "xst%d" % i)

        def load_x(t):
            if t == 0:
                x_dma(0, 0)
                x_dma(0, 1)
            for sub in range(4):
                i = sub % 2
                transpose_in(xst[i][:], [rxst[i]], sub, ALPHA)
                if sub + 2 < 4:
                    x_dma(t, sub + 2)
                elif t + 1 < nt:
                    x_dma(t + 1, sub - 2)

        def w_in_phase(t):
            def ev_q(h, b):
                A(qdz[0][0:64, h, :], banks[b][0:64, :], AF.Copy, [bres[b]], [rqd[0][h]])
                A(qdz[1][64:128, h, :], banks[b][64:128, :], AF.Copy, [bres[b]], [rqd[1][h]])
            proj_fm(4, 128, ev_q)
            release()
            proj_fm(4, 128, lambda h, b: CP("dve", Kc[:, h, t * TT:(t + 1) * TT], banks[b][:], [bres[b]], [rK[h][t]]))
            release()
            view, rs, u = acquire()
            proj_tm(view, rs, 0, 512, lambda sub, b: A(Vc[:, 4 * t + sub, :, 0:128], banks[b][:].rearrange("p (a c) -> p a c", c=128),
                                                       AF.Copy, [bres[b]], [rV[4 * t + sub]]))
            release()
            view, rs = proj_fm(4, 64, lambda h, b: A(qgT[0:64, h, :], banks[b][0:64, :], AF.Copy, [bres[b]], [rG[4 + h]], scale=0.125))
            proj_tm(view, rs, 256, 256, lambda sub, b: CP("dve", kg_sb[:, sub, :], banks[b][:, 0:256], [bres[b]], [rG[8 + sub]]))
            b = take()
            for kc in range(8):
                MM(banks[b][0:16, :], winglr[:, kc, :], hT[:, kc, :], kc == 0, kc == 7, [res("winglr"), rhT[kc]], [bres[b]])
            CP("dve", glr[0:16, :], banks[b][0:16, :], [bres[b]], [res("glr")])
            free(b)
            release()
            view, rs, u = acquire()
            proj_tm(view, rs, 0, 512, lambda sub, b: CP("dve", vg_sb[:, sub, :], banks[b][:], [bres[b]], [rG[12 + sub]]))
            release()
            view, rs, u = acquire()

            def ev_r(sub, b):
                i = sub % 2
                A(sg[i][:], banks[b][:], AF.Silu, [bres[b]], [rsg[i]])
                TT_("pool", gate[:, sub, :], sg[i][:], gng4[:], ALU.mult, [rsg[i], res("gng4")], [rG[16 + sub]])
            proj_tm(view, rs, 0, 512, ev_r)
            release()

        pti = [0]

        def da_head_half(t, h, c, pump=None):
            nkt = 4 * t + 4
            ps = slice(c * 64, (c + 1) * 64)
            accb = [take(), take()]

            def acc(s):
                return banks[accb[s // 2]][:, (s % 2) * 256:(s % 2) * 256 + 129]

            def qk(kt):
                j = kt - 4 * t
                q0 = 128 * j if j >= 0 else 0
                b = take()
                MM(banks[b][:, q0:512], Kc[:, h, kt * 128:(kt + 1) * 128], qdz[c][:, h, q0:512], True, True,
                   [rK[h][kt // 4], rqd[c][h]], [bres[b]])
                i = pti[0] % 3
                pti[0] += 1
                A(PT[i][:, q0:512], banks[b][:, q0:512], AF.Exp, [bres[b]], [rPT[i]], scale=0.125)
                if j >= 0:
                    MEMSET("pool", PT[i][64:128, q0:q0 + 64], 0.0, [], [rPT[i]])
                free(b)
                return i, j

            def pv(kt, i, j):
                for s in range(max(j, 0), 4):
                    MM(acc(s), PT[i][:, s * 128:(s + 1) * 128], Vc[:, kt, h, :], (kt == 0 and s % 2 == 0), kt == 4 * t + s,
                       [rPT[i], rV[kt]], [bres[accb[s // 2]]], skip=True)

            nxt = qk(0)
            for kt in range(nkt):
                cur = nxt
                if kt + 1 < nkt:
                    nxt = qk(kt + 1)
                pv(kt, cur[0], cur[1])
                if pump is not None:
                    pump()
            rb = [bres[accb[0]], bres[accb[1]]]
            if c == 0:
                for s in range(4):
                    RECIP(sm[:, s:s + 1], acc(s)[:, 128:129], [rb[s // 2]], [res("sm_r0_%d" % s)])
                    TS("dve", o0[:, s, :], acc(s)[:, 0:128], sm[:, s:s + 1], None, ALU.mult, None,
                       [rb[s // 2], res("sm_r0_%d" % s)], [res("o0")])
            else:
                for s in range(4):
                    RECIP(sm[:, 4 + s:5 + s], acc(s)[:, 128:129], [rb[s // 2]], [res("sm_r1")])
                TS("dve", sm[:, 8:12], sm[:, 4:8], neglam, None, ALU.mult, None, [res("sm_r1"), rlam], [res("sm_r1l")])
                for s in range(4):
                    STT(od[:, s, :], acc(s)[:, 0:128], sm[:, 8 + s:9 + s], o0[:, s, :], ALU.mult, ALU.add,
                        [rb[s // 2], res("sm_r1l"), res("o0")], [res("od")])
                A(sg[0][:], od[:].rearrange("p a b -> p (a b)"), AF.Square, [res("od")], [rsg[0]])
                RED(sm[:, 12:16], sg[0][:].rearrange("p (a b) -> p a b", b=128), [rsg[0]], [res("sm_ss")])
                rms_rs(sm[:, 12:16], [res("sm_ss")], 128.0)
                TT_("dve", od[:], od[:], sm[:, 20:24].unsqueeze(2).to_broadcast([128, 4, 128]), ALU.mult,
                    [res("od"), res("sm_rs")], [res("od")])
                TT_("pool", ocat[:, :, h * 128:(h + 1) * 128], od[:], gda4[:].rearrange("p (a b) -> p a b", b=128), ALU.mult,
                    [res("od"), res("gda4")], rocat)
            free(accb[0])
            free(accb[1])

        def da_phase(t):
            gen = gla_gen(t)
            state = {"done": False, "n": 0}
            per = max(1, (8 * (4 * t + 4)) // 30)

            def pump():
                state["n"] += 1
                if state["done"] or state["n"] % per != 0:
                    return
                try:
                    next(gen)
                except StopIteration:
                    state["done"] = True
            for h in range(4):
                for c in range(2):
                    da_head_half(t, h, c, pump)
            if not state["done"]:
                for _ in gen:
                    pass

        sbi = [0]

        def gla_sub_gen(t, sub):
            i = sub % 2
            cs = slice(sub * 128, (sub + 1) * 128)
            bz = take()
            MM(banks[bz][:, 0:256], glr[0:32, cs], w2a[0:32, :], True, True, [res("glr"), res("w2a")], [bres[bz]])
            A(la[i][:], banks[bz][:, 0:256], AF.Exp, [bres[bz]], [rla[i]], scale=-1.0)
            A(la[i][:], la[i][:], AF.Ln, [rla[i]], [rla[i]], bias=1.0)
            free(bz)
            yield
            bD = take()
            MM(banks[bD][:, 0:256], m1[:], la[i][:], True, True, [res("m1"), rla[i]], [bres[bD]])
            for h in range(4):
                MM(banks[bD][0:64, 256 + 2 * h:258 + 2 * h], la[i][:, h * 64:(h + 1) * 64], ind[:], True, True,
                   [res("ind"), rla[i]], [bres[bD]], skip=True)
            A(ed[i][:], banks[bD][:, 0:256], AF.Exp, [bres[bD]], [red[i]])
            A(dec[:, sub, :], banks[bD][0:64, 256:264], AF.Exp, [bres[bD]], [res("dec")])
            TT_("dve", kend[:, sub, :], kg_sb[:, sub, :], ed[i][:], ALU.mult, [rG[8 + sub], red[i]], [rG[20 + sub // 2]])
            free(bD)
            yield
            bO = take()
            for c in range(2):
                ps = slice(c * 64, (c + 1) * 64)
                bS = take()
                for h in range(4):
                    MM(banks[bS][0:64, h * 128:(h + 1) * 128], kend[ps, sub, h * 64:(h + 1) * 64], vg_sb[ps, sub, h * 128:(h + 1) * 128],
                       True, True, [rG[20 + sub // 2], rG[12 + sub]], [bres[bS]], skip=True)
                for h in range(4):
                    STT(Sst[:, h * 128:(h + 1) * 128], Sst[:, h * 128:(h + 1) * 128], dec[:, sub, 2 * h + c:2 * h + c + 1],
                        banks[bS][0:64, h * 128:(h + 1) * 128], ALU.mult, ALU.add, [rSst[h], res("dec"), bres[bS]], [rSst[h]])
                k = sbi[0] % 2
                sbi[0] += 1
                CP("pool", Sbf[k][:], Sst[:], rSst, [rSbf[k]])
                free(bS)
                yield
                for h in range(4):
                    MM(banks[bO][ps, h * 128:(h + 1) * 128], qgT[0:64, h, sub * 128 + c * 64:sub * 128 + (c + 1) * 64],
                       Sbf[k][:, h * 128:(h + 1) * 128], True, True, [rG[4 + h], rSbf[k]], [bres[bO]], skip=True)
                yield
            A(sg[1][:], banks[bO][:], AF.Square, [bres[bO]], [rsg[1]])
            RED(sm[:, 12:16], sg[1][:].rearrange("p (a b) -> p a b", b=128), [rsg[1]], [res("sm_ss")])
            rms_rs(sm[:, 12:16], [res("sm_ss")], 128.0)
            TT_("dve", sg[1][:].rearrange("p (a b) -> p a b", b=128), banks[bO][:].rearrange("p (a b) -> p a b", b=128),
                sm[:, 20:24].unsqueeze(2).to_broadcast([128, 4, 128]), ALU.mult, [bres[bO], res("sm_rs"), rsg[1]], [rsg[1]])
            TT_("pool", ocat[:, sub, 512:1024], sg[1][:], gate[:, sub, :], ALU.mult, [rsg[1], rG[16 + sub]], [rocat[sub]])
            free(bO)

        def gla_gen(t):
            for sub in range(4):
                yield from gla_sub_gen(t, sub)

        def gla_phase(t):
            for _ in gla_gen(t):
                pass

        rci = [0]

        def cross_phase():
            for uu in range(2):
                proj_fm(4, 128, lambda c, b, uu=uu: A(qcT[:, uu * 4 + c, :], banks[b][:], AF.Copy, [bres[b]], [rG[uu * 4 + c]]))
                release()
            for hd in range(4):
                for ms in range(2):
                    b = take()
                    for j in range(2):
                        MM(banks[b][:], kmT[:, 2 * hd + j, ms * 128:(ms + 1) * 128], qcT[:, 2 * hd + j, :], j == 0, j == 1,
                           [res("kmT"), rG[2 * hd + j]], [bres[b]])
                    A(PTc[:, hd * 2 + ms, :], banks[b][:], AF.Exp, [bres[b]], [rG[8 + hd * 2 + ms]], scale=1.0 / 16.0)
                    free(b)
            for hd in range(4):
                for sub in range(4):
                    b = take()
                    for ms in range(2):
                        MM(banks[b][:, 0:257], PTc[:, hd * 2 + ms, sub * 128:(sub + 1) * 128], Vm[:, ms, hd, :], ms == 0, ms == 1,
                           [rG[8 + hd * 2 + ms], res("Vm")], [bres[b]])
                    k = 24 + (rci[0] % 8)
                    rci[0] += 1
                    RECIP(sm[:, k:k + 1], banks[b][:, 256:257], [bres[b]], [res("sm_rc%d" % k)])
                    A(ocat[:, sub, hd * 256:(hd + 1) * 256], banks[b][:, 0:256], AF.Identity, [bres[b], res("sm_rc%d" % k)], [rocat[sub]],
                      scale=sm[:, k:k + 1])
                    free(b)

        def store_out(t):
            for sub in range(4):
                i = sub % 2
                for g in range(2):
                    b = take()
                    for j in range(4):
                        kc = g * 4 + j
                        TR(banks[b][:, j * 128:(j + 1) * 128], RT[:, kc, sub * 128:(sub + 1) * 128], identf[:], [rRT[kc], res("identf")], [bres[b]])
                    if g == 0:
                        A(tok[i][:, g * 512:(g + 1) * 512], banks[b][:], AF.Copy, [bres[b]], rtok[i])
                    else:
                        CP("dve", tok[i][:, g * 512:(g + 1) * 512], banks[b][:], [bres[b]], rtok[i])
                    free(b)
                r0 = t * TT + sub * 128
                DMA("sp", out_d[r0:r0 + 128, :], tok[i], rtok[i], [], "tok%d" % i)

        S.phase = "memkv"
        mem_kv()
        for t in range(nt):
            S.phase = "t%d:x" % t
            if not (skip_lastx and t == nt - 1):
                load_x(t)
            if stop_after == "x" and t == nt - 1:
                break
            S.phase = "t%d:ffn1" % t
            ffn(0)
            if stop_after == "ln1" and t == nt - 1:
                break
            S.phase = "t%d:win" % t
            w_in_phase(t)
            if stop_after == "win" and t == nt - 1:
                break
            S.phase = "t%d:da" % t
            da_phase(t)
            if stop_after == "da" and t == nt - 1:
                break
            if stop_after == "gla" and t == nt - 1:
                break
            S.phase = "t%d:wout" % t
            transpose_ocat()
            proj_residual_ln(8, lambda f: hT[:, f, :], lambda f: [rhT[f]], 1.0, 1)
            if stop_after == "ln2" and t == nt - 1:
                break
            S.phase = "t%d:cross" % t
            cross_phase()
            S.phase = "t%d:wo" % t
            transpose_ocat()
            proj_residual_ln(8, lambda f: hT[:, f, :], lambda f: [rhT[f]], 1.0, 2)
            if stop_after == "ln3" and t == nt - 1:
                break
            S.phase = "t%d:ffn2" % t
            ffn(3)
            S.phase = "t%d:out" % t
            store_out(t)
        if dbg:
            DMA("sp", dbg_d, RT[:], rRT, [], "dbg")
            DMA("sp", dbg2_d, hT[:], rhT, [], "dbg2")
        if stop_after is None:
            assert ws["acq"] == len(units) == ws["rel"], (ws, len(units))
        S.finalize()
        S.emit(nc, st, final_lanes=["tok0", "tok1", "dbg", "dbg2"])
    return nc


def make_in_maps(inputs, cores):
    f = lambda a: np.ascontiguousarray(np.asarray(a, dtype=np.float32))
    g = {k: f(v) for k, v in inputs.items()}
    lnp = np.zeros((128, 64), np.float32)
    for i, nm in enumerate(["ln1_g", "ln1_b", "ln2_g", "ln2_b", "ln3_g", "ln3_b", "ln4_g", "ln4_b"]):
        lnp[:, i * 8:(i + 1) * 8] = g[nm][0].reshape(8, 128).T
    lamv = np.concatenate([g["da_lambda_q1"][0], g["da_lambda_k1"][0], g["da_lambda_q2"][0], g["da_lambda_k2"][0]])
    lamv = np.ascontiguousarray(np.broadcast_to(lamv[None, :], (128, 256)))
    dng4 = np.ascontiguousarray(np.broadcast_to(np.tile(g["da_norm_g"][0], 4)[None, :], (128, 512)))
    gng4 = np.ascontiguousarray(np.broadcast_to(np.tile(g["gla_norm_g"][0], 4)[None, :], (128, 512)))
    w2aug = np.zeros((32, 256), np.float32)
    w2aug[0:16] = g["gla_w_gate2"][0]
    w2aug[16] = g["gla_b_gate"][0]
    s_idx = np.arange(128)
    same = (s_idx[:, None] // 64) == (s_idx[None, :] // 64)
    m1 = np.where(same & (s_idx[:, None] > s_idx[None, :]), -1.0 / 16.0, 0.0).astype(np.float32)
    ind = np.zeros((128, 2), np.float32)
    ind[0:64, 0] = -1.0 / 16.0
    ind[64:128, 1] = -1.0 / 16.0
    common = {
        "ffn1_w_gate": g["ffn1_w_gate"][0], "ffn1_w_up": g["ffn1_w_up"][0], "ffn1_w_down": g["ffn1_w_down"][0],
        "w_in": g["w_in"][0], "w_out": g["w_out"][0], "cross_wq": g["cross_wq"][0], "cross_wkv": g["cross_wkv"][0],
        "cross_wo": g["cross_wo"][0], "ffn2_w_gate": g["ffn2_w_gate"][0], "ffn2_w_up": g["ffn2_w_up"][0],
        "ffn2_w_down": g["ffn2_w_down"][0],
        "lnp": lnp, "lamv": lamv, "dng4": dng4, "gng4": gng4, "w2aug": w2aug,
        "identf": np.eye(128, dtype=np.float32), "identb": np.eye(128, dtype=np.float32).astype(ml_dtypes.bfloat16),
        "m1": m1, "ind": ind, "onesb": np.full((128, 128), 1.0 / 1024.0, np.float32).astype(ml_dtypes.bfloat16),
    }
    maps = []
    for c in cores:
        m = dict(common)
        m["x"] = g["x"][c]
        m["mem"] = g["mem"][c]
        maps.append(m)
    return maps


_CACHE = {}


def kernel(**inputs):
    if "nc" not in _CACHE:
        _CACHE["nc"] = build_program()
    nc = _CACHE["nc"]
    cores = list(range(8))
    in_maps = make_in_maps(inputs, cores)
    res = run_bass_kernel_spmd(nc, in_maps, core_ids=cores)
    out = np.stack([np.asarray(r["out"], dtype=np.float32) for r in res.results], axis=0)
    return out
```

```python
import contextlib
import numpy as np
import ml_dtypes
import concourse.bass as bass
import concourse.mybir as mybir
from concourse.bass_utils import run_bass_kernel_spmd

F32 = mybir.dt.float32
BF16 = mybir.dt.bfloat16
AF = mybir.ActivationFunctionType
ALU = mybir.AluOpType
AX = mybir.AxisListType

NT = 8
TT = 512
SEQ = 4096
D = 1024
FH = 2816
NFC = 22
INW = 3088
ALPHA = 2.0 ** 0.25
EPS = 1e-5
LAMBDA_INIT = 0.8 - 0.6 * 1.0
NR = 4
SLOT = 4096


class Res:
    __slots__ = ("name", "w", "rs", "excl")

    def __init__(self, name, excl=False):
        self.name = name
        self.w = None
        self.rs = []
        self.excl = excl


class Op:
    __slots__ = ("eng", "fn", "deps", "dma", "lane", "needs", "idx", "laneval", "waits", "phase")

    def __init__(self, eng, fn, dma, lane):
        self.phase = None
        self.eng = eng
        self.fn = fn
        self.dma = dma
        self.lane = lane
        self.deps = []
        self.needs = False
        self.idx = None
        self.laneval = None
        self.waits = []


ENGS = ("pe", "act", "dve", "pool", "sp")
NROT = 4


class Sched:
    def __init__(self):
        self.ops = {e: [] for e in ENGS}
        self.lane_tok = {}
        self.phase = ""

    def add(self, eng, fn, reads=(), writes=(), lane=None):
        dma = lane is not None
        op = Op(eng, fn, dma, lane)
        op.phase = self.phase
        reads = list(reads)
        writes = list(writes)
        if dma:
            if lane not in self.lane_tok:
                self.lane_tok[lane] = Res("lane_" + str(lane))
            writes.append(self.lane_tok[lane])
        raw = set()
        other = set()
        for r in reads:
            if r.w is not None:
                raw.add(r.w)
            if r.excl:
                for q in r.rs:
                    if q.eng != eng:
                        other.add(q)
        for w in writes:
            if w.w is not None:
                other.add(w.w)
            for q in w.rs:
                other.add(q)
        deps = set()
        for d in raw | other:
            if d is op:
                continue
            if (not d.dma) and (not dma) and d.eng == eng:
                if eng == "pe":
                    continue
                if d not in raw:
                    continue
            deps.add(d)
        op.deps = list(deps)
        for d in deps:
            d.needs = True
        for r in reads:
            if r.excl:
                r.rs = [op]
                continue
            if not dma:
                r.rs = [q for q in r.rs if q.dma or q.eng != eng]
            r.rs.append(op)
        for w in writes:
            w.w = op
            w.rs = []
        self.ops[eng].append(op)
        return op

    def finalize(self):
        lanecnt = {}
        for e in ENGS:
            c = 0
            for op in self.ops[e]:
                if op.dma:
                    lanecnt[op.lane] = lanecnt.get(op.lane, 0) + 1
                    op.laneval = 16 * lanecnt[op.lane]
                elif op.needs:
                    c += 1
                    op.idx = c
        self.lanecnt = lanecnt
        for e in ENGS:
            waited = {}
            for op in self.ops[e]:
                best = {}
                for d in op.deps:
                    if d.dma:
                        key = ("lane", d.lane)
                        val = d.laneval
                    else:
                        key = ("eng", d.eng)
                        val = d.idx
                    if waited.get(key, 0) >= val:
                        continue
                    if best.get(key, 0) < val:
                        best[key] = val
                for key, val in best.items():
                    waited[key] = val
                    op.waits.append((key, val))

    def emit(self, nc, stack, final_lanes=()):
        sems = {}
        for e in ENGS:
            for r in range(NROT):
                sems[("eng", e, r)] = stack.enter_context(nc.semaphore("s_%s%d" % (e, r)))
        for lane in self.lanecnt:
            sems[("lane", lane)] = stack.enter_context(nc.semaphore("l_%s" % str(lane)))
        block = stack.enter_context(nc.Block())
        names = {"pe": "tensor", "act": "scalar", "dve": "vector", "pool": "gpsimd", "sp": "sync"}

        def run(e, eng):
            for op in self.ops[e]:
                for key, val in op.waits:
                    if key[0] == "lane":
                        eng.wait_ge(sems[key], val)
                    else:
                        j = val - 1
                        eng.wait_ge(sems[("eng", key[1], j % NROT)], j // NROT + 1)
                inst = op.fn(eng)
                if op.dma:
                    inst.then_inc(sems[("lane", op.lane)], 16)
                elif op.idx is not None:
                    j = op.idx - 1
                    inst.then_inc(sems[("eng", e, j % NROT)], 1)
            if e == "sp":
                for lane in self.lanecnt:
                    eng.wait_ge(sems[("lane", lane)], 16 * self.lanecnt[lane])

        for e in ENGS:
            getattr(block, names[e])(lambda eng, e=e: run(e, eng))


def build_program(nt=NT, dbg=False, stop_after=None, unit_limit=None, skip_lastx=False, xmode=0):
    nc = bass.Bass("TRN2", target_bir_lowering=False, dynamic_dma_scratch_size=8192)
    S = Sched()

    def din(name, shape, dt=F32):
        return nc.dram_tensor(name, list(shape), dt, kind="ExternalInput").ap()

    x_d = din("x", [SEQ, D])
    mem_d = din("mem", [256, D])
    wsrc = {
        "wg1": din("ffn1_w_gate", [D, FH]), "wu1": din("ffn1_w_up", [D, FH]), "wd1": din("ffn1_w_down", [FH, D]),
        "win": din("w_in", [D, INW]), "wout": din("w_out", [D, D]), "wq": din("cross_wq", [D, D]),
        "wkv": din("cross_wkv", [D, 2 * D]), "wo": din("cross_wo", [D, D]),
        "wg2": din("ffn2_w_gate", [D, FH]), "wu2": din("ffn2_w_up", [D, FH]), "wd2": din("ffn2_w_down", [FH, D]),
    }
    lnp_d = din("lnp", [128, 64])
    lamv_d = din("lamv", [128, 256])
    dng_d = din("dng4", [128, 512])
    gng_d = din("gng4", [128, 512])
    w2a_d = din("w2aug", [32, 256])
    identf_d = din("identf", [128, 128])
    identb_d = din("identb", [128, 128], BF16)
    m1_d = din("m1", [128, 128])
    ind_d = din("ind", [128, 2])
    onesb_d = din("onesb", [128, 128], BF16)
    out_d = nc.dram_tensor("out", [SEQ, D], F32, kind="ExternalOutput").ap()
    wbf = {k: nc.dram_tensor("bf_" + k, list(v.shape), BF16, kind="Internal").ap() for k, v in wsrc.items()}
    if dbg:
        dbg_d = nc.dram_tensor("dbg", [128, 8, 512], F32, kind="ExternalOutput").ap()
        dbg2_d = nc.dram_tensor("dbg2", [128, 8, 512], BF16, kind="ExternalOutput").ap()

    with contextlib.ExitStack() as st:
        def sb(n, s, d):
            return st.enter_context(nc.sbuf_tensor(n, list(s), d))

        Kc = sb("Kc", [128, 4, SEQ], BF16)
        Vc = sb("Vc", [128, 32, 4, 129], BF16)
        ring = [sb("ring%d" % i, [128, SLOT], BF16) for i in range(NR)]
        RT = sb("RT", [128, 8, TT], F32)
        hT = sb("hT", [128, 8, TT], BF16)
        HT = sb("HT", [128, NFC, TT], BF16)
        tokbuf = sb("tokbuf", [128, 2048], F32)
        qdz = [sb("qdz%d" % i, [128, 4, 512], BF16) for i in range(2)]
        xst = [sb("xst%d" % i, [128, 1024], F32) for i in range(2)]
        sg = [sb("sg%d" % i, [128, 512], F32) for i in range(2)]
        PT = [sb("PT%d" % i, [128, 512], BF16) for i in range(4)]
        zsq = [sb("zsq%d" % i, [128, 512], BF16) for i in range(2)]
        lnt = sb("lnt", [128, 512], F32)
        Sst = sb("Sst", [64, 512], F32)
        Sbf = [sb("Sbf%d" % i, [64, 512], BF16) for i in range(2)]
        glr = sb("glr", [32, 512], F32)
        w2a = sb("w2a", [32, 256], F32)
        winglr = sb("winglr", [128, 8, 16], BF16)
        identf = sb("identf_s", [128, 128], F32)
        identb = sb("identb_s", [128, 128], BF16)
        m1 = sb("m1_s", [128, 128], F32)
        ind = sb("ind_s", [128, 2], F32)
        onesb = sb("onesb_s", [128, 128], BF16)
        lnp = sb("lnp_s", [128, 64], F32)
        lnpa = sb("lnpa_s", [128, 64], F32)
        lamv = sb("lamv_s", [128, 256], F32)
        lamt = sb("lamt", [128, 64], F32)
        lams = sb("lams", [128, 8], F32)
        gda4 = sb("gda4", [128, 512], F32)
        gng4 = sb("gng4_s", [128, 512], F32)
        kmT = sb("kmT", [128, 8, 256], BF16)
        Vm = sb("Vm", [128, 2, 4, 257], BF16)
        la = [sb("la%d" % i, [128, 256], F32) for i in range(2)]
        ed = [sb("ed%d" % i, [128, 256], F32) for i in range(2)]
        dec = sb("dec", [64, 4, 8], F32)
        o0 = sb("o0", [128, 4, 128], F32)
        od = sb("od", [128, 4, 128], F32)
        sm = sb("sm", [128, 32], F32)
        banks = [st.enter_context(nc.psum_tensor("pb%d" % i, [128, 512], F32)) for i in range(8)]
        bres = [Res("pb%d" % i, excl=True) for i in range(8)]
        free_banks = list(range(8))

        def take():
            return free_banks.pop(0)

        def free(b):
            free_banks.append(b)

        scr = HT[:].rearrange("p a b -> p (a b)")
        def scr_bf(g0, ng, shape_str=None, **kw):
            v = scr[:, g0 * 512:(g0 + ng) * 512]
            return v.rearrange(shape_str, **kw) if shape_str else v

        def scr_f32(g0, ng, shape_str=None, **kw):
            v = scr[:, g0 * 512:(g0 + ng) * 512].bitcast(F32)
            return v.rearrange(shape_str, **kw) if shape_str else v

        qdT = scr_bf(0, 4, "p (a b) -> p a b", b=512)
        qgT = scr_bf(4, 4, "p (a b) -> p a b", b=512)
        kg_sb = scr_f32(8, 4, "p (a b) -> p a b", b=256)
        vg_sb = scr_bf(12, 4, "p (a b) -> p a b", b=512)
        gate = scr_bf(16, 4, "p (a b) -> p a b", b=512)
        kend = scr_bf(20, 2, "p (a b) -> p a b", b=256)
        qcT = scr_bf(0, 8, "p (a b) -> p a b", b=512)
        PTc = scr_bf(8, 8, "p (a b) -> p a b", b=512)
        tok = [tokbuf[:, i * 1024:(i + 1) * 1024] for i in range(2)]
        ocat = tokbuf[:].bitcast(BF16).rearrange("p (a b) -> p a b", b=1024)

        R = {}
        def res(name):
            if name not in R:
                R[name] = Res(name)
            return R[name]
        rRT = [res("RT%d" % k) for k in range(8)]
        rhT = [res("hT%d" % k) for k in range(8)]
        rG = [res("G%d" % g) for g in range(NFC)]
        rring = [res("ring%d" % i) for i in range(NR)]
        rK = [[res("K%d_%d" % (h, t)) for t in range(NT)] for h in range(4)]
        rV = [res("V%d" % k) for k in range(32)]
        rocat = [res("ocat%d" % s) for s in range(4)]
        rtok = [[rocat[0], rocat[1]], [rocat[2], rocat[3]]]
        rsg = [res("sg%d" % i) for i in range(2)]
        rqd = [[res("qd%d_%d" % (c, h)) for h in range(4)] for c in range(2)]
        rxst = [res("xst%d" % i) for i in range(2)]
        rPT = [res("PT%d" % i) for i in range(4)]
        rzsq = [res("zsq%d" % i) for i in range(2)]
        rSbf = [res("Sbf%d" % i) for i in range(2)]
        rla = [res("la%d" % i) for i in range(2)]
        red = [res("ed%d" % i) for i in range(2)]

        def gr(g0, ng):
            return rG[g0:g0 + ng]

        def MM(out, lhsT, rhs, start, stop, r, w, skip=False):
            if skip:
                return S.add("pe", lambda e: e.matmul(out, lhsT=lhsT, rhs=rhs, start=start, stop=stop, skip_group_check=True), r, w)
            return S.add("pe", lambda e: e.matmul(out, lhsT=lhsT, rhs=rhs, start=start, stop=stop), r, w)

        def TR(out, in_, ident, r, w):
            return S.add("pe", lambda e: e.transpose(out=out, in_=in_, identity=ident), r, w)

        def A(out, in_, func, r, w, scale=None, bias=None):
            kw = {}
            if scale is not None:
                kw["scale"] = scale
            if bias is not None:
                kw["bias"] = bias
            return S.add("act", lambda e: e.activation(out=out, in_=in_, func=func, **kw), r, w)

        def TT_(eng, out, in0, in1, op, r, w):
            return S.add(eng, lambda e: e.tensor_tensor(out=out, in0=in0, in1=in1, op=op), r, w)

        def TS(eng, out, in0, s1, s2, op0, op1, r, w):
            if op1 is None:
                return S.add(eng, lambda e: e.tensor_scalar(out=out, in0=in0, scalar1=s1, scalar2=None, op0=op0), r, w)
            return S.add(eng, lambda e: e.tensor_scalar(out=out, in0=in0, scalar1=s1, scalar2=s2, op0=op0, op1=op1), r, w)

        def STT(out, in0, scalar, in1, op0, op1, r, w):
            return S.add("dve", lambda e: e.scalar_tensor_tensor(out=out, in0=in0, scalar=scalar, in1=in1, op0=op0, op1=op1), r, w)

        def CP(eng, out, in_, r, w):
            return S.add(eng, lambda e: e.tensor_copy(out=out, in_=in_), r, w)

        def RECIP(out, in_, r, w):
            return S.add("dve", lambda e: e.reciprocal(out=out, in_=in_), r, w)

        def RED(out, in_, r, w):
            return S.add("dve", lambda e: e.tensor_reduce(out=out, in_=in_, axis=AX.X, op=ALU.add), r, w)

        def MEMSET(eng, ap, val, r, w):
            return S.add(eng, lambda e: e.memset(ap, val), r, w)

        def DMA(eng, out, in_, r, w, lane):
            return S.add(eng, lambda e: e.dma_start(out=out, in_=in_), r, w, lane=lane)

        cnt = {"lane": 0, "cast": 0}

        def const_load(dst, src, r):
            cnt["lane"] += 1
            DMA("act", dst, src, [], [r], "k%d" % cnt["lane"])

        const_load(identf[:], identf_d, res("identf"))
        const_load(identb[:], identb_d, res("identb"))
        const_load(onesb[:], onesb_d, res("onesb"))
        const_load(lnp[:], lnp_d, res("lnp"))
        const_load(lamv[:], lamv_d, res("lamv"))
        const_load(gda4[:], dng_d, res("gda4"))
        const_load(gng4[:], gng_d, res("gng4"))
        const_load(w2a[:], w2a_d, res("w2a"))
        const_load(m1[:], m1_d, res("m1"))
        const_load(ind[:], ind_d, res("ind"))

        rW = {}

        def cast(name, c0, c1):
            key = (name, c0)
            rW[key] = Res("w_%s_%d" % key)
            lane = "cast%d" % (cnt["cast"] % 8)
            cnt["cast"] += 1
            DMA("pool", wbf[name][:, c0:c1], wsrc[name][:, c0:c1], [], [rW[key]], lane)

        def cast_cols(name, ncols, step=512):
            for c0 in range(0, ncols, step):
                cast(name, c0, min(ncols, c0 + step))

        def cast_ffn(i):
            for c0 in range(0, FH, 512):
                cast("wg%d" % i, c0, min(FH, c0 + 512))
                cast("wu%d" % i, c0, min(FH, c0 + 512))
            cast_cols("wd%d" % i, D)

        cast_cols("wkv", 2 * D)
        cast_ffn(1)
        cast_cols("win", 3072)
        cast("win", 3072, 3088)
        cast_cols("wout", D)
        cast_cols("wq", D)
        cast_cols("wo", D)
        cast_ffn(2)

        units = []

        def u_k8(name, c0, ncols):
            units.append((name, c0, ncols, 0, 8))

        def u_ffn(i):
            for c0 in range(0, FH, 512):
                u_k8("wg%d" % i, c0, min(512, FH - c0))
                u_k8("wu%d" % i, c0, min(512, FH - c0))
            for grp in range(2):
                for part in range(3):
                    nf = 8 if part < 2 else 6
                    units.append(("wd%d" % i, grp * 512, 512, part * 8, nf))

        for c0 in range(0, 2 * D, 512):
            u_k8("wkv", c0, 512)
        for t in range(nt):
            u_ffn(1)
            for c0 in range(0, 3072, 512):
                u_k8("win", c0, 512)
            for nm in ("wout", "wq", "wo"):
                u_k8(nm, 0, 512)
                u_k8(nm, 512, 512)
            u_ffn(2)

        ws = {"issued": 0, "acq": 0, "rel": 0}

        def ws_issue():
            i = ws["issued"]
            if i >= len(units) or (unit_limit is not None and i >= unit_limit):
                return
            name, c0, ncols, f0, nf = units[i]
            slot = i % NR
            src = wbf[name].rearrange("(kc p) n -> p kc n", p=128)[:, f0:f0 + nf, c0:c0 + ncols]
            dst = ring[slot][:, 0:nf * ncols].rearrange("p (a b) -> p a b", b=ncols)
            rr = [rW[(name, c)] for c in range((c0 // 512) * 512, c0 + ncols, 512)]
            DMA("sp", dst, src, rr, [rring[slot]], "ring%d" % slot)
            ws["issued"] += 1

        def acquire():
            i = ws["acq"]
            assert i < ws["issued"], "ring underflow"
            name, c0, ncols, f0, nf = units[i]
            slot = i % NR
            ws["acq"] += 1
            view = ring[slot][:, 0:nf * ncols].rearrange("p (a b) -> p a b", b=ncols)
            return view, rring[slot], units[i]

        def release():
            ws["rel"] += 1
            ws_issue()

        for _ in range(NR):
            ws_issue()

        DMA("sp", winglr[:], wbf["win"].rearrange("(kc p) n -> p kc n", p=128)[:, :, 3072:3088],
            [rW[("win", 3072)]], [res("winglr")], "k_winglr")

        rlnpa = res("lnpa")
        TS("dve", lnpa[:, 0:48], lnp[:, 0:48], ALPHA, None, ALU.mult, None, [res("lnp")], [rlnpa])
        CP("dve", lnpa[:, 48:64], lnp[:, 48:64], [res("lnp")], [rlnpa])
        rlam = res("lam")
        TT_("dve", lamt[:, 0:64], lamv[:, 0:64], lamv[:, 64:128], ALU.mult, [res("lamv")], [res("lamt")])
        RED(lams[:, 0:1], lamt[:, 0:64], [res("lamt")], [res("lams0")])
        TT_("dve", lamt[:, 0:64], lamv[:, 128:192], lamv[:, 192:256], ALU.mult, [res("lamv"), res("lams0")], [res("lamt")])
        RED(lams[:, 1:2], lamt[:, 0:64], [res("lamt")], [res("lams1")])
        A(lams[:, 2:4], lams[:, 0:2], AF.Exp, [res("lams0"), res("lams1")], [res("lams2")])
        TT_("dve", lams[:, 4:5], lams[:, 3:4], lams[:, 2:3], ALU.subtract, [res("lams2")], [res("lams4")])
        TS("dve", lams[:, 5:6], lams[:, 4:5], -LAMBDA_INIT, None, ALU.add, None, [res("lams4")], [rlam])
        neglam = lams[:, 5:6]
        TS("dve", gda4[:], gda4[:], 1.0 - LAMBDA_INIT, None, ALU.mult, None, [res("gda4")], [res("gda4")])
        MEMSET("pool", Vc[:, :, :, 128:129], 1.0, [], rV)
        MEMSET("pool", Vm[:, :, :, 256:257], 1.0, [], [res("Vm")])
        MEMSET("pool", glr[:], 1.0, [], [res("glr")])
        MEMSET("pool", qdz[0][64:128, :, :], 0.0, [], rqd[0])
        MEMSET("pool", qdz[1][0:64, :, :], 0.0, [], rqd[1])
        rSst = [res("Sst%d" % h) for h in range(4)]
        MEMSET("pool", Sst[:], 0.0, [], rSst)

        def transpose_in(src_tok, r_src, sub, scale_rt, xm=0):
            cs = slice(sub * 128, (sub + 1) * 128)
            for g in range(2):
                b = take()
                for j in range(4):
                    kc = g * 4 + j
                    TR(banks[b][:, j * 128:(j + 1) * 128], src_tok[:, kc * 128:(kc + 1) * 128], identf[:],
                       r_src + [res("identf")], [bres[b]])
                bv = banks[b][:].rearrange("p (a c) -> p a c", c=128)
                if scale_rt is not None:
                    A(RT[:, g * 4:(g + 1) * 4, cs], bv, AF.Copy, [bres[b]], rRT[g * 4:(g + 1) * 4], scale=scale_rt)
                    TS("dve", hT[:, g * 4:(g + 1) * 4, cs], RT[:, g * 4:(g + 1) * 4, cs], 1.0 / scale_rt, None, ALU.mult, None,
                       rRT[g * 4:(g + 1) * 4], rhT[g * 4:(g + 1) * 4])
                else:
                    CP("dve", hT[:, g * 4:(g + 1) * 4, cs], bv, [bres[b]], rhT[g * 4:(g + 1) * 4])
                free(b)

        def transpose_ocat():
            for sub in range(4):
                cs = slice(sub * 128, (sub + 1) * 128)
                for g in range(2):
                    b = take()
                    bvb = banks[b][:].bitcast(BF16)
                    for j in range(4):
                        kc = g * 4 + j
                        TR(bvb[:, j * 128:(j + 1) * 128], ocat[:, sub, kc * 128:(kc + 1) * 128], identb[:],
                           [rocat[sub], res("identb")], [bres[b]])
                    bv = bvb[:, 0:512].rearrange("p (a c) -> p a c", c=128)
                    if (sub + g) % 2 == 0:
                        CP("dve", hT[:, g * 4:(g + 1) * 4, cs], bv, [bres[b]], rhT[g * 4:(g + 1) * 4])
                    else:
                        A(hT[:, g * 4:(g + 1) * 4, cs], bv, AF.Copy, [bres[b]], rhT[g * 4:(g + 1) * 4])
                    free(b)

        st_ = {"zi": 0}

        def evac_residual(b, kc, factor):
            STT(RT[:, kc, :], banks[b][:], factor, RT[:, kc, :], ALU.mult, ALU.add, [bres[b], rRT[kc]], [rRT[kc]])
            A(hT[:, kc, :], RT[:, kc, :], AF.Copy, [rRT[kc]], [rhT[kc]])
            zi = st_["zi"] % 2
            st_["zi"] += 1
            A(zsq[zi][:], RT[:, kc, :], AF.Square, [rRT[kc]], [rzsq[zi]])
            return zi

        def stats_mm(kc, zi, bM, bQ):
            MM(banks[bM][:], onesb[:], hT[:, kc, :], kc == 0, kc == 7, [res("onesb"), rhT[kc]], [bres[bM]])
            MM(banks[bQ][:], onesb[:], zsq[zi][:], kc == 0, kc == 7, [res("onesb"), rzsq[zi]], [bres[bQ]])

        def layer_norm(li, bM, bQ, final=False):
            rl = res("lnt")
            A(lnt[:], banks[bM][:], AF.Square, [bres[bM]], [rl])
            TT_("dve", lnt[:], banks[bQ][:], lnt[:], ALU.subtract, [bres[bQ], rl], [rl])
            A(lnt[:], lnt[:], AF.Ln, [rl], [rl], bias=EPS)
            A(banks[bQ][:], lnt[:], AF.Exp, [rl], [bres[bQ]], scale=-0.5)
            gi, bi = 2 * li, 2 * li + 1
            for kc in range(8):
                TT_("dve", RT[:, kc, :], RT[:, kc, :], banks[bM][:], ALU.subtract, [rRT[kc], bres[bM]], [rRT[kc]])
                TT_("dve", RT[:, kc, :], RT[:, kc, :], banks[bQ][:], ALU.mult, [rRT[kc], bres[bQ]], [rRT[kc]])
                if not final:
                    A(hT[:, kc, :], RT[:, kc, :], AF.Identity, [rRT[kc], res("lnp")], [rhT[kc]],
                      scale=lnp[:, gi * 8 + kc:gi * 8 + kc + 1], bias=lnp[:, bi * 8 + kc:bi * 8 + kc + 1])
                A(RT[:, kc, :], RT[:, kc, :], AF.Identity, [rRT[kc], rlnpa], [rRT[kc]],
                  scale=lnpa[:, gi * 8 + kc:gi * 8 + kc + 1], bias=lnpa[:, bi * 8 + kc:bi * 8 + kc + 1])
            free(bQ)
            free(bM)

        def proj_residual_ln(n_k, rhs_fn, r_rhs_fn, factor, li, final=False, wd=False):
            grp_banks = []
            for grp in range(2):
                bs = [take() for _ in range(4)]
                grp_banks.append(bs)
                nparts = 3 if wd else 1
                for part in range(nparts):
                    view, rs, u = acquire()
                    nf = u[4]
                    for dmc in range(4):
                        for fl in range(nf):
                            f = part * 8 + fl
                            MM(banks[bs[dmc]][:], view[:, fl, dmc * 128:(dmc + 1) * 128], rhs_fn(f), f == 0, f == n_k - 1,
                               [rs] + r_rhs_fn(f), [bres[bs[dmc]]])
                    release()
            bM = bQ = None
            pend = []
            for grp in range(2):
                for dmc in range(4):
                    kc = grp * 4 + dmc
                    zi = evac_residual(grp_banks[grp][dmc], kc, factor)
                    free(grp_banks[grp][dmc])
                    pend.append((kc, zi))
                    if kc == 0:
                        continue
                    if bM is None:
                        bM = take()
                        bQ = take()
                    for (k2, z2) in pend:
                        stats_mm(k2, z2, bM, bQ)
                    pend = []
            layer_norm(li, bM, bQ, final)

        def ffn(li):
            si = 0
            for fg in range(6):
                vg_, rg_, ug_ = acquire()
                vu_, ru_, uu_ = acquire()
                ncols = ug_[2]
                for j in range(ncols // 128):
                    fc = fg * 4 + j
                    bG = take()
                    bU = take()
                    for kc in range(8):
                        MM(banks[bG][:], vg_[:, kc, j * 128:(j + 1) * 128], hT[:, kc, :], kc == 0, kc == 7, [rg_, rhT[kc]], [bres[bG]])
                    for kc in range(8):
                        MM(banks[bU][:], vu_[:, kc, j * 128:(j + 1) * 128], hT[:, kc, :], kc == 0, kc == 7, [ru_, rhT[kc]], [bres[bU]])
                    i = si % 2
                    si += 1
                    A(sg[i][:], banks[bG][:], AF.Silu, [bres[bG]], [rsg[i]])
                    TT_("dve", HT[:, fc, :], sg[i][:], banks[bU][:], ALU.mult, [rsg[i], bres[bU]], [rG[fc]])
                    free(bG)
                    free(bU)
                release()
                release()
            proj_residual_ln(NFC, lambda f: HT[:, f, :], lambda f: [rG[f]], 0.5, li, final=(li == 3), wd=True)

        def proj_fm(n_chunks, lhs_cols, evac):
            view, rs, u = acquire()
            for c in range(n_chunks):
                b = take()
                for kc in range(8):
                    MM(banks[b][0:lhs_cols, :], view[:, kc, c * lhs_cols:(c + 1) * lhs_cols], hT[:, kc, :], kc == 0, kc == 7,
                       [rs, rhT[kc]], [bres[b]])
                evac(c, b)
                free(b)
            return view, rs

        def proj_tm(view, rs, c0, ncols, evac):
            for sub in range(4):
                b = take()
                for kc in range(8):
                    MM(banks[b][:, 0:ncols], hT[:, kc, sub * 128:(sub + 1) * 128], view[:, kc, c0:c0 + ncols], kc == 0, kc == 7,
                       [rs, rhT[kc]], [bres[b]])
                evac(sub, b)
                free(b)

        def rms_rs(ss_ap, r_in, n):
            A(sm[:, 16:20], ss_ap, AF.Ln, r_in, [res("sm_ln")], scale=1.0 / n, bias=EPS)
            A(sm[:, 20:24], sm[:, 16:20], AF.Exp, [res("sm_ln")], [res("sm_rs")], scale=-0.5)

        def mem_kv():
            for ms in range(2):
                DMA("sp", tok[ms], mem_d[ms * 128:(ms + 1) * 128, :], [], rtok[ms], "tok%d" % ms)
            for ms in range(2):
                transpose_in(tok[ms], rtok[ms], ms, None)
            for uu in range(2):
                view, rs, u = acquire()
                for c in range(4):
                    ch = uu * 4 + c
                    b = take()
                    for kc in range(8):
                        MM(banks[b][:, 0:256], view[:, kc, c * 128:(c + 1) * 128], hT[:, kc, 0:256], kc == 0, kc == 7, [rs, rhT[kc]], [bres[b]])
                    A(kmT[:, ch, :], banks[b][:, 0:256], AF.Copy, [bres[b]], [res("kmT")])
                    free(b)
                release()
            for uu in range(2):
                view, rs, u = acquire()
                for ms in range(2):
                    b = take()
                    for kc in range(8):
                        MM(banks[b][:], hT[:, kc, ms * 128:(ms + 1) * 128], view[:, kc, :], kc == 0, kc == 7, [rs, rhT[kc]], [bres[b]])
                    CP("dve", Vm[:, ms, uu * 2:(uu + 1) * 2, 0:256], banks[b][:].rearrange("p (a c) -> p a c", c=256), [bres[b]], [res("Vm")])
                    free(b)
                release()

        def x_dma(t, sub):
            i = sub % 2
            r0 = t * TT + sub * 128
            DMA("sp", xst[i][:], x_d[r0:r0 + 128, :], [], [rxst[i]], "xst%d" % i)

        def load_x(t):
            if t == 0:
                x_dma(0, 0)
                x_dma(0, 1)
            for sub in range(4):
                i = sub % 2
                transpose_in(xst[i][:], [rxst[i]], sub, ALPHA)
                if sub + 2 < 4:
                    x_dma(t, sub + 2)
                elif t + 1 < nt:
                    x_dma(t + 1, sub - 2)

        def w_in_phase(t):
            def ev_q(h, b):
                A(qdz[0][0:64, h, :], banks[b][0:64, :], AF.Copy, [bres[b]], [rqd[0][h]])
                A(qdz[1][64:128, h, :], banks[b][64:128, :], AF.Copy, [bres[b]], [rqd[1][h]])
            proj_fm(4, 128, ev_q)
            release()
            proj_fm(4, 128, lambda h, b: CP("dve", Kc[:, h, t * TT:(t + 1) * TT], banks[b][:], [bres[b]], [rK[h][t]]))
            release()
            view, rs, u = acquire()
            proj_tm(view, rs, 0, 512, lambda sub, b: A(Vc[:, 4 * t + sub, :, 0:128], banks[b][:].rearrange("p (a c) -> p a c", c=128),
                                                       AF.Copy, [bres[b]], [rV[4 * t + sub]]))
            release()
            view, rs = proj_fm(4, 64, lambda h, b: A(qgT[0:64, h, :], banks[b][0:64, :], AF.Copy, [bres[b]], [rG[4 + h]], scale=0.125))
            proj_tm(view, rs, 256, 256, lambda sub, b: CP("dve", kg_sb[:, sub, :], banks[b][:, 0:256], [bres[b]], [rG[8 + sub]]))
            b = take()
            for kc in range(8):
                MM(banks[b][0:16, :], winglr[:, kc, :], hT[:, kc, :], kc == 0, kc == 7, [res("winglr"), rhT[kc]], [bres[b]])
            CP("dve", glr[0:16, :], banks[b][0:16, :], [bres[b]], [res("glr")])
            free(b)
            release()
            view, rs, u = acquire()
            proj_tm(view, rs, 0, 512, lambda sub, b: CP("dve", vg_sb[:, sub, :], banks[b][:], [bres[b]], [rG[12 + sub]]))
            release()
            view, rs, u = acquire()

            def ev_r(sub, b):
                i = sub % 2
                A(sg[i][:], banks[b][:], AF.Silu, [bres[b]], [rsg[i]])
                TT_("pool", gate[:, sub, :], sg[i][:], gng4[:], ALU.mult, [rsg[i], res("gng4")], [rG[16 + sub]])
            proj_tm(view, rs, 0, 512, ev_r)
            release()

        pti = [0]

        def da_head_half(t, h, c, pump=None):
            nkt = 4 * t + 4
            ps = slice(c * 64, (c + 1) * 64)
            accb = [take(), take()]

            def acc(s):
                return banks[accb[s // 2]][:, (s % 2) * 256:(s % 2) * 256 + 129]

            def qk(kt):
                j = kt - 4 * t
                q0 = 128 * j if j >= 0 else 0
                b = take()
                MM(banks[b][:, q0:512], Kc[:, h, kt * 128:(kt + 1) * 128], qdz[c][:, h, q0:512], True, True,
                   [rK[h][kt // 4], rqd[c][h]], [bres[b]])
                i = pti[0] % 4
                pti[0] += 1
                A(PT[i][:, q0:512], banks[b][:, q0:512], AF.Exp, [bres[b]], [rPT[i]], scale=0.125)
                if j >= 0:
                    MEMSET("pool", PT[i][64:128, q0:q0 + 64], 0.0, [], [rPT[i]])
                free(b)
                return i, j

            def pv(kt, i, j):
                for s in range(max(j, 0), 4):
                    MM(acc(s), PT[i][:, s * 128:(s + 1) * 128], Vc[:, kt, h, :], (kt == 0 and s % 2 == 0), kt == 4 * t + s,
                       [rPT[i], rV[kt]], [bres[accb[s // 2]]], skip=True)

            DEPTH = 2
            pend = [qk(k) for k in range(min(DEPTH, nkt))]
            for kt in range(nkt):
                cur = pend.pop(0)
                if kt + DEPTH < nkt:
                    pend.append(qk(kt + DEPTH))
                pv(kt, cur[0], cur[1])
                if pump is not None:
                    pump()
            rb = [bres[accb[0]], bres[accb[1]]]
            if c == 0:
                for s in range(4):
                    RECIP(sm[:, s:s + 1], acc(s)[:, 128:129], [rb[s // 2]], [res("sm_r0_%d" % s)])
                    TS("dve", o0[:, s, :], acc(s)[:, 0:128], sm[:, s:s + 1], None, ALU.mult, None,
                       [rb[s // 2], res("sm_r0_%d" % s)], [res("o0")])
            else:
                for s in range(4):
                    RECIP(sm[:, 4 + s:5 + s], acc(s)[:, 128:129], [rb[s // 2]], [res("sm_r1")])
                TS("dve", sm[:, 8:12], sm[:, 4:8], neglam, None, ALU.mult, None, [res("sm_r1"), rlam], [res("sm_r1l")])
                for s in range(4):
                    STT(od[:, s, :], acc(s)[:, 0:128], sm[:, 8 + s:9 + s], o0[:, s, :], ALU.mult, ALU.add,
                        [rb[s // 2], res("sm_r1l"), res("o0")], [res("od")])
                A(sg[0][:], od[:].rearrange("p a b -> p (a b)"), AF.Square, [res("od")], [rsg[0]])
                RED(sm[:, 12:16], sg[0][:].rearrange("p (a b) -> p a b", b=128), [rsg[0]], [res("sm_ss")])
                rms_rs(sm[:, 12:16], [res("sm_ss")], 128.0)
                TT_("dve", od[:], od[:], sm[:, 20:24].unsqueeze(2).to_broadcast([128, 4, 128]), ALU.mult,
                    [res("od"), res("sm_rs")], [res("od")])
                TT_("pool", ocat[:, :, h * 128:(h + 1) * 128], od[:], gda4[:].rearrange("p (a b) -> p a b", b=128), ALU.mult,
                    [res("od"), res("gda4")], rocat)
            free(accb[0])
            free(accb[1])

        def da_phase(t):
            gen = gla_gen(t)
            state = {"done": False, "n": 0}
            per = max(1, (8 * (4 * t + 4)) // 30)

            def pump():
                state["n"] += 1
                if state["done"] or state["n"] % per != 0:
                    return
                try:
                    next(gen)
                except StopIteration:
                    state["done"] = True
            for h in range(4):
                for c in range(2):
                    da_head_half(t, h, c, pump)
            if not state["done"]:
                for _ in gen:
                    pass

        sbi = [0]

        def gla_sub_gen(t, sub):
            i = sub % 2
            cs = slice(sub * 128, (sub + 1) * 128)
            bz = take()
            MM(banks[bz][:, 0:256], glr[0:32, cs], w2a[0:32, :], True, True, [res("glr"), res("w2a")], [bres[bz]])
            A(la[i][:], banks[bz][:, 0:256], AF.Exp, [bres[bz]], [rla[i]], scale=-1.0)
            A(la[i][:], la[i][:], AF.Ln, [rla[i]], [rla[i]], bias=1.0)
            free(bz)
            yield
            bD = take()
            MM(banks[bD][:, 0:256], m1[:], la[i][:], True, True, [res("m1"), rla[i]], [bres[bD]])
            for h in range(4):
                MM(banks[bD][0:64, 256 + 2 * h:258 + 2 * h], la[i][:, h * 64:(h + 1) * 64], ind[:], True, True,
                   [res("ind"), rla[i]], [bres[bD]], skip=True)
            A(ed[i][:], banks[bD][:, 0:256], AF.Exp, [bres[bD]], [red[i]])
            A(dec[:, sub, :], banks[bD][0:64, 256:264], AF.Exp, [bres[bD]], [res("dec")])
            TT_("dve", kend[:, sub, :], kg_sb[:, sub, :], ed[i][:], ALU.mult, [rG[8 + sub], red[i]], [rG[20 + sub // 2]])
            free(bD)
            yield
            bO = take()
            for c in range(2):
                ps = slice(c * 64, (c + 1) * 64)
                bS = take()
                for h in range(4):
                    MM(banks[bS][0:64, h * 128:(h + 1) * 128], kend[ps, sub, h * 64:(h + 1) * 64], vg_sb[ps, sub, h * 128:(h + 1) * 128],
                       True, True, [rG[20 + sub // 2], rG[12 + sub]], [bres[bS]], skip=True)
                for h in range(4):
                    STT(Sst[:, h * 128:(h + 1) * 128], Sst[:, h * 128:(h + 1) * 128], dec[:, sub, 2 * h + c:2 * h + c + 1],
                        banks[bS][0:64, h * 128:(h + 1) * 128], ALU.mult, ALU.add, [rSst[h], res("dec"), bres[bS]], [rSst[h]])
                k = sbi[0] % 2
                sbi[0] += 1
                CP("pool", Sbf[k][:], Sst[:], rSst, [rSbf[k]])
                free(bS)
                yield
                for h in range(4):
                    MM(banks[bO][ps, h * 128:(h + 1) * 128], qgT[0:64, h, sub * 128 + c * 64:sub * 128 + (c + 1) * 64],
                       Sbf[k][:, h * 128:(h + 1) * 128], True, True, [rG[4 + h], rSbf[k]], [bres[bO]], skip=True)
                yield
            A(sg[1][:], banks[bO][:], AF.Square, [bres[bO]], [rsg[1]])
            RED(sm[:, 12:16], sg[1][:].rearrange("p (a b) -> p a b", b=128), [rsg[1]], [res("sm_ss")])
            rms_rs(sm[:, 12:16], [res("sm_ss")], 128.0)
            TT_("dve", sg[1][:].rearrange("p (a b) -> p a b", b=128), banks[bO][:].rearrange("p (a b) -> p a b", b=128),
                sm[:, 20:24].unsqueeze(2).to_broadcast([128, 4, 128]), ALU.mult, [bres[bO], res("sm_rs"), rsg[1]], [rsg[1]])
            TT_("pool", ocat[:, sub, 512:1024], sg[1][:], gate[:, sub, :], ALU.mult, [rsg[1], rG[16 + sub]], [rocat[sub]])
            free(bO)

        def gla_gen(t):
            for sub in range(4):
                yield from gla_sub_gen(t, sub)

        def gla_phase(t):
            for _ in gla_gen(t):
                pass

        rci = [0]

        def cross_phase():
            for uu in range(2):
                proj_fm(4, 128, lambda c, b, uu=uu: A(qcT[:, uu * 4 + c, :], banks[b][:], AF.Copy, [bres[b]], [rG[uu * 4 + c]]))
                release()
            def scores(hd):
                for ms in range(2):
                    b = take()
                    for j in range(2):
                        MM(banks[b][:], kmT[:, 2 * hd + j, ms * 128:(ms + 1) * 128], qcT[:, 2 * hd + j, :], j == 0, j == 1,
                           [res("kmT"), rG[2 * hd + j]], [bres[b]])
                    A(PTc[:, hd * 2 + ms, :], banks[b][:], AF.Exp, [bres[b]], [rG[8 + hd * 2 + ms]], scale=1.0 / 16.0)
                    free(b)

            def pvx(hd):
                for sub in range(4):
                    b = take()
                    for ms in range(2):
                        MM(banks[b][:, 0:257], PTc[:, hd * 2 + ms, sub * 128:(sub + 1) * 128], Vm[:, ms, hd, :], ms == 0, ms == 1,
                           [rG[8 + hd * 2 + ms], res("Vm")], [bres[b]])
                    k = 24 + (rci[0] % 8)
                    rci[0] += 1
                    RECIP(sm[:, k:k + 1], banks[b][:, 256:257], [bres[b]], [res("sm_rc%d" % k)])
                    A(ocat[:, sub, hd * 256:(hd + 1) * 256], banks[b][:, 0:256], AF.Identity, [bres[b], res("sm_rc%d" % k)], [rocat[sub]],
                      scale=sm[:, k:k + 1])
                    free(b)

            scores(0)
            scores(1)
            for hd in range(4):
                pvx(hd)
                if hd + 2 < 4:
                    scores(hd + 2)

        def store_out(t):
            for sub in range(4):
                i = sub % 2
                for g in range(2):
                    b = take()
                    for j in range(4):
                        kc = g * 4 + j
                        TR(banks[b][:, j * 128:(j + 1) * 128], RT[:, kc, sub * 128:(sub + 1) * 128], identf[:], [rRT[kc], res("identf")], [bres[b]])
                    if g == 0:
                        A(tok[i][:, g * 512:(g + 1) * 512], banks[b][:], AF.Copy, [bres[b]], rtok[i])
                    else:
                        CP("dve", tok[i][:, g * 512:(g + 1) * 512], banks[b][:], [bres[b]], rtok[i])
                    free(b)
                r0 = t * TT + sub * 128
                DMA("sp", out_d[r0:r0 + 128, :], tok[i], rtok[i], [], "tok%d" % i)

        S.phase = "memkv"
        mem_kv()
        for t in range(nt):
            S.phase = "t%d:x" % t
            if not (skip_lastx and t == nt - 1):
                load_x(t)
            if stop_after == "x" and t == nt - 1:
                break
            S.phase = "t%d:ffn1" % t
            ffn(0)
            if stop_after == "ln1" and t == nt - 1:
                break
            S.phase = "t%d:win" % t
            w_in_phase(t)
            if stop_after == "win" and t == nt - 1:
                break
            S.phase = "t%d:da" % t
            da_phase(t)
            if stop_after == "da" and t == nt - 1:
                break
            if stop_after == "gla" and t == nt - 1:
                break
            S.phase = "t%d:wout" % t
            transpose_ocat()
            proj_residual_ln(8, lambda f: hT[:, f, :], lambda f: [rhT[f]], 1.0, 1)
            if stop_after == "ln2" and t == nt - 1:
                break
            S.phase = "t%d:cross" % t
            cross_phase()
            S.phase = "t%d:wo" % t
            transpose_ocat()
            proj_residual_ln(8, lambda f: hT[:, f, :], lambda f: [rhT[f]], 1.0, 2)
            if stop_after == "ln3" and t == nt - 1:
                break
            S.phase = "t%d:ffn2" % t
            ffn(3)
            S.phase = "t%d:out" % t
            store_out(t)
        if dbg:
            DMA("sp", dbg_d, RT[:], rRT, [], "dbg")
            DMA("sp", dbg2_d, hT[:], rhT, [], "dbg2")
        if stop_after is None:
            assert ws["acq"] == len(units) == ws["rel"], (ws, len(units))
        S.finalize()
        S.emit(nc, st, final_lanes=["tok0", "tok1", "dbg", "dbg2"])
    return nc


def make_in_maps(inputs, cores):
    f = lambda a: np.ascontiguousarray(np.asarray(a, dtype=np.float32))
    g = {k: f(v) for k, v in inputs.items()}
    lnp = np.zeros((128, 64), np.float32)
    for i, nm in enumerate(["ln1_g", "ln1_b", "ln2_g", "ln2_b", "ln3_g", "ln3_b", "ln4_g", "ln4_b"]):
        lnp[:, i * 8:(i + 1) * 8] = g[nm][0].reshape(8, 128).T
    lamv = np.concatenate([g["da_lambda_q1"][0], g["da_lambda_k1"][0], g["da_lambda_q2"][0], g["da_lambda_k2"][0]])
    lamv = np.ascontiguousarray(np.broadcast_to(lamv[None, :], (128, 256)))
    dng4 = np.ascontiguousarray(np.broadcast_to(np.tile(g["da_norm_g"][0], 4)[None, :], (128, 512)))
    gng4 = np.ascontiguousarray(np.broadcast_to(np.tile(g["gla_norm_g"][0], 4)[None, :], (128, 512)))
    w2aug = np.zeros((32, 256), np.float32)
    w2aug[0:16] = g["gla_w_gate2"][0]
    w2aug[16] = g["gla_b_gate"][0]
    s_idx = np.arange(128)
    same = (s_idx[:, None] // 64) == (s_idx[None, :] // 64)
    m1 = np.where(same & (s_idx[:, None] > s_idx[None, :]), -1.0 / 16.0, 0.0).astype(np.float32)
    ind = np.zeros((128, 2), np.float32)
    ind[0:64, 0] = -1.0 / 16.0
    ind[64:128, 1] = -1.0 / 16.0
    common = {
        "ffn1_w_gate": g["ffn1_w_gate"][0], "ffn1_w_up": g["ffn1_w_up"][0], "ffn1_w_down": g["ffn1_w_down"][0],
        "w_in": g["w_in"][0], "w_out": g["w_out"][0], "cross_wq": g["cross_wq"][0], "cross_wkv": g["cross_wkv"][0],
        "cross_wo": g["cross_wo"][0], "ffn2_w_gate": g["ffn2_w_gate"][0], "ffn2_w_up": g["ffn2_w_up"][0],
        "ffn2_w_down": g["ffn2_w_down"][0],
        "lnp": lnp, "lamv": lamv, "dng4": dng4, "gng4": gng4, "w2aug": w2aug,
        "identf": np.eye(128, dtype=np.float32), "identb": np.eye(128, dtype=np.float32).astype(ml_dtypes.bfloat16),
        "m1": m1, "ind": ind, "onesb": np.full((128, 128), 1.0 / 1024.0, np.float32).astype(ml_dtypes.bfloat16),
    }
    maps = []
    for c in cores:
        m = dict(common)
        m["x"] = g["x"][c]
        m["mem"] = g["mem"][c]
        maps.append(m)
    return maps


_CACHE = {}


def kernel(**inputs):
    if "nc" not in _CACHE:
        _CACHE["nc"] = build_program()
    nc = _CACHE["nc"]
    cores = list(range(8))
    in_maps = make_in_maps(inputs, cores)
    res = run_bass_kernel_spmd(nc, in_maps, core_ids=cores)
    out = np.stack([np.asarray(r["out"], dtype=np.float32) for r in res.results], axis=0)
    return out
```

```python
import contextlib
import numpy as np
import ml_dtypes
import concourse.bass as bass
import concourse.mybir as mybir
from concourse.bass_utils import run_bass_kernel_spmd

F32 = mybir.dt.float32
BF16 = mybir.dt.bfloat16
AF = mybir.ActivationFunctionType
ALU = mybir.AluOpType
AX = mybir.AxisListType

NT = 8
TT = 512
SEQ = 4096
D = 1024
FH = 2816
NFC = 22
INW = 3088
ALPHA = 2.0 ** 0.25
EPS = 1e-5
LAMBDA_INIT = 0.8 - 0.6 * 1.0
NR = 4
SLOT = 4096


class Res:
    __slots__ = ("name", "w", "rs", "excl")

    def __init__(self, name, excl=False):
        self.name = name
        self.w = None
        self.rs = []
        self.excl = excl


class Op:
    __slots__ = ("eng", "fn", "deps", "dma", "lane", "needs", "idx", "laneval", "waits", "phase")

    def __init__(self, eng, fn, dma, lane):
        self.phase = None
        self.eng = eng
        self.fn = fn
        self.dma = dma
        self.lane = lane
        self.deps = []
        self.needs = False
        self.idx = None
        self.laneval = None
        self.waits = []


ENGS = ("pe", "act", "dve", "pool", "sp")
NROT = 4


class Sched:
    def __init__(self):
        self.ops = {e: [] for e in ENGS}
        self.lane_tok = {}
        self.phase = ""

    def add(self, eng, fn, reads=(), writes=(), lane=None):
        dma = lane is not None
        op = Op(eng, fn, dma, lane)
        op.phase = self.phase
        reads = list(reads)
        writes = list(writes)
        if dma:
            if lane not in self.lane_tok:
                self.lane_tok[lane] = Res("lane_" + str(lane))
            writes.append(self.lane_tok[lane])
        raw = set()
        other = set()
        for r in reads:
            if r.w is not None:
                raw.add(r.w)
            if r.excl:
                for q in r.rs:
                    if q.eng != eng:
                        other.add(q)
        for w in writes:
            if w.w is not None:
                other.add(w.w)
            for q in w.rs:
                other.add(q)
        deps = set()
        for d in raw | other:
            if d is op:
                continue
            if (not d.dma) and (not dma) and d.eng == eng:
                if eng == "pe":
                    continue
                if d not in raw:
                    continue
            deps.add(d)
        op.deps = list(deps)
        for d in deps:
            d.needs = True
        for r in reads:
            if r.excl:
                r.rs = [op]
                continue
            if not dma:
                r.rs = [q for q in r.rs if q.dma or q.eng != eng]
            r.rs.append(op)
        for w in writes:
            w.w = op
            w.rs = []
        self.ops[eng].append(op)
        return op

    def finalize(self):
        lanecnt = {}
        for e in ENGS:
            c = 0
            for op in self.ops[e]:
                if op.dma:
                    lanecnt[op.lane] = lanecnt.get(op.lane, 0) + 1
                    op.laneval = 16 * lanecnt[op.lane]
                elif op.needs:
                    c += 1
                    op.idx = c
        self.lanecnt = lanecnt
        for e in ENGS:
            waited = {}
            for op in self.ops[e]:
                best = {}
                for d in op.deps:
                    if d.dma:
                        key = ("lane", d.lane)
                        val = d.laneval
                    else:
                        key = ("eng", d.eng)
                        val = d.idx
                    if waited.get(key, 0) >= val:
                        continue
                    if best.get(key, 0) < val:
                        best[key] = val
                for key, val in best.items():
                    waited[key] = val
                    op.waits.append((key, val))

    def emit(self, nc, stack, final_lanes=()):
        sems = {}
        for e in ENGS:
            for r in range(NROT):
                sems[("eng", e, r)] = stack.enter_context(nc.semaphore("s_%s%d" % (e, r)))
        for lane in self.lanecnt:
            sems[("lane", lane)] = stack.enter_context(nc.semaphore("l_%s" % str(lane)))
        block = stack.enter_context(nc.Block())
        names = {"pe": "tensor", "act": "scalar", "dve": "vector", "pool": "gpsimd", "sp": "sync"}

        def run(e, eng):
            for op in self.ops[e]:
                for key, val in op.waits:
                    if key[0] == "lane":
                        eng.wait_ge(sems[key], val)
                    else:
                        j = val - 1
                        eng.wait_ge(sems[("eng", key[1], j % NROT)], j // NROT + 1)
                inst = op.fn(eng)
                if op.dma:
                    inst.then_inc(sems[("lane", op.lane)], 16)
                elif op.idx is not None:
                    j = op.idx - 1
                    inst.then_inc(sems[("eng", e, j % NROT)], 1)
            if e == "sp":
                for lane in self.lanecnt:
                    eng.wait_ge(sems[("lane", lane)], 16 * self.lanecnt[lane])

        for e in ENGS:
            getattr(block, names[e])(lambda eng, e=e: run(e, eng))


def build_program(nt=NT, dbg=False, stop_after=None, unit_limit=None, skip_lastx=False, xmode=0):
    nc = bass.Bass("TRN2", target_bir_lowering=False, dynamic_dma_scratch_size=8192)
    S = Sched()

    def din(name, shape, dt=F32):
        return nc.dram_tensor(name, list(shape), dt, kind="ExternalInput").ap()

    x_d = din("x", [SEQ, D])
    mem_d = din("mem", [256, D])
    wsrc = {
        "wg1": din("ffn1_w_gate", [D, FH]), "wu1": din("ffn1_w_up", [D, FH]), "wd1": din("ffn1_w_down", [FH, D]),
        "win": din("w_in", [D, INW]), "wout": din("w_out", [D, D]), "wq": din("cross_wq", [D, D]),
        "wkv": din("cross_wkv", [D, 2 * D]), "wo": din("cross_wo", [D, D]),
        "wg2": din("ffn2_w_gate", [D, FH]), "wu2": din("ffn2_w_up", [D, FH]), "wd2": din("ffn2_w_down", [FH, D]),
    }
    lnp_d = din("lnp", [128, 64])
    lamv_d = din("lamv", [128, 256])
    dng_d = din("dng4", [128, 512])
    gng_d = din("gng4", [128, 512])
    w2a_d = din("w2aug", [32, 256])
    identf_d = din("identf", [128, 128])
    identb_d = din("identb", [128, 128], BF16)
    m1_d = din("m1", [128, 128])
    ind_d = din("ind", [128, 2])
    onesb_d = din("onesb", [128, 128], BF16)
    out_d = nc.dram_tensor("out", [SEQ, D], F32, kind="ExternalOutput").ap()
    wbf = {k: nc.dram_tensor("bf_" + k, list(v.shape), BF16, kind="Internal").ap() for k, v in wsrc.items()}
    if dbg:
        dbg_d = nc.dram_tensor("dbg", [128, 8, 512], F32, kind="ExternalOutput").ap()
        dbg2_d = nc.dram_tensor("dbg2", [128, 8, 512], BF16, kind="ExternalOutput").ap()

    with contextlib.ExitStack() as st:
        def sb(n, s, d):
            return st.enter_context(nc.sbuf_tensor(n, list(s), d))

        Kc = sb("Kc", [128, 4, SEQ], BF16)
        Vc = sb("Vc", [128, 32, 4, 129], BF16)
        ring = [sb("ring%d" % i, [128, SLOT], BF16) for i in range(NR)]
        RT = sb("RT", [128, 8, TT], F32)
        hT = sb("hT", [128, 8, TT], BF16)
        HT = sb("HT", [128, NFC, TT], BF16)
        tokbuf = sb("tokbuf", [128, 2048], F32)
        qdz = [sb("qdz%d" % i, [128, 4, 512], BF16) for i in range(2)]
        xst = [sb("xst%d" % i, [128, 1024], F32) for i in range(2)]
        sg = [sb("sg%d" % i, [128, 512], F32) for i in range(2)]
        PT = [sb("PT%d" % i, [128, 512], BF16) for i in range(4)]
        zsq = [sb("zsq%d" % i, [128, 512], BF16) for i in range(2)]
        lnt = sb("lnt", [128, 512], F32)
        Sst = sb("Sst", [64, 512], F32)
        Sbf = [sb("Sbf%d" % i, [64, 512], BF16) for i in range(2)]
        glr = sb("glr", [32, 512], F32)
        w2a = sb("w2a", [32, 256], F32)
        winglr = sb("winglr", [128, 8, 16], BF16)
        identf = sb("identf_s", [128, 128], F32)
        identb = sb("identb_s", [128, 128], BF16)
        m1 = sb("m1_s", [128, 128], F32)
        ind = sb("ind_s", [128, 2], F32)
        onesb = sb("onesb_s", [128, 128], BF16)
        lnp = sb("lnp_s", [128, 64], F32)
        lnpa = sb("lnpa_s", [128, 64], F32)
        lamv = sb("lamv_s", [128, 256], F32)
        lamt = sb("lamt", [128, 64], F32)
        lams = sb("lams", [128, 8], F32)
        gda4 = sb("gda4", [128, 512], F32)
        gng4 = sb("gng4_s", [128, 512], F32)
        kmT = sb("kmT", [128, 8, 256], BF16)
        Vm = sb("Vm", [128, 2, 4, 257], BF16)
        la = [sb("la%d" % i, [128, 256], F32) for i in range(2)]
        ed = [sb("ed%d" % i, [128, 256], F32) for i in range(2)]
        dec = sb("dec", [64, 4, 8], F32)
        o0 = sb("o0", [128, 4, 128], F32)
        od = sb("od", [128, 4, 128], F32)
        sm = sb("sm", [128, 32], F32)
        banks = [st.enter_context(nc.psum_tensor("pb%d" % i, [128, 512], F32)) for i in range(8)]
        bres = [Res("pb%d" % i, excl=True) for i in range(8)]
        free_banks = list(range(8))

        def take():
            return free_banks.pop(0)

        def free(b):
            free_banks.append(b)

        scr = HT[:].rearrange("p a b -> p (a b)")
        def scr_bf(g0, ng, shape_str=None, **kw):
            v = scr[:, g0 * 512:(g0 + ng) * 512]
            return v.rearrange(shape_str, **kw) if shape_str else v

        def scr_f32(g0, ng, shape_str=None, **kw):
            v = scr[:, g0 * 512:(g0 + ng) * 512].bitcast(F32)
            return v.rearrange(shape_str, **kw) if shape_str else v

        qdT = scr_bf(0, 4, "p (a b) -> p a b", b=512)
        qgT = scr_bf(4, 4, "p (a b) -> p a b", b=512)
        kg_sb = scr_f32(8, 4, "p (a b) -> p a b", b=256)
        vg_sb = scr_bf(12, 4, "p (a b) -> p a b", b=512)
        gate = scr_bf(16, 4, "p (a b) -> p a b", b=512)
        kend = scr_bf(20, 2, "p (a b) -> p a b", b=256)
        qcT = scr_bf(0, 8, "p (a b) -> p a b", b=512)
        PTc = scr_bf(8, 8, "p (a b) -> p a b", b=512)
        tok = [tokbuf[:, i * 1024:(i + 1) * 1024] for i in range(2)]
        ocat = tokbuf[:].bitcast(BF16).rearrange("p (a b) -> p a b", b=1024)

        R = {}
        def res(name):
            if name not in R:
                R[name] = Res(name)
            return R[name]
        rRT = [res("RT%d" % k) for k in range(8)]
        rhT = [res("hT%d" % k) for k in range(8)]
        rG = [res("G%d" % g) for g in range(NFC)]
        rring = [res("ring%d" % i) for i in range(NR)]
        rK = [[res("K%d_%d" % (h, t)) for t in range(NT)] for h in range(4)]
        rV = [res("V%d" % k) for k in range(32)]
        rocat = [res("ocat%d" % s) for s in range(4)]
        rtok = [[rocat[0], rocat[1]], [rocat[2], rocat[3]]]
        rsg = [res("sg%d" % i) for i in range(2)]
        rqd = [[res("qd%d_%d" % (c, h)) for h in range(4)] for c in range(2)]
        rxst = [res("xst%d" % i) for i in range(2)]
        rPT = [res("PT%d" % i) for i in range(4)]
        rzsq = [res("zsq%d" % i) for i in range(2)]
        rSbf = [res("Sbf%d" % i) for i in range(2)]
        rla = [res("la%d" % i) for i in range(2)]
        red = [res("ed%d" % i) for i in range(2)]

        def gr(g0, ng):
            return rG[g0:g0 + ng]

        def MM(out, lhsT, rhs, start, stop, r, w, skip=False):
            if skip:
                return S.add("pe", lambda e: e.matmul(out, lhsT=lhsT, rhs=rhs, start=start, stop=stop, skip_group_check=True), r, w)
            return S.add("pe", lambda e: e.matmul(out, lhsT=lhsT, rhs=rhs, start=start, stop=stop), r, w)

        def TR(out, in_, ident, r, w):
            return S.add("pe", lambda e: e.transpose(out=out, in_=in_, identity=ident), r, w)

        def A(out, in_, func, r, w, scale=None, bias=None):
            kw = {}
            if scale is not None:
                kw["scale"] = scale
            if bias is not None:
                kw["bias"] = bias
            return S.add("act", lambda e: e.activation(out=out, in_=in_, func=func, **kw), r, w)

        def TT_(eng, out, in0, in1, op, r, w):
            return S.add(eng, lambda e: e.tensor_tensor(out=out, in0=in0, in1=in1, op=op), r, w)

        def TS(eng, out, in0, s1, s2, op0, op1, r, w):
            if op1 is None:
                return S.add(eng, lambda e: e.tensor_scalar(out=out, in0=in0, scalar1=s1, scalar2=None, op0=op0), r, w)
            return S.add(eng, lambda e: e.tensor_scalar(out=out, in0=in0, scalar1=s1, scalar2=s2, op0=op0, op1=op1), r, w)

        def STT(out, in0, scalar, in1, op0, op1, r, w):
            return S.add("dve", lambda e: e.scalar_tensor_tensor(out=out, in0=in0, scalar=scalar, in1=in1, op0=op0, op1=op1), r, w)

        def CP(eng, out, in_, r, w):
            return S.add(eng, lambda e: e.tensor_copy(out=out, in_=in_), r, w)

        def RECIP(out, in_, r, w):
            return S.add("dve", lambda e: e.reciprocal(out=out, in_=in_), r, w)

        def RED(out, in_, r, w):
            return S.add("dve", lambda e: e.tensor_reduce(out=out, in_=in_, axis=AX.X, op=ALU.add), r, w)

        def MEMSET(eng, ap, val, r, w):
            return S.add(eng, lambda e: e.memset(ap, val), r, w)

        def DMA(eng, out, in_, r, w, lane):
            return S.add(eng, lambda e: e.dma_start(out=out, in_=in_), r, w, lane=lane)

        cnt = {"lane": 0, "cast": 0}

        def const_load(dst, src, r):
            cnt["lane"] += 1
            DMA("act", dst, src, [], [r], "k%d" % cnt["lane"])

        const_load(identf[:], identf_d, res("identf"))
        const_load(identb[:], identb_d, res("identb"))
        const_load(onesb[:], onesb_d, res("onesb"))
        const_load(lnp[:], lnp_d, res("lnp"))
        const_load(lamv[:], lamv_d, res("lamv"))
        const_load(gda4[:], dng_d, res("gda4"))
        const_load(gng4[:], gng_d, res("gng4"))
        const_load(w2a[:], w2a_d, res("w2a"))
        const_load(m1[:], m1_d, res("m1"))
        const_load(ind[:], ind_d, res("ind"))

        rW = {}

        def cast(name, c0, c1):
            key = (name, c0)
            rW[key] = Res("w_%s_%d" % key)
            lane = "cast%d" % (cnt["cast"] % 2)
            cnt["cast"] += 1
            DMA("pool", wbf[name][:, c0:c1], wsrc[name][:, c0:c1], [], [rW[key]], lane)

        def cast_cols(name, ncols, step=512):
            for c0 in range(0, ncols, step):
                cast(name, c0, min(ncols, c0 + step))

        def cast_ffn(i):
            for c0 in range(0, FH, 512):
                cast("wg%d" % i, c0, min(FH, c0 + 512))
                cast("wu%d" % i, c0, min(FH, c0 + 512))
            cast_cols("wd%d" % i, D)

        cast_cols("wkv", 2 * D)
        cast_ffn(1)
        cast_cols("win", 3072)
        cast("win", 3072, 3088)
        cast_cols("wout", D)
        cast_cols("wq", D)
        cast_cols("wo", D)
        cast_ffn(2)

        units = []

        def u_k8(name, c0, ncols):
            units.append((name, c0, ncols, 0, 8))

        def u_ffn(i):
            for c0 in range(0, FH, 512):
                u_k8("wg%d" % i, c0, min(512, FH - c0))
                u_k8("wu%d" % i, c0, min(512, FH - c0))
            for grp in range(2):
                for part in range(3):
                    nf = 8 if part < 2 else 6
                    units.append(("wd%d" % i, grp * 512, 512, part * 8, nf))

        for c0 in range(0, 2 * D, 512):
            u_k8("wkv", c0, 512)
        for t in range(nt):
            u_ffn(1)
            for c0 in range(0, 3072, 512):
                u_k8("win", c0, 512)
            for nm in ("wout", "wq", "wo"):
                u_k8(nm, 0, 512)
                u_k8(nm, 512, 512)
            u_ffn(2)

        ws = {"issued": 0, "acq": 0, "rel": 0}

        def ws_issue():
            i = ws["issued"]
            if i >= len(units) or (unit_limit is not None and i >= unit_limit):
                return
            name, c0, ncols, f0, nf = units[i]
            slot = i % NR
            src = wbf[name].rearrange("(kc p) n -> p kc n", p=128)[:, f0:f0 + nf, c0:c0 + ncols]
            dst = ring[slot][:, 0:nf * ncols].rearrange("p (a b) -> p a b", b=ncols)
            rr = [rW[(name, c)] for c in range((c0 // 512) * 512, c0 + ncols, 512)]
            DMA("sp", dst, src, rr, [rring[slot]], "ring%d" % slot)
            ws["issued"] += 1

        def acquire():
            i = ws["acq"]
            assert i < ws["issued"], "ring underflow"
            name, c0, ncols, f0, nf = units[i]
            slot = i % NR
            ws["acq"] += 1
            view = ring[slot][:, 0:nf * ncols].rearrange("p (a b) -> p a b", b=ncols)
            return view, rring[slot], units[i]

        def release():
            ws["rel"] += 1
            ws_issue()

        for _ in range(NR):
            ws_issue()

        DMA("sp", winglr[:], wbf["win"].rearrange("(kc p) n -> p kc n", p=128)[:, :, 3072:3088],
            [rW[("win", 3072)]], [res("winglr")], "k_winglr")

        rlnpa = res("lnpa")
        TS("dve", lnpa[:, 0:48], lnp[:, 0:48], ALPHA, None, ALU.mult, None, [res("lnp")], [rlnpa])
        CP("dve", lnpa[:, 48:64], lnp[:, 48:64], [res("lnp")], [rlnpa])
        rlam = res("lam")
        TT_("dve", lamt[:, 0:64], lamv[:, 0:64], lamv[:, 64:128], ALU.mult, [res("lamv")], [res("lamt")])
        RED(lams[:, 0:1], lamt[:, 0:64], [res("lamt")], [res("lams0")])
        TT_("dve", lamt[:, 0:64], lamv[:, 128:192], lamv[:, 192:256], ALU.mult, [res("lamv"), res("lams0")], [res("lamt")])
        RED(lams[:, 1:2], lamt[:, 0:64], [res("lamt")], [res("lams1")])
        A(lams[:, 2:4], lams[:, 0:2], AF.Exp, [res("lams0"), res("lams1")], [res("lams2")])
        TT_("dve", lams[:, 4:5], lams[:, 3:4], lams[:, 2:3], ALU.subtract, [res("lams2")], [res("lams4")])
        TS("dve", lams[:, 5:6], lams[:, 4:5], -LAMBDA_INIT, None, ALU.add, None, [res("lams4")], [rlam])
        neglam = lams[:, 5:6]
        TS("dve", gda4[:], gda4[:], 1.0 - LAMBDA_INIT, None, ALU.mult, None, [res("gda4")], [res("gda4")])
        MEMSET("pool", Vc[:, :, :, 128:129], 1.0, [], rV)
        MEMSET("pool", Vm[:, :, :, 256:257], 1.0, [], [res("Vm")])
        MEMSET("pool", glr[:], 1.0, [], [res("glr")])
        MEMSET("pool", qdz[0][64:128, :, :], 0.0, [], rqd[0])
        MEMSET("pool", qdz[1][0:64, :, :], 0.0, [], rqd[1])
        rSst = [res("Sst%d" % h) for h in range(4)]
        MEMSET("pool", Sst[:], 0.0, [], rSst)

        def transpose_in(src_tok, r_src, sub, scale_rt, xm=0):
            cs = slice(sub * 128, (sub + 1) * 128)
            for g in range(2):
                b = take()
                for j in range(4):
                    kc = g * 4 + j
                    TR(banks[b][:, j * 128:(j + 1) * 128], src_tok[:, kc * 128:(kc + 1) * 128], identf[:],
                       r_src + [res("identf")], [bres[b]])
                bv = banks[b][:].rearrange("p (a c) -> p a c", c=128)
                if scale_rt is not None:
                    A(RT[:, g * 4:(g + 1) * 4, cs], bv, AF.Copy, [bres[b]], rRT[g * 4:(g + 1) * 4], scale=scale_rt)
                    TS("dve", hT[:, g * 4:(g + 1) * 4, cs], RT[:, g * 4:(g + 1) * 4, cs], 1.0 / scale_rt, None, ALU.mult, None,
                       rRT[g * 4:(g + 1) * 4], rhT[g * 4:(g + 1) * 4])
                else:
                    CP("dve", hT[:, g * 4:(g + 1) * 4, cs], bv, [bres[b]], rhT[g * 4:(g + 1) * 4])
                free(b)

        def transpose_ocat():
            for sub in range(4):
                cs = slice(sub * 128, (sub + 1) * 128)
                for g in range(2):
                    b = take()
                    bvb = banks[b][:].bitcast(BF16)
                    for j in range(4):
                        kc = g * 4 + j
                        TR(bvb[:, j * 128:(j + 1) * 128], ocat[:, sub, kc * 128:(kc + 1) * 128], identb[:],
                           [rocat[sub], res("identb")], [bres[b]])
                    bv = bvb[:, 0:512].rearrange("p (a c) -> p a c", c=128)
                    if (sub + g) % 2 == 0:
                        CP("dve", hT[:, g * 4:(g + 1) * 4, cs], bv, [bres[b]], rhT[g * 4:(g + 1) * 4])
                    else:
                        A(hT[:, g * 4:(g + 1) * 4, cs], bv, AF.Copy, [bres[b]], rhT[g * 4:(g + 1) * 4])
                    free(b)

        st_ = {"zi": 0}

        def evac_residual(b, kc, factor):
            STT(RT[:, kc, :], banks[b][:], factor, RT[:, kc, :], ALU.mult, ALU.add, [bres[b], rRT[kc]], [rRT[kc]])
            A(hT[:, kc, :], RT[:, kc, :], AF.Copy, [rRT[kc]], [rhT[kc]])
            zi = st_["zi"] % 2
            st_["zi"] += 1
            A(zsq[zi][:], RT[:, kc, :], AF.Square, [rRT[kc]], [rzsq[zi]])
            return zi

        def stats_mm(kc, zi, bM, bQ):
            MM(banks[bM][:], onesb[:], hT[:, kc, :], kc == 0, kc == 7, [res("onesb"), rhT[kc]], [bres[bM]])
            MM(banks[bQ][:], onesb[:], zsq[zi][:], kc == 0, kc == 7, [res("onesb"), rzsq[zi]], [bres[bQ]])

        def layer_norm(li, bM, bQ, final=False):
            rl = res("lnt")
            A(lnt[:], banks[bM][:], AF.Square, [bres[bM]], [rl])
            TT_("dve", lnt[:], banks[bQ][:], lnt[:], ALU.subtract, [bres[bQ], rl], [rl])
            A(lnt[:], lnt[:], AF.Ln, [rl], [rl], bias=EPS)
            A(banks[bQ][:], lnt[:], AF.Exp, [rl], [bres[bQ]], scale=-0.5)
            gi, bi = 2 * li, 2 * li + 1
            for kc in range(8):
                TT_("dve", RT[:, kc, :], RT[:, kc, :], banks[bM][:], ALU.subtract, [rRT[kc], bres[bM]], [rRT[kc]])
                TT_("dve", RT[:, kc, :], RT[:, kc, :], banks[bQ][:], ALU.mult, [rRT[kc], bres[bQ]], [rRT[kc]])
                if not final:
                    A(hT[:, kc, :], RT[:, kc, :], AF.Identity, [rRT[kc], res("lnp")], [rhT[kc]],
                      scale=lnp[:, gi * 8 + kc:gi * 8 + kc + 1], bias=lnp[:, bi * 8 + kc:bi * 8 + kc + 1])
                A(RT[:, kc, :], RT[:, kc, :], AF.Identity, [rRT[kc], rlnpa], [rRT[kc]],
                  scale=lnpa[:, gi * 8 + kc:gi * 8 + kc + 1], bias=lnpa[:, bi * 8 + kc:bi * 8 + kc + 1])
            free(bQ)
            free(bM)

        def proj_residual_ln(n_k, rhs_fn, r_rhs_fn, factor, li, final=False, wd=False):
            grp_banks = []
            for grp in range(2):
                bs = [take() for _ in range(4)]
                grp_banks.append(bs)
                nparts = 3 if wd else 1
                for part in range(nparts):
                    view, rs, u = acquire()
                    nf = u[4]
                    for dmc in range(4):
                        for fl in range(nf):
                            f = part * 8 + fl
                            MM(banks[bs[dmc]][:], view[:, fl, dmc * 128:(dmc + 1) * 128], rhs_fn(f), f == 0, f == n_k - 1,
                               [rs] + r_rhs_fn(f), [bres[bs[dmc]]])
                    release()
            bM = bQ = None
            pend = []
            for grp in range(2):
                for dmc in range(4):
                    kc = grp * 4 + dmc
                    zi = evac_residual(grp_banks[grp][dmc], kc, factor)
                    free(grp_banks[grp][dmc])
                    pend.append((kc, zi))
                    if kc == 0:
                        continue
                    if bM is None:
                        bM = take()
                        bQ = take()
                    for (k2, z2) in pend:
                        stats_mm(k2, z2, bM, bQ)
                    pend = []
            layer_norm(li, bM, bQ, final)

        def ffn(li):
            si = 0
            for fg in range(6):
                vg_, rg_, ug_ = acquire()
                vu_, ru_, uu_ = acquire()
                ncols = ug_[2]
                for j in range(ncols // 128):
                    fc = fg * 4 + j
                    bG = take()
                    bU = take()
                    for kc in range(8):
                        MM(banks[bG][:], vg_[:, kc, j * 128:(j + 1) * 128], hT[:, kc, :], kc == 0, kc == 7, [rg_, rhT[kc]], [bres[bG]])
                    for kc in range(8):
                        MM(banks[bU][:], vu_[:, kc, j * 128:(j + 1) * 128], hT[:, kc, :], kc == 0, kc == 7, [ru_, rhT[kc]], [bres[bU]])
                    i = si % 2
                    si += 1
                    A(sg[i][:], banks[bG][:], AF.Silu, [bres[bG]], [rsg[i]])
                    TT_("dve", HT[:, fc, :], sg[i][:], banks[bU][:], ALU.mult, [rsg[i], bres[bU]], [rG[fc]])
                    free(bG)
                    free(bU)
                release()
                release()
            proj_residual_ln(NFC, lambda f: HT[:, f, :], lambda f: [rG[f]], 0.5, li, final=(li == 3), wd=True)

        def proj_fm(n_chunks, lhs_cols, evac):
            view, rs, u = acquire()
            for c in range(n_chunks):
                b = take()
                for kc in range(8):
                    MM(banks[b][0:lhs_cols, :], view[:, kc, c * lhs_cols:(c + 1) * lhs_cols], hT[:, kc, :], kc == 0, kc == 7,
                       [rs, rhT[kc]], [bres[b]])
                evac(c, b)
                free(b)
            return view, rs

        def proj_tm(view, rs, c0, ncols, evac):
            for sub in range(4):
                b = take()
                for kc in range(8):
                    MM(banks[b][:, 0:ncols], hT[:, kc, sub * 128:(sub + 1) * 128], view[:, kc, c0:c0 + ncols], kc == 0, kc == 7,
                       [rs, rhT[kc]], [bres[b]])
                evac(sub, b)
                free(b)

        def rms_rs(ss_ap, r_in, n):
            A(sm[:, 16:20], ss_ap, AF.Ln, r_in, [res("sm_ln")], scale=1.0 / n, bias=EPS)
            A(sm[:, 20:24], sm[:, 16:20], AF.Exp, [res("sm_ln")], [res("sm_rs")], scale=-0.5)

        def mem_kv():
            for ms in range(2):
                DMA("sp", tok[ms], mem_d[ms * 128:(ms + 1) * 128, :], [], rtok[ms], "tok%d" % ms)
            for ms in range(2):
                transpose_in(tok[ms], rtok[ms], ms, None)
            for uu in range(2):
                view, rs, u = acquire()
                for c in range(4):
                    ch = uu * 4 + c
                    b = take()
                    for kc in range(8):
                        MM(banks[b][:, 0:256], view[:, kc, c * 128:(c + 1) * 128], hT[:, kc, 0:256], kc == 0, kc == 7, [rs, rhT[kc]], [bres[b]])
                    A(kmT[:, ch, :], banks[b][:, 0:256], AF.Copy, [bres[b]], [res("kmT")])
                    free(b)
                release()
            for uu in range(2):
                view, rs, u = acquire()
                for ms in range(2):
                    b = take()
                    for kc in range(8):
                        MM(banks[b][:], hT[:, kc, ms * 128:(ms + 1) * 128], view[:, kc, :], kc == 0, kc == 7, [rs, rhT[kc]], [bres[b]])
                    CP("dve", Vm[:, ms, uu * 2:(uu + 1) * 2, 0:256], banks[b][:].rearrange("p (a c) -> p a c", c=256), [bres[b]], [res("Vm")])
                    free(b)
                release()

        def x_dma(t, sub):
            i = sub % 2
            r0 = t * TT + sub * 128
            DMA("sp", xst[i][:], x_d[r0:r0 + 128, :], [], [rxst[i]], "xst%d" % i)

        def load_x(t):
            if t == 0:
                x_dma(0, 0)
                x_dma(0, 1)
            for sub in range(4):
                i = sub % 2
                transpose_in(xst[i][:], [rxst[i]], sub, ALPHA)
                if sub + 2 < 4:
                    x_dma(t, sub + 2)
                elif t + 1 < nt:
                    x_dma(t + 1, sub - 2)

        def w_in_phase(t):
            def ev_q(h, b):
                A(qdz[0][0:64, h, :], banks[b][0:64, :], AF.Copy, [bres[b]], [rqd[0][h]])
                A(qdz[1][64:128, h, :], banks[b][64:128, :], AF.Copy, [bres[b]], [rqd[1][h]])
            proj_fm(4, 128, ev_q)
            release()
            proj_fm(4, 128, lambda h, b: CP("dve", Kc[:, h, t * TT:(t + 1) * TT], banks[b][:], [bres[b]], [rK[h][t]]))
            release()
            view, rs, u = acquire()
            proj_tm(view, rs, 0, 512, lambda sub, b: A(Vc[:, 4 * t + sub, :, 0:128], banks[b][:].rearrange("p (a c) -> p a c", c=128),
                                                       AF.Copy, [bres[b]], [rV[4 * t + sub]]))
            release()
            view, rs = proj_fm(4, 64, lambda h, b: A(qgT[0:64, h, :], banks[b][0:64, :], AF.Copy, [bres[b]], [rG[4 + h]], scale=0.125))
            proj_tm(view, rs, 256, 256, lambda sub, b: CP("dve", kg_sb[:, sub, :], banks[b][:, 0:256], [bres[b]], [rG[8 + sub]]))
            b = take()
            for kc in range(8):
                MM(banks[b][0:16, :], winglr[:, kc, :], hT[:, kc, :], kc == 0, kc == 7, [res("winglr"), rhT[kc]], [bres[b]])
            CP("dve", glr[0:16, :], banks[b][0:16, :], [bres[b]], [res("glr")])
            free(b)
            release()
            view, rs, u = acquire()
            proj_tm(view, rs, 0, 512, lambda sub, b: CP("dve", vg_sb[:, sub, :], banks[b][:], [bres[b]], [rG[12 + sub]]))
            release()
            view, rs, u = acquire()

            def ev_r(sub, b):
                i = sub % 2
                A(sg[i][:], banks[b][:], AF.Silu, [bres[b]], [rsg[i]])
                TT_("pool", gate[:, sub, :], sg[i][:], gng4[:], ALU.mult, [rsg[i], res("gng4")], [rG[16 + sub]])
            proj_tm(view, rs, 0, 512, ev_r)
            release()

        pti = [0]

        def da_phase(t):
            nkt = 4 * t + 4
            gen = gla_gen(t)
            state = {"done": False, "n": 0}
            per = max(1, (8 * nkt) // 30)

            def pump():
                state["n"] += 1
                if state["done"] or state["n"] % per != 0:
                    return
                try:
                    next(gen)
                except StopIteration:
                    state["done"] = True

            def qk(h, c, kt):
                j = kt - 4 * t
                q0 = 128 * j if j >= 0 else 0
                b = take()
                MM(banks[b][:, q0:512], Kc[:, h, kt * 128:(kt + 1) * 128], qdz[c][:, h, q0:512], True, True,
                   [rK[h][kt // 4], rqd[c][h]], [bres[b]])
                i = pti[0] % 4
                pti[0] += 1
                A(PT[i][:, q0:512], banks[b][:, q0:512], AF.Exp, [bres[b]], [rPT[i]], scale=0.125)
                if j >= 0:
                    MEMSET("pool", PT[i][64:128, q0:q0 + 64], 0.0, [], [rPT[i]])
                free(b)
                return i, j

            def acc_ap(accb, s):
                return banks[accb[s // 2]][:, (s % 2) * 256:(s % 2) * 256 + 129]

            def pv(h, kt, i, j, accb):
                for s in range(max(j, 0), 4):
                    MM(acc_ap(accb, s), PT[i][:, s * 128:(s + 1) * 128], Vc[:, kt, h, :], (kt == 0 and s % 2 == 0), kt == 4 * t + s,
                       [rPT[i], rV[kt]], [bres[accb[s // 2]]], skip=True)

            def epilogue(h, c, accb):
                rb = [bres[accb[0]], bres[accb[1]]]
                if c == 0:
                    for s in range(4):
                        RECIP(sm[:, s:s + 1], acc_ap(accb, s)[:, 128:129], [rb[s // 2]], [res("sm_r0_%d" % s)])
                        TS("dve", o0[:, s, :], acc_ap(accb, s)[:, 0:128], sm[:, s:s + 1], None, ALU.mult, None,
                           [rb[s // 2], res("sm_r0_%d" % s)], [res("o0")])
                else:
                    for s in range(4):
                        RECIP(sm[:, 4 + s:5 + s], acc_ap(accb, s)[:, 128:129], [rb[s // 2]], [res("sm_r1")])
                    TS("dve", sm[:, 8:12], sm[:, 4:8], neglam, None, ALU.mult, None, [res("sm_r1"), rlam], [res("sm_r1l")])
                    for s in range(4):
                        STT(od[:, s, :], acc_ap(accb, s)[:, 0:128], sm[:, 8 + s:9 + s], o0[:, s, :], ALU.mult, ALU.add,
                            [rb[s // 2], res("sm_r1l"), res("o0")], [res("od")])
                    A(sg[0][:], od[:].rearrange("p a b -> p (a b)"), AF.Square, [res("od")], [rsg[0]])
                    RED(sm[:, 12:16], sg[0][:].rearrange("p (a b) -> p a b", b=128), [rsg[0]], [res("sm_ss")])
                    rms_rs(sm[:, 12:16], [res("sm_ss")], 128.0)
                    TT_("dve", od[:], od[:], sm[:, 20:24].unsqueeze(2).to_broadcast([128, 4, 128]), ALU.mult,
                        [res("od"), res("sm_rs")], [res("od")])
                    TT_("pool", ocat[:, :, h * 128:(h + 1) * 128], od[:], gda4[:].rearrange("p (a b) -> p a b", b=128), ALU.mult,
                        [res("od"), res("gda4")], rocat)
                free(accb[0])
                free(accb[1])

            steps = [(h, c, kt) for h in range(4) for c in range(2) for kt in range(nkt)]
            DEPTH = 2
            pend = [qk(*steps[k]) for k in range(min(DEPTH, len(steps)))]
            accb = None
            for n, (h, c, kt) in enumerate(steps):
                cur = pend.pop(0)
                if n + DEPTH < len(steps):
                    pend.append(qk(*steps[n + DEPTH]))
                if kt == 0:
                    accb = [take(), take()]
                pv(h, kt, cur[0], cur[1], accb)
                if kt == nkt - 1:
                    epilogue(h, c, accb)
                pump()
            if not state["done"]:
                for _ in gen:
                    pass

        sbi = [0]

        def gla_sub_gen(t, sub):
            i = sub % 2
            cs = slice(sub * 128, (sub + 1) * 128)
            bz = take()
            MM(banks[bz][:, 0:256], glr[0:32, cs], w2a[0:32, :], True, True, [res("glr"), res("w2a")], [bres[bz]])
            A(la[i][:], banks[bz][:, 0:256], AF.Exp, [bres[bz]], [rla[i]], scale=-1.0)
            A(la[i][:], la[i][:], AF.Ln, [rla[i]], [rla[i]], bias=1.0)
            free(bz)
            yield
            bD = take()
            MM(banks[bD][:, 0:256], m1[:], la[i][:], True, True, [res("m1"), rla[i]], [bres[bD]])
            for h in range(4):
                MM(banks[bD][0:64, 256 + 2 * h:258 + 2 * h], la[i][:, h * 64:(h + 1) * 64], ind[:], True, True,
                   [res("ind"), rla[i]], [bres[bD]], skip=True)
            A(ed[i][:], banks[bD][:, 0:256], AF.Exp, [bres[bD]], [red[i]])
            A(dec[:, sub, :], banks[bD][0:64, 256:264], AF.Exp, [bres[bD]], [res("dec")])
            TT_("dve", kend[:, sub, :], kg_sb[:, sub, :], ed[i][:], ALU.mult, [rG[8 + sub], red[i]], [rG[20 + sub // 2]])
            free(bD)
            yield
            bO = take()
            for c in range(2):
                ps = slice(c * 64, (c + 1) * 64)
                bS = take()
                for h in range(4):
                    MM(banks[bS][0:64, h * 128:(h + 1) * 128], kend[ps, sub, h * 64:(h + 1) * 64], vg_sb[ps, sub, h * 128:(h + 1) * 128],
                       True, True, [rG[20 + sub // 2], rG[12 + sub]], [bres[bS]], skip=True)
                for h in range(4):
                    STT(Sst[:, h * 128:(h + 1) * 128], Sst[:, h * 128:(h + 1) * 128], dec[:, sub, 2 * h + c:2 * h + c + 1],
                        banks[bS][0:64, h * 128:(h + 1) * 128], ALU.mult, ALU.add, [rSst[h], res("dec"), bres[bS]], [rSst[h]])
                k = sbi[0] % 2
                sbi[0] += 1
                CP("pool", Sbf[k][:], Sst[:], rSst, [rSbf[k]])
                free(bS)
                yield
                for h in range(4):
                    MM(banks[bO][ps, h * 128:(h + 1) * 128], qgT[0:64, h, sub * 128 + c * 64:sub * 128 + (c + 1) * 64],
                       Sbf[k][:, h * 128:(h + 1) * 128], True, True, [rG[4 + h], rSbf[k]], [bres[bO]], skip=True)
                yield
            A(sg[1][:], banks[bO][:], AF.Square, [bres[bO]], [rsg[1]])
            RED(sm[:, 12:16], sg[1][:].rearrange("p (a b) -> p a b", b=128), [rsg[1]], [res("sm_ss")])
            rms_rs(sm[:, 12:16], [res("sm_ss")], 128.0)
            TT_("dve", sg[1][:].rearrange("p (a b) -> p a b", b=128), banks[bO][:].rearrange("p (a b) -> p a b", b=128),
                sm[:, 20:24].unsqueeze(2).to_broadcast([128, 4, 128]), ALU.mult, [bres[bO], res("sm_rs"), rsg[1]], [rsg[1]])
            TT_("pool", ocat[:, sub, 512:1024], sg[1][:], gate[:, sub, :], ALU.mult, [rsg[1], rG[16 + sub]], [rocat[sub]])
            free(bO)

        def gla_gen(t):
            for sub in range(4):
                yield from gla_sub_gen(t, sub)

        def gla_phase(t):
            for _ in gla_gen(t):
                pass

        rci = [0]

        def cross_phase():
            for uu in range(2):
                proj_fm(4, 128, lambda c, b, uu=uu: A(qcT[:, uu * 4 + c, :], banks[b][:], AF.Copy, [bres[b]], [rG[uu * 4 + c]]))
                release()
            def scores(hd):
                for ms in range(2):
                    b = take()
                    for j in range(2):
                        MM(banks[b][:], kmT[:, 2 * hd + j, ms * 128:(ms + 1) * 128], qcT[:, 2 * hd + j, :], j == 0, j == 1,
                           [res("kmT"), rG[2 * hd + j]], [bres[b]])
                    A(PTc[:, hd * 2 + ms, :], banks[b][:], AF.Exp, [bres[b]], [rG[8 + hd * 2 + ms]], scale=1.0 / 16.0)
                    free(b)

            def pvx(hd):
                for sub in range(4):
                    b = take()
                    for ms in range(2):
                        MM(banks[b][:, 0:257], PTc[:, hd * 2 + ms, sub * 128:(sub + 1) * 128], Vm[:, ms, hd, :], ms == 0, ms == 1,
                           [rG[8 + hd * 2 + ms], res("Vm")], [bres[b]])
                    k = 24 + (rci[0] % 8)
                    rci[0] += 1
                    RECIP(sm[:, k:k + 1], banks[b][:, 256:257], [bres[b]], [res("sm_rc%d" % k)])
                    A(ocat[:, sub, hd * 256:(hd + 1) * 256], banks[b][:, 0:256], AF.Identity, [bres[b], res("sm_rc%d" % k)], [rocat[sub]],
                      scale=sm[:, k:k + 1])
                    free(b)

            scores(0)
            scores(1)
            for hd in range(4):
                pvx(hd)
                if hd + 2 < 4:
                    scores(hd + 2)

        def store_out(t):
            for sub in range(4):
                i = sub % 2
                for g in range(2):
                    b = take()
                    for j in range(4):
                        kc = g * 4 + j
                        TR(banks[b][:, j * 128:(j + 1) * 128], RT[:, kc, sub * 128:(sub + 1) * 128], identf[:], [rRT[kc], res("identf")], [bres[b]])
                    if g == 0:
                        A(tok[i][:, g * 512:(g + 1) * 512], banks[b][:], AF.Copy, [bres[b]], rtok[i])
                    else:
                        CP("dve", tok[i][:, g * 512:(g + 1) * 512], banks[b][:], [bres[b]], rtok[i])
                    free(b)
                r0 = t * TT + sub * 128
                DMA("sp", out_d[r0:r0 + 128, :], tok[i], rtok[i], [], "tok%d" % i)

        S.phase = "memkv"
        mem_kv()
        for t in range(nt):
            S.phase = "t%d:x" % t
            if not (skip_lastx and t == nt - 1):
                load_x(t)
            if stop_after == "x" and t == nt - 1:
                break
            S.phase = "t%d:ffn1" % t
            ffn(0)
            if stop_after == "ln1" and t == nt - 1:
                break
            S.phase = "t%d:win" % t
            w_in_phase(t)
            if stop_after == "win" and t == nt - 1:
                break
            S.phase = "t%d:da" % t
            da_phase(t)
            if stop_after == "da" and t == nt - 1:
                break
            if stop_after == "gla" and t == nt - 1:
                break
            S.phase = "t%d:wout" % t
            transpose_ocat()
            proj_residual_ln(8, lambda f: hT[:, f, :], lambda f: [rhT[f]], 1.0, 1)
            if stop_after == "ln2" and t == nt - 1:
                break
            S.phase = "t%d:cross" % t
            cross_phase()
            S.phase = "t%d:wo" % t
            transpose_ocat()
            proj_residual_ln(8, lambda f: hT[:, f, :], lambda f: [rhT[f]], 1.0, 2)
            if stop_after == "ln3" and t == nt - 1:
                break
            S.phase = "t%d:ffn2" % t
            ffn(3)
            S.phase = "t%d:out" % t
            store_out(t)
        if dbg:
            DMA("sp", dbg_d, RT[:], rRT, [], "dbg")
            DMA("sp", dbg2_d, hT[:], rhT, [], "dbg2")
        if stop_after is None:
            assert ws["acq"] == len(units) == ws["rel"], (ws, len(units))
        S.finalize()
        S.emit(nc, st, final_lanes=["tok0", "tok1", "dbg", "dbg2"])
    return nc


def make_in_maps(inputs, cores):
    f = lambda a: np.ascontiguousarray(np.asarray(a, dtype=np.float32))
    g = {k: f(v) for k, v in inputs.items()}
    lnp = np.zeros((128, 64), np.float32)
    for i, nm in enumerate(["ln1_g", "ln1_b", "ln2_g", "ln2_b", "ln3_g", "ln3_b", "ln4_g", "ln4_b"]):
        lnp[:, i * 8:(i + 1) * 8] = g[nm][0].reshape(8, 128).T
    lamv = np.concatenate([g["da_lambda_q1"][0], g["da_lambda_k1"][0], g["da_lambda_q2"][0], g["da_lambda_k2"][0]])
    lamv = np.ascontiguousarray(np.broadcast_to(lamv[None, :], (128, 256)))
    dng4 = np.ascontiguousarray(np.broadcast_to(np.tile(g["da_norm_g"][0], 4)[None, :], (128, 512)))
    gng4 = np.ascontiguousarray(np.broadcast_to(np.tile(g["gla_norm_g"][0], 4)[None, :], (128, 512)))
    w2aug = np.zeros((32, 256), np.float32)
    w2aug[0:16] = g["gla_w_gate2"][0]
    w2aug[16] = g["gla_b_gate"][0]
    s_idx = np.arange(128)
    same = (s_idx[:, None] // 64) == (s_idx[None, :] // 64)
    m1 = np.where(same & (s_idx[:, None] > s_idx[None, :]), -1.0 / 16.0, 0.0).astype(np.float32)
    ind = np.zeros((128, 2), np.float32)
    ind[0:64, 0] = -1.0 / 16.0
    ind[64:128, 1] = -1.0 / 16.0
    common = {
        "ffn1_w_gate": g["ffn1_w_gate"][0], "ffn1_w_up": g["ffn1_w_up"][0], "ffn1_w_down": g["ffn1_w_down"][0],
        "w_in": g["w_in"][0], "w_out": g["w_out"][0], "cross_wq": g["cross_wq"][0], "cross_wkv": g["cross_wkv"][0],
        "cross_wo": g["cross_wo"][0], "ffn2_w_gate": g["ffn2_w_gate"][0], "ffn2_w_up": g["ffn2_w_up"][0],
        "ffn2_w_down": g["ffn2_w_down"][0],
        "lnp": lnp, "lamv": lamv, "dng4": dng4, "gng4": gng4, "w2aug": w2aug,
        "identf": np.eye(128, dtype=np.float32), "identb": np.eye(128, dtype=np.float32).astype(ml_dtypes.bfloat16),
        "m1": m1, "ind": ind, "onesb": np.full((128, 128), 1.0 / 1024.0, np.float32).astype(ml_dtypes.bfloat16),
    }
    maps = []
    for c in cores:
        m = dict(common)
        m["x"] = g["x"][c]
        m["mem"] = g["mem"][c]
        maps.append(m)
    return maps


_CACHE = {}


def kernel(**inputs):
    if "nc" not in _CACHE:
        _CACHE["nc"] = build_program()
    nc = _CACHE["nc"]
    cores = list(range(8))
    in_maps = make_in_maps(inputs, cores)
    res = run_bass_kernel_spmd(nc, in_maps, core_ids=cores)
    out = np.stack([np.asarray(r["out"], dtype=np.float32) for r in res.results], axis=0)
    return out
```

```python
import contextlib
import numpy as np
import ml_dtypes
import concourse.bass as bass
import concourse.mybir as mybir
from concourse.bass_utils import run_bass_kernel_spmd

F32 = mybir.dt.float32
BF16 = mybir.dt.bfloat16
AF = mybir.ActivationFunctionType
ALU = mybir.AluOpType
AX = mybir.AxisListType

NT = 8
TT = 512
SEQ = 4096
D = 1024
FH = 2816
NFC = 22
INW = 3088
ALPHA = 2.0 ** 0.25
EPS = 1e-5
LAMBDA_INIT = 0.8 - 0.6 * 1.0
NR = 4
SLOT = 4096


class Res:
    __slots__ = ("name", "w", "rs", "excl")

    def __init__(self, name, excl=False):
        self.name = name
        self.w = None
        self.rs = []
        self.excl = excl


class Op:
    __slots__ = ("eng", "fn", "deps", "dma", "lane", "needs", "idx", "laneval", "waits", "phase")

    def __init__(self, eng, fn, dma, lane):
        self.phase = None
        self.eng = eng
        self.fn = fn
        self.dma = dma
        self.lane = lane
        self.deps = []
        self.needs = False
        self.idx = None
        self.laneval = None
        self.waits = []


ENGS = ("pe", "act", "dve", "pool", "sp")
NROT = 4


class Sched:
    def __init__(self):
        self.ops = {e: [] for e in ENGS}
        self.lane_tok = {}
        self.phase = ""

    def add(self, eng, fn, reads=(), writes=(), lane=None):
        dma = lane is not None
        op = Op(eng, fn, dma, lane)
        op.phase = self.phase
        reads = list(reads)
        writes = list(writes)
        if dma:
            if lane not in self.lane_tok:
                self.lane_tok[lane] = Res("lane_" + str(lane))
            writes.append(self.lane_tok[lane])
        raw = set()
        other = set()
        for r in reads:
            if r.w is not None:
                raw.add(r.w)
            if r.excl:
                for q in r.rs:
                    if q.eng != eng:
                        other.add(q)
        for w in writes:
            if w.w is not None:
                other.add(w.w)
            for q in w.rs:
                other.add(q)
        deps = set()
        for d in raw | other:
            if d is op:
                continue
            if (not d.dma) and (not dma) and d.eng == eng:
                if eng == "pe":
                    continue
                if d not in raw:
                    continue
            deps.add(d)
        op.deps = list(deps)
        for d in deps:
            d.needs = True
        for r in reads:
            if r.excl:
                r.rs = [op]
                continue
            if not dma:
                r.rs = [q for q in r.rs if q.dma or q.eng != eng]
            r.rs.append(op)
        for w in writes:
            w.w = op
            w.rs = []
        self.ops[eng].append(op)
        return op

    def finalize(self):
        lanecnt = {}
        for e in ENGS:
            c = 0
            for op in self.ops[e]:
                if op.dma:
                    lanecnt[op.lane] = lanecnt.get(op.lane, 0) + 1
                    op.laneval = 16 * lanecnt[op.lane]
                elif op.needs:
                    c += 1
                    op.idx = c
        self.lanecnt = lanecnt
        for e in ENGS:
            waited = {}
            for op in self.ops[e]:
                best = {}
                for d in op.deps:
                    if d.dma:
                        key = ("lane", d.lane)
                        val = d.laneval
                    else:
                        key = ("eng", d.eng)
                        val = d.idx
                    if waited.get(key, 0) >= val:
                        continue
                    if best.get(key, 0) < val:
                        best[key] = val
                for key, val in best.items():
                    waited[key] = val
                    op.waits.append((key, val))

    def emit(self, nc, stack, final_lanes=()):
        sems = {}
        for e in ENGS:
            for r in range(NROT):
                sems[("eng", e, r)] = stack.enter_context(nc.semaphore("s_%s%d" % (e, r)))
        for lane in self.lanecnt:
            sems[("lane", lane)] = stack.enter_context(nc.semaphore("l_%s" % str(lane)))
        block = stack.enter_context(nc.Block())
        names = {"pe": "tensor", "act": "scalar", "dve": "vector", "pool": "gpsimd", "sp": "sync"}

        def run(e, eng):
            for op in self.ops[e]:
                for key, val in op.waits:
                    if key[0] == "lane":
                        eng.wait_ge(sems[key], val)
                    else:
                        j = val - 1
                        eng.wait_ge(sems[("eng", key[1], j % NROT)], j // NROT + 1)
                inst = op.fn(eng)
                if op.dma:
                    inst.then_inc(sems[("lane", op.lane)], 16)
                elif op.idx is not None:
                    j = op.idx - 1
                    inst.then_inc(sems[("eng", e, j % NROT)], 1)
            if e == "sp":
                for lane in self.lanecnt:
                    eng.wait_ge(sems[("lane", lane)], 16 * self.lanecnt[lane])

        for e in ENGS:
            getattr(block, names[e])(lambda eng, e=e: run(e, eng))


def build_program(nt=NT, dbg=False, stop_after=None, unit_limit=None, skip_lastx=False, xmode=0):
    nc = bass.Bass("TRN2", target_bir_lowering=False, dynamic_dma_scratch_size=8192)
    S = Sched()

    def din(name, shape, dt=F32):
        return nc.dram_tensor(name, list(shape), dt, kind="ExternalInput").ap()

    x_d = din("x", [SEQ, D])
    mem_d = din("mem", [256, D])
    wsrc = {
        "wg1": din("ffn1_w_gate", [D, FH]), "wu1": din("ffn1_w_up", [D, FH]), "wd1": din("ffn1_w_down", [FH, D]),
        "win": din("w_in", [D, INW]), "wout": din("w_out", [D, D]), "wq": din("cross_wq", [D, D]),
        "wkv": din("cross_wkv", [D, 2 * D]), "wo": din("cross_wo", [D, D]),
        "wg2": din("ffn2_w_gate", [D, FH]), "wu2": din("ffn2_w_up", [D, FH]), "wd2": din("ffn2_w_down", [FH, D]),
    }
    lnp_d = din("lnp", [128, 64])
    lamv_d = din("lamv", [128, 256])
    dng_d = din("dng4", [128, 512])
    gng_d = din("gng4", [128, 512])
    w2a_d = din("w2aug", [32, 256])
    identf_d = din("identf", [128, 128])
    identb_d = din("identb", [128, 128], BF16)
    m1_d = din("m1", [128, 128])
    ind_d = din("ind", [128, 2])
    onesb_d = din("onesb", [128, 128], BF16)
    out_d = nc.dram_tensor("out", [SEQ, D], F32, kind="ExternalOutput").ap()
    wbf = {k: nc.dram_tensor("bf_" + k, list(v.shape), BF16, kind="Internal").ap() for k, v in wsrc.items()}
    if dbg:
        dbg_d = nc.dram_tensor("dbg", [128, 8, 512], F32, kind="ExternalOutput").ap()
        dbg2_d = nc.dram_tensor("dbg2", [128, 8, 512], BF16, kind="ExternalOutput").ap()

    with contextlib.ExitStack() as st:
        def sb(n, s, d):
            return st.enter_context(nc.sbuf_tensor(n, list(s), d))

        Kc = sb("Kc", [128, 4, SEQ], BF16)
        Vc = sb("Vc", [128, 32, 4, 129], BF16)
        ring = [sb("ring%d" % i, [128, SLOT], BF16) for i in range(NR)]
        RT = sb("RT", [128, 8, TT], F32)
        hT = sb("hT", [128, 8, TT], BF16)
        HT = sb("HT", [128, NFC, TT], BF16)
        tokbuf = sb("tokbuf", [128, 2048], F32)
        qdz = [sb("qdz%d" % i, [128, 4, 512], BF16) for i in range(2)]
        xst = [sb("xst%d" % i, [128, 1024], F32) for i in range(2)]
        sg = [sb("sg%d" % i, [128, 512], F32) for i in range(2)]
        PT = [sb("PT%d" % i, [128, 512], BF16) for i in range(4)]
        zsq = [sb("zsq%d" % i, [128, 512], BF16) for i in range(2)]
        lnt = sb("lnt", [128, 512], F32)
        Sst = sb("Sst", [64, 512], F32)
        Sbf = [sb("Sbf%d" % i, [64, 512], BF16) for i in range(2)]
        glr = sb("glr", [32, 512], F32)
        w2a = sb("w2a", [32, 256], F32)
        winglr = sb("winglr", [128, 8, 16], BF16)
        identf = sb("identf_s", [128, 128], F32)
        identb = sb("identb_s", [128, 128], BF16)
        m1 = sb("m1_s", [128, 128], F32)
        ind = sb("ind_s", [128, 2], F32)
        onesb = sb("onesb_s", [128, 128], BF16)
        lnp = sb("lnp_s", [128, 64], F32)
        lnpa = sb("lnpa_s", [128, 64], F32)
        lamv = sb("lamv_s", [128, 256], F32)
        lamt = sb("lamt", [128, 64], F32)
        lams = sb("lams", [128, 8], F32)
        gda4 = sb("gda4", [128, 512], F32)
        gng4 = sb("gng4_s", [128, 512], F32)
        kmT = sb("kmT", [128, 8, 256], BF16)
        Vm = sb("Vm", [128, 2, 4, 257], BF16)
        la = [sb("la%d" % i, [128, 256], F32) for i in range(2)]
        ed = [sb("ed%d" % i, [128, 256], F32) for i in range(2)]
        dec = sb("dec", [64, 4, 8], F32)
        o0 = sb("o0", [128, 4, 128], F32)
        od = sb("od", [128, 4, 128], F32)
        sm = sb("sm", [128, 32], F32)
        banks = [st.enter_context(nc.psum_tensor("pb%d" % i, [128, 512], F32)) for i in range(8)]
        bres = [Res("pb%d" % i, excl=True) for i in range(8)]
        free_banks = list(range(8))

        def take():
            return free_banks.pop(0)

        def free(b):
            free_banks.append(b)

        scr = HT[:].rearrange("p a b -> p (a b)")
        def scr_bf(g0, ng, shape_str=None, **kw):
            v = scr[:, g0 * 512:(g0 + ng) * 512]
            return v.rearrange(shape_str, **kw) if shape_str else v

        def scr_f32(g0, ng, shape_str=None, **kw):
            v = scr[:, g0 * 512:(g0 + ng) * 512].bitcast(F32)
            return v.rearrange(shape_str, **kw) if shape_str else v

        qdT = scr_bf(0, 4, "p (a b) -> p a b", b=512)
        qgT = scr_bf(4, 4, "p (a b) -> p a b", b=512)
        kg_sb = scr_f32(8, 4, "p (a b) -> p a b", b=256)
        vg_sb = scr_bf(12, 4, "p (a b) -> p a b", b=512)
        gate = scr_bf(16, 4, "p (a b) -> p a b", b=512)
        kend = scr_bf(20, 2, "p (a b) -> p a b", b=256)
        qcT = scr_bf(0, 8, "p (a b) -> p a b", b=512)
        PTc = scr_bf(8, 8, "p (a b) -> p a b", b=512)
        tok = [tokbuf[:, i * 1024:(i + 1) * 1024] for i in range(2)]
        ocat = tokbuf[:].bitcast(BF16).rearrange("p (a b) -> p a b", b=1024)

        R = {}
        def res(name):
            if name not in R:
                R[name] = Res(name)
            return R[name]
        rRT = [res("RT%d" % k) for k in range(8)]
        rhT = [res("hT%d" % k) for k in range(8)]
        rG = [res("G%d" % g) for g in range(NFC)]
        rring = [res("ring%d" % i) for i in range(NR)]
        rK = [[res("K%d_%d" % (h, t)) for t in range(NT)] for h in range(4)]
        rV = [res("V%d" % k) for k in range(32)]
        rocat = [res("ocat%d" % s) for s in range(4)]
        rtok = [[rocat[0], rocat[1]], [rocat[2], rocat[3]]]
        rsg = [res("sg%d" % i) for i in range(2)]
        rqd = [[res("qd%d_%d" % (c, h)) for h in range(4)] for c in range(2)]
        rxst = [res("xst%d" % i) for i in range(2)]
        rPT = [res("PT%d" % i) for i in range(4)]
        rzsq = [res("zsq%d" % i) for i in range(2)]
        rSbf = [res("Sbf%d" % i) for i in range(2)]
        rla = [res("la%d" % i) for i in range(2)]
        red = [res("ed%d" % i) for i in range(2)]

        def gr(g0, ng):
            return rG[g0:g0 + ng]

        def MM(out, lhsT, rhs, start, stop, r, w, skip=False):
            if skip:
                return S.add("pe", lambda e: e.matmul(out, lhsT=lhsT, rhs=rhs, start=start, stop=stop, skip_group_check=True), r, w)
            return S.add("pe", lambda e: e.matmul(out, lhsT=lhsT, rhs=rhs, start=start, stop=stop), r, w)

        def TR(out, in_, ident, r, w):
            return S.add("pe", lambda e: e.transpose(out=out, in_=in_, identity=ident), r, w)

        def A(out, in_, func, r, w, scale=None, bias=None):
            kw = {}
            if scale is not None:
                kw["scale"] = scale
            if bias is not None:
                kw["bias"] = bias
            return S.add("act", lambda e: e.activation(out=out, in_=in_, func=func, **kw), r, w)

        def TT_(eng, out, in0, in1, op, r, w):
            return S.add(eng, lambda e: e.tensor_tensor(out=out, in0=in0, in1=in1, op=op), r, w)

        def TS(eng, out, in0, s1, s2, op0, op1, r, w):
            if op1 is None:
                return S.add(eng, lambda e: e.tensor_scalar(out=out, in0=in0, scalar1=s1, scalar2=None, op0=op0), r, w)
            return S.add(eng, lambda e: e.tensor_scalar(out=out, in0=in0, scalar1=s1, scalar2=s2, op0=op0, op1=op1), r, w)

        def STT(out, in0, scalar, in1, op0, op1, r, w):
            return S.add("dve", lambda e: e.scalar_tensor_tensor(out=out, in0=in0, scalar=scalar, in1=in1, op0=op0, op1=op1), r, w)

        def CP(eng, out, in_, r, w):
            return S.add(eng, lambda e: e.tensor_copy(out=out, in_=in_), r, w)

        def RECIP(out, in_, r, w):
            return S.add("dve", lambda e: e.reciprocal(out=out, in_=in_), r, w)

        def RED(out, in_, r, w):
            return S.add("dve", lambda e: e.tensor_reduce(out=out, in_=in_, axis=AX.X, op=ALU.add), r, w)

        def MEMSET(eng, ap, val, r, w):
            return S.add(eng, lambda e: e.memset(ap, val), r, w)

        def DMA(eng, out, in_, r, w, lane):
            return S.add(eng, lambda e: e.dma_start(out=out, in_=in_), r, w, lane=lane)

        cnt = {"lane": 0, "cast": 0}

        def const_load(dst, src, r):
            cnt["lane"] += 1
            DMA("act", dst, src, [], [r], "k%d" % cnt["lane"])

        const_load(identf[:], identf_d, res("identf"))
        const_load(identb[:], identb_d, res("identb"))
        const_load(onesb[:], onesb_d, res("onesb"))
        const_load(lnp[:], lnp_d, res("lnp"))
        const_load(lamv[:], lamv_d, res("lamv"))
        const_load(gda4[:], dng_d, res("gda4"))
        const_load(gng4[:], gng_d, res("gng4"))
        const_load(w2a[:], w2a_d, res("w2a"))
        const_load(m1[:], m1_d, res("m1"))
        const_load(ind[:], ind_d, res("ind"))

        rW = {}

        def cast(name, c0, c1):
            key = (name, c0)
            rW[key] = Res("w_%s_%d" % key)
            lane = "cast%d" % (cnt["cast"] % 2)
            cnt["cast"] += 1
            DMA("pool", wbf[name][:, c0:c1], wsrc[name][:, c0:c1], [], [rW[key]], lane)

        def cast_cols(name, ncols, step=512):
            for c0 in range(0, ncols, step):
                cast(name, c0, min(ncols, c0 + step))

        def cast_ffn(i):
            for c0 in range(0, FH, 512):
                cast("wg%d" % i, c0, min(FH, c0 + 512))
                cast("wu%d" % i, c0, min(FH, c0 + 512))
            cast_cols("wd%d" % i, D)

        cast_ffn(1)
        cast_cols("win", 3072)
        cast("win", 3072, 3088)
        cast_cols("wout", D)
        cast_cols("wkv", 2 * D)
        cast_cols("wq", D)
        cast_cols("wo", D)
        cast_ffn(2)

        units = []

        def u_k8(name, c0, ncols):
            units.append((name, c0, ncols, 0, 8))

        def u_ffn(i):
            for c0 in range(0, FH, 512):
                u_k8("wg%d" % i, c0, min(512, FH - c0))
                u_k8("wu%d" % i, c0, min(512, FH - c0))
            for grp in range(2):
                for part in range(3):
                    nf = 8 if part < 2 else 6
                    units.append(("wd%d" % i, grp * 512, 512, part * 8, nf))

        for t in range(nt):
            u_ffn(1)
            for c0 in range(0, 3072, 512):
                u_k8("win", c0, 512)
            u_k8("wout", 0, 512)
            u_k8("wout", 512, 512)
            if t == 0:
                for c0 in range(0, 2 * D, 512):
                    u_k8("wkv", c0, 512)
            for nm in ("wq", "wo"):
                u_k8(nm, 0, 512)
                u_k8(nm, 512, 512)
            u_ffn(2)

        ws = {"issued": 0, "acq": 0, "rel": 0}

        def ws_issue():
            i = ws["issued"]
            if i >= len(units) or (unit_limit is not None and i >= unit_limit):
                return
            name, c0, ncols, f0, nf = units[i]
            slot = i % NR
            src = wbf[name].rearrange("(kc p) n -> p kc n", p=128)[:, f0:f0 + nf, c0:c0 + ncols]
            dst = ring[slot][:, 0:nf * ncols].rearrange("p (a b) -> p a b", b=ncols)
            rr = [rW[(name, c)] for c in range((c0 // 512) * 512, c0 + ncols, 512)]
            DMA("sp", dst, src, rr, [rring[slot]], "ring%d" % slot)
            ws["issued"] += 1

        def acquire():
            i = ws["acq"]
            assert i < ws["issued"], "ring underflow"
            name, c0, ncols, f0, nf = units[i]
            slot = i % NR
            ws["acq"] += 1
            view = ring[slot][:, 0:nf * ncols].rearrange("p (a b) -> p a b", b=ncols)
            return view, rring[slot], units[i]

        def release():
            ws["rel"] += 1
            ws_issue()

        for _ in range(NR):
            ws_issue()


        rlnpa = res("lnpa")
        TS("dve", lnpa[:, 0:48], lnp[:, 0:48], ALPHA, None, ALU.mult, None, [res("lnp")], [rlnpa])
        CP("dve", lnpa[:, 48:64], lnp[:, 48:64], [res("lnp")], [rlnpa])
        rlam = res("lam")
        TT_("dve", lamt[:, 0:64], lamv[:, 0:64], lamv[:, 64:128], ALU.mult, [res("lamv")], [res("lamt")])
        RED(lams[:, 0:1], lamt[:, 0:64], [res("lamt")], [res("lams0")])
        TT_("dve", lamt[:, 0:64], lamv[:, 128:192], lamv[:, 192:256], ALU.mult, [res("lamv"), res("lams0")], [res("lamt")])
        RED(lams[:, 1:2], lamt[:, 0:64], [res("lamt")], [res("lams1")])
        A(lams[:, 2:4], lams[:, 0:2], AF.Exp, [res("lams0"), res("lams1")], [res("lams2")])
        TT_("dve", lams[:, 4:5], lams[:, 3:4], lams[:, 2:3], ALU.subtract, [res("lams2")], [res("lams4")])
        TS("dve", lams[:, 5:6], lams[:, 4:5], -LAMBDA_INIT, None, ALU.add, None, [res("lams4")], [rlam])
        neglam = lams[:, 5:6]
        TS("dve", gda4[:], gda4[:], 1.0 - LAMBDA_INIT, None, ALU.mult, None, [res("gda4")], [res("gda4")])
        MEMSET("pool", Vc[:, :, :, 128:129], 1.0, [], rV)
        MEMSET("pool", Vm[:, :, :, 256:257], 1.0, [], [res("Vm")])
        MEMSET("pool", glr[:], 1.0, [], [res("glr")])
        MEMSET("pool", qdz[0][64:128, :, :], 0.0, [], rqd[0])
        MEMSET("pool", qdz[1][0:64, :, :], 0.0, [], rqd[1])
        rSst = [res("Sst%d" % h) for h in range(4)]
        MEMSET("pool", Sst[:], 0.0, [], rSst)

        def transpose_in(src_tok, r_src, sub, scale_rt, xm=0):
            cs = slice(sub * 128, (sub + 1) * 128)
            for g in range(2):
                b = take()
                for j in range(4):
                    kc = g * 4 + j
                    TR(banks[b][:, j * 128:(j + 1) * 128], src_tok[:, kc * 128:(kc + 1) * 128], identf[:],
                       r_src + [res("identf")], [bres[b]])
                bv = banks[b][:].rearrange("p (a c) -> p a c", c=128)
                if scale_rt is not None:
                    A(RT[:, g * 4:(g + 1) * 4, cs], bv, AF.Copy, [bres[b]], rRT[g * 4:(g + 1) * 4], scale=scale_rt)
                    TS("dve", hT[:, g * 4:(g + 1) * 4, cs], RT[:, g * 4:(g + 1) * 4, cs], 1.0 / scale_rt, None, ALU.mult, None,
                       rRT[g * 4:(g + 1) * 4], rhT[g * 4:(g + 1) * 4])
                else:
                    CP("dve", hT[:, g * 4:(g + 1) * 4, cs], bv, [bres[b]], rhT[g * 4:(g + 1) * 4])
                free(b)

        def transpose_ocat():
            for sub in range(4):
                cs = slice(sub * 128, (sub + 1) * 128)
                for g in range(2):
                    b = take()
                    bvb = banks[b][:].bitcast(BF16)
                    for j in range(4):
                        kc = g * 4 + j
                        TR(bvb[:, j * 128:(j + 1) * 128], ocat[:, sub, kc * 128:(kc + 1) * 128], identb[:],
                           [rocat[sub], res("identb")], [bres[b]])
                    bv = bvb[:, 0:512].rearrange("p (a c) -> p a c", c=128)
                    if (sub + g) % 2 == 0:
                        CP("dve", hT[:, g * 4:(g + 1) * 4, cs], bv, [bres[b]], rhT[g * 4:(g + 1) * 4])
                    else:
                        A(hT[:, g * 4:(g + 1) * 4, cs], bv, AF.Copy, [bres[b]], rhT[g * 4:(g + 1) * 4])
                    free(b)

        st_ = {"zi": 0}

        def evac_residual(b, kc, factor):
            STT(RT[:, kc, :], banks[b][:], factor, RT[:, kc, :], ALU.mult, ALU.add, [bres[b], rRT[kc]], [rRT[kc]])
            A(hT[:, kc, :], RT[:, kc, :], AF.Copy, [rRT[kc]], [rhT[kc]])
            zi = st_["zi"] % 2
            st_["zi"] += 1
            A(zsq[zi][:], RT[:, kc, :], AF.Square, [rRT[kc]], [rzsq[zi]])
            return zi

        def stats_mm(kc, zi, bM, bQ):
            MM(banks[bM][:], onesb[:], hT[:, kc, :], kc == 0, kc == 7, [res("onesb"), rhT[kc]], [bres[bM]])
            MM(banks[bQ][:], onesb[:], zsq[zi][:], kc == 0, kc == 7, [res("onesb"), rzsq[zi]], [bres[bQ]])

        def layer_norm(li, bM, bQ, final=False):
            rl = res("lnt")
            A(lnt[:], banks[bM][:], AF.Square, [bres[bM]], [rl])
            TT_("dve", lnt[:], banks[bQ][:], lnt[:], ALU.subtract, [bres[bQ], rl], [rl])
            A(lnt[:], lnt[:], AF.Ln, [rl], [rl], bias=EPS)
            A(banks[bQ][:], lnt[:], AF.Exp, [rl], [bres[bQ]], scale=-0.5)
            gi, bi = 2 * li, 2 * li + 1
            for kc in range(8):
                TT_("dve", RT[:, kc, :], RT[:, kc, :], banks[bM][:], ALU.subtract, [rRT[kc], bres[bM]], [rRT[kc]])
                TT_("dve", RT[:, kc, :], RT[:, kc, :], banks[bQ][:], ALU.mult, [rRT[kc], bres[bQ]], [rRT[kc]])
                if not final:
                    A(hT[:, kc, :], RT[:, kc, :], AF.Identity, [rRT[kc], res("lnp")], [rhT[kc]],
                      scale=lnp[:, gi * 8 + kc:gi * 8 + kc + 1], bias=lnp[:, bi * 8 + kc:bi * 8 + kc + 1])
                A(RT[:, kc, :], RT[:, kc, :], AF.Identity, [rRT[kc], rlnpa], [rRT[kc]],
                  scale=lnpa[:, gi * 8 + kc:gi * 8 + kc + 1], bias=lnpa[:, bi * 8 + kc:bi * 8 + kc + 1])
            free(bQ)
            free(bM)

        def proj_residual_ln(n_k, rhs_fn, r_rhs_fn, factor, li, final=False, wd=False):
            grp_banks = []
            for grp in range(2):
                bs = [take() for _ in range(4)]
                grp_banks.append(bs)
                nparts = 3 if wd else 1
                for part in range(nparts):
                    view, rs, u = acquire()
                    nf = u[4]
                    for dmc in range(4):
                        for fl in range(nf):
                            f = part * 8 + fl
                            MM(banks[bs[dmc]][:], view[:, fl, dmc * 128:(dmc + 1) * 128], rhs_fn(f), f == 0, f == n_k - 1,
                               [rs] + r_rhs_fn(f), [bres[bs[dmc]]])
                    release()
            bM = bQ = None
            pend = []
            for grp in range(2):
                for dmc in range(4):
                    kc = grp * 4 + dmc
                    zi = evac_residual(grp_banks[grp][dmc], kc, factor)
                    free(grp_banks[grp][dmc])
                    pend.append((kc, zi))
                    if kc == 0:
                        continue
                    if bM is None:
                        bM = take()
                        bQ = take()
                    for (k2, z2) in pend:
                        stats_mm(k2, z2, bM, bQ)
                    pend = []
            layer_norm(li, bM, bQ, final)

        def ffn(li):
            si = 0
            for fg in range(6):
                vg_, rg_, ug_ = acquire()
                vu_, ru_, uu_ = acquire()
                ncols = ug_[2]
                for j in range(ncols // 128):
                    fc = fg * 4 + j
                    bG = take()
                    bU = take()
                    for kc in range(8):
                        MM(banks[bG][:], vg_[:, kc, j * 128:(j + 1) * 128], hT[:, kc, :], kc == 0, kc == 7, [rg_, rhT[kc]], [bres[bG]])
                    for kc in range(8):
                        MM(banks[bU][:], vu_[:, kc, j * 128:(j + 1) * 128], hT[:, kc, :], kc == 0, kc == 7, [ru_, rhT[kc]], [bres[bU]])
                    i = si % 2
                    si += 1
                    A(sg[i][:], banks[bG][:], AF.Silu, [bres[bG]], [rsg[i]])
                    TT_("dve", HT[:, fc, :], sg[i][:], banks[bU][:], ALU.mult, [rsg[i], bres[bU]], [rG[fc]])
                    free(bG)
                    free(bU)
                release()
                release()
            proj_residual_ln(NFC, lambda f: HT[:, f, :], lambda f: [rG[f]], 0.5, li, final=(li == 3), wd=True)

        def proj_fm(n_chunks, lhs_cols, evac):
            view, rs, u = acquire()
            for c in range(n_chunks):
                b = take()
                for kc in range(8):
                    MM(banks[b][0:lhs_cols, :], view[:, kc, c * lhs_cols:(c + 1) * lhs_cols], hT[:, kc, :], kc == 0, kc == 7,
                       [rs, rhT[kc]], [bres[b]])
                evac(c, b)
                free(b)
            return view, rs

        def proj_tm(view, rs, c0, ncols, evac):
            for sub in range(4):
                b = take()
                for kc in range(8):
                    MM(banks[b][:, 0:ncols], hT[:, kc, sub * 128:(sub + 1) * 128], view[:, kc, c0:c0 + ncols], kc == 0, kc == 7,
                       [rs, rhT[kc]], [bres[b]])
                evac(sub, b)
                free(b)

        def rms_rs(ss_ap, r_in, n):
            A(sm[:, 16:20], ss_ap, AF.Ln, r_in, [res("sm_ln")], scale=1.0 / n, bias=EPS)
            A(sm[:, 20:24], sm[:, 16:20], AF.Exp, [res("sm_ln")], [res("sm_rs")], scale=-0.5)

        memT = scr_bf(16, 4, "p (a b) -> p a b", b=256)

        def mem_kv():
            for ms in range(2):
                DMA("sp", tok[ms], mem_d[ms * 128:(ms + 1) * 128, :], [], rtok[ms], "tok%d" % ms)
            for ms in range(2):
                cs = slice(ms * 128, (ms + 1) * 128)
                for g in range(2):
                    b = take()
                    for j in range(4):
                        kc = g * 4 + j
                        TR(banks[b][:, j * 128:(j + 1) * 128], tok[ms][:, kc * 128:(kc + 1) * 128], identf[:],
                           rtok[ms] + [res("identf")], [bres[b]])
                    CP("dve", memT[:, g * 4:(g + 1) * 4, cs], banks[b][:].rearrange("p (a c) -> p a c", c=128), [bres[b]], gr(16, 4))
                    free(b)
            for uu in range(2):
                view, rs, u = acquire()
                for c in range(4):
                    ch = uu * 4 + c
                    b = take()
                    for kc in range(8):
                        MM(banks[b][:, 0:256], view[:, kc, c * 128:(c + 1) * 128], memT[:, kc, :], kc == 0, kc == 7, [rs] + gr(16, 4), [bres[b]])
                    A(kmT[:, ch, :], banks[b][:, 0:256], AF.Copy, [bres[b]], [res("kmT")])
                    free(b)
                release()
            for uu in range(2):
                view, rs, u = acquire()
                for ms in range(2):
                    b = take()
                    for kc in range(8):
                        MM(banks[b][:], memT[:, kc, ms * 128:(ms + 1) * 128], view[:, kc, :], kc == 0, kc == 7, [rs] + gr(16, 4), [bres[b]])
                    CP("dve", Vm[:, ms, uu * 2:(uu + 1) * 2, 0:256], banks[b][:].rearrange("p (a c) -> p a c", c=256), [bres[b]], [res("Vm")])
                    free(b)
                release()

        def x_dma(t, sub):
            i = sub % 2
            r0 = t * TT + sub * 128
            DMA("sp", xst[i][:], x_d[r0:r0 + 128, :], [], [rxst[i]], "xst%d" % i)

        def load_x(t):
            if t == 0:
                x_dma(0, 0)
                x_dma(0, 1)
            for sub in range(4):
                i = sub % 2
                transpose_in(xst[i][:], [rxst[i]], sub, ALPHA)
                if sub + 2 < 4:
                    x_dma(t, sub + 2)
                elif t + 1 < nt:
                    x_dma(t + 1, sub - 2)

        def w_in_phase(t):
            def ev_q(h, b):
                A(qdz[0][0:64, h, :], banks[b][0:64, :], AF.Copy, [bres[b]], [rqd[0][h]])
                A(qdz[1][64:128, h, :], banks[b][64:128, :], AF.Copy, [bres[b]], [rqd[1][h]])
            proj_fm(4, 128, ev_q)
            release()
            proj_fm(4, 128, lambda h, b: CP("dve", Kc[:, h, t * TT:(t + 1) * TT], banks[b][:], [bres[b]], [rK[h][t]]))
            release()
            view, rs, u = acquire()
            proj_tm(view, rs, 0, 512, lambda sub, b: A(Vc[:, 4 * t + sub, :, 0:128], banks[b][:].rearrange("p (a c) -> p a c", c=128),
                                                       AF.Copy, [bres[b]], [rV[4 * t + sub]]))
            release()
            view, rs = proj_fm(4, 64, lambda h, b: A(qgT[0:64, h, :], banks[b][0:64, :], AF.Copy, [bres[b]], [rG[4 + h]], scale=0.125))
            proj_tm(view, rs, 256, 256, lambda sub, b: CP("dve", kg_sb[:, sub, :], banks[b][:, 0:256], [bres[b]], [rG[8 + sub]]))
            b = take()
            for kc in range(8):
                MM(banks[b][0:16, :], winglr[:, kc, :], hT[:, kc, :], kc == 0, kc == 7, [res("winglr"), rhT[kc]], [bres[b]])
            CP("dve", glr[0:16, :], banks[b][0:16, :], [bres[b]], [res("glr")])
            free(b)
            release()
            view, rs, u = acquire()
            proj_tm(view, rs, 0, 512, lambda sub, b: CP("dve", vg_sb[:, sub, :], banks[b][:], [bres[b]], [rG[12 + sub]]))
            release()
            view, rs, u = acquire()

            def ev_r(sub, b):
                i = sub % 2
                A(sg[i][:], banks[b][:], AF.Silu, [bres[b]], [rsg[i]])
                TT_("pool", gate[:, sub, :], sg[i][:], gng4[:], ALU.mult, [rsg[i], res("gng4")], [rG[16 + sub]])
            proj_tm(view, rs, 0, 512, ev_r)
            release()

        pti = [0]

        def da_phase(t):
            nkt = 4 * t + 4
            gen = gla_gen(t)
            state = {"done": False, "n": 0}
            per = max(1, (8 * nkt) // 30)

            def pump():
                state["n"] += 1
                if state["done"] or state["n"] % per != 0:
                    return
                try:
                    next(gen)
                except StopIteration:
                    state["done"] = True

            def qk(h, c, kt):
                j = kt - 4 * t
                q0 = 128 * j if j >= 0 else 0
                b = take()
                MM(banks[b][:, q0:512], Kc[:, h, kt * 128:(kt + 1) * 128], qdz[c][:, h, q0:512], True, True,
                   [rK[h][kt // 4], rqd[c][h]], [bres[b]])
                i = pti[0] % 4
                pti[0] += 1
                A(PT[i][:, q0:512], banks[b][:, q0:512], AF.Exp, [bres[b]], [rPT[i]], scale=0.125)
                if j >= 0:
                    MEMSET("pool", PT[i][64:128, q0:q0 + 64], 0.0, [], [rPT[i]])
                free(b)
                return i, j

            def acc_ap(accb, s):
                return banks[accb[s // 2]][:, (s % 2) * 256:(s % 2) * 256 + 129]

            def pv(h, kt, i, j, accb):
                for s in range(max(j, 0), 4):
                    MM(acc_ap(accb, s), PT[i][:, s * 128:(s + 1) * 128], Vc[:, kt, h, :], (kt == 0 and s % 2 == 0), kt == 4 * t + s,
                       [rPT[i], rV[kt]], [bres[accb[s // 2]]], skip=True)

            def epilogue(h, c, accb):
                rb = [bres[accb[0]], bres[accb[1]]]
                if c == 0:
                    for s in range(4):
                        RECIP(sm[:, s:s + 1], acc_ap(accb, s)[:, 128:129], [rb[s // 2]], [res("sm_r0_%d" % s)])
                        TS("dve", o0[:, s, :], acc_ap(accb, s)[:, 0:128], sm[:, s:s + 1], None, ALU.mult, None,
                           [rb[s // 2], res("sm_r0_%d" % s)], [res("o0")])
                else:
                    for s in range(4):
                        RECIP(sm[:, 4 + s:5 + s], acc_ap(accb, s)[:, 128:129], [rb[s // 2]], [res("sm_r1")])
                    TS("dve", sm[:, 8:12], sm[:, 4:8], neglam, None, ALU.mult, None, [res("sm_r1"), rlam], [res("sm_r1l")])
                    for s in range(4):
                        STT(od[:, s, :], acc_ap(accb, s)[:, 0:128], sm[:, 8 + s:9 + s], o0[:, s, :], ALU.mult, ALU.add,
                            [rb[s // 2], res("sm_r1l"), res("o0")], [res("od")])
                    A(sg[0][:], od[:].rearrange("p a b -> p (a b)"), AF.Square, [res("od")], [rsg[0]])
                    RED(sm[:, 12:16], sg[0][:].rearrange("p (a b) -> p a b", b=128), [rsg[0]], [res("sm_ss")])
                    rms_rs(sm[:, 12:16], [res("sm_ss")], 128.0)
                    TT_("dve", od[:], od[:], sm[:, 20:24].unsqueeze(2).to_broadcast([128, 4, 128]), ALU.mult,
                        [res("od"), res("sm_rs")], [res("od")])
                    TT_("pool", ocat[:, :, h * 128:(h + 1) * 128], od[:], gda4[:].rearrange("p (a b) -> p a b", b=128), ALU.mult,
                        [res("od"), res("gda4")], rocat)
                free(accb[0])
                free(accb[1])

            steps = [(h, c, kt) for h in range(4) for c in range(2) for kt in range(nkt)]
            DEPTH = 2
            pend = [qk(*steps[k]) for k in range(min(DEPTH, len(steps)))]
            accb = None
            for n, (h, c, kt) in enumerate(steps):
                cur = pend.pop(0)
                if n + DEPTH < len(steps):
                    pend.append(qk(*steps[n + DEPTH]))
                if kt == 0:
                    accb = [take(), take()]
                pv(h, kt, cur[0], cur[1], accb)
                if kt == nkt - 1:
                    epilogue(h, c, accb)
                pump()
            if not state["done"]:
                for _ in gen:
                    pass

        sbi = [0]

        def gla_sub_gen(t, sub):
            i = sub % 2
            cs = slice(sub * 128, (sub + 1) * 128)
            bz = take()
            MM(banks[bz][:, 0:256], glr[0:32, cs], w2a[0:32, :], True, True, [res("glr"), res("w2a")], [bres[bz]])
            A(la[i][:], banks[bz][:, 0:256], AF.Exp, [bres[bz]], [rla[i]], scale=-1.0)
            A(la[i][:], la[i][:], AF.Ln, [rla[i]], [rla[i]], bias=1.0)
            free(bz)
            yield
            bD = take()
            MM(banks[bD][:, 0:256], m1[:], la[i][:], True, True, [res("m1"), rla[i]], [bres[bD]])
            for h in range(4):
                MM(banks[bD][0:64, 256 + 2 * h:258 + 2 * h], la[i][:, h * 64:(h + 1) * 64], ind[:], True, True,
                   [res("ind"), rla[i]], [bres[bD]], skip=True)
            A(ed[i][:], banks[bD][:, 0:256], AF.Exp, [bres[bD]], [red[i]])
            A(dec[:, sub, :], banks[bD][0:64, 256:264], AF.Exp, [bres[bD]], [res("dec")])
            TT_("dve", kend[:, sub, :], kg_sb[:, sub, :], ed[i][:], ALU.mult, [rG[8 + sub], red[i]], [rG[20 + sub // 2]])
            free(bD)
            yield
            bO = take()
            for c in range(2):
                ps = slice(c * 64, (c + 1) * 64)
                bS = take()
                for h in range(4):
                    MM(banks[bS][0:64, h * 128:(h + 1) * 128], kend[ps, sub, h * 64:(h + 1) * 64], vg_sb[ps, sub, h * 128:(h + 1) * 128],
                       True, True, [rG[20 + sub // 2], rG[12 + sub]], [bres[bS]], skip=True)
                for h in range(4):
                    STT(Sst[:, h * 128:(h + 1) * 128], Sst[:, h * 128:(h + 1) * 128], dec[:, sub, 2 * h + c:2 * h + c + 1],
                        banks[bS][0:64, h * 128:(h + 1) * 128], ALU.mult, ALU.add, [rSst[h], res("dec"), bres[bS]], [rSst[h]])
                k = sbi[0] % 2
                sbi[0] += 1
                CP("pool", Sbf[k][:], Sst[:], rSst, [rSbf[k]])
                free(bS)
                yield
                for h in range(4):
                    MM(banks[bO][ps, h * 128:(h + 1) * 128], qgT[0:64, h, sub * 128 + c * 64:sub * 128 + (c + 1) * 64],
                       Sbf[k][:, h * 128:(h + 1) * 128], True, True, [rG[4 + h], rSbf[k]], [bres[bO]], skip=True)
                yield
            A(sg[1][:], banks[bO][:], AF.Square, [bres[bO]], [rsg[1]])
            RED(sm[:, 12:16], sg[1][:].rearrange("p (a b) -> p a b", b=128), [rsg[1]], [res("sm_ss")])
            rms_rs(sm[:, 12:16], [res("sm_ss")], 128.0)
            TT_("dve", sg[1][:].rearrange("p (a b) -> p a b", b=128), banks[bO][:].rearrange("p (a b) -> p a b", b=128),
                sm[:, 20:24].unsqueeze(2).to_broadcast([128, 4, 128]), ALU.mult, [bres[bO], res("sm_rs"), rsg[1]], [rsg[1]])
            TT_("pool", ocat[:, sub, 512:1024], sg[1][:], gate[:, sub, :], ALU.mult, [rsg[1], rG[16 + sub]], [rocat[sub]])
            free(bO)

        def gla_gen(t):
            for sub in range(4):
                yield from gla_sub_gen(t, sub)

        def gla_phase(t):
            for _ in gla_gen(t):
                pass

        rci = [0]

        def cross_phase():
            for uu in range(2):
                proj_fm(4, 128, lambda c, b, uu=uu: A(qcT[:, uu * 4 + c, :], banks[b][:], AF.Copy, [bres[b]], [rG[uu * 4 + c]]))
                release()
            def scores(hd):
                for ms in range(2):
                    b = take()
                    for j in range(2):
                        MM(banks[b][:], kmT[:, 2 * hd + j, ms * 128:(ms + 1) * 128], qcT[:, 2 * hd + j, :], j == 0, j == 1,
                           [res("kmT"), rG[2 * hd + j]], [bres[b]])
                    A(PTc[:, hd * 2 + ms, :], banks[b][:], AF.Exp, [bres[b]], [rG[8 + hd * 2 + ms]], scale=1.0 / 16.0)
                    free(b)

            def pvx(hd):
                for sub in range(4):
                    b = take()
                    for ms in range(2):
                        MM(banks[b][:, 0:257], PTc[:, hd * 2 + ms, sub * 128:(sub + 1) * 128], Vm[:, ms, hd, :], ms == 0, ms == 1,
                           [rG[8 + hd * 2 + ms], res("Vm")], [bres[b]])
                    k = 24 + (rci[0] % 8)
                    rci[0] += 1
                    RECIP(sm[:, k:k + 1], banks[b][:, 256:257], [bres[b]], [res("sm_rc%d" % k)])
                    A(ocat[:, sub, hd * 256:(hd + 1) * 256], banks[b][:, 0:256], AF.Identity, [bres[b], res("sm_rc%d" % k)], [rocat[sub]],
                      scale=sm[:, k:k + 1])
                    free(b)

            scores(0)
            scores(1)
            for hd in range(4):
                pvx(hd)
                if hd + 2 < 4:
                    scores(hd + 2)

        def store_out(t):
            for sub in range(4):
                i = sub % 2
                for g in range(2):
                    b = take()
                    for j in range(4):
                        kc = g * 4 + j
                        TR(banks[b][:, j * 128:(j + 1) * 128], RT[:, kc, sub * 128:(sub + 1) * 128], identf[:], [rRT[kc], res("identf")], [bres[b]])
                    if g == 0:
                        A(tok[i][:, g * 512:(g + 1) * 512], banks[b][:], AF.Copy, [bres[b]], rtok[i])
                    else:
                        CP("dve", tok[i][:, g * 512:(g + 1) * 512], banks[b][:], [bres[b]], rtok[i])
                    free(b)
                r0 = t * TT + sub * 128
                DMA("sp", out_d[r0:r0 + 128, :], tok[i], rtok[i], [], "tok%d" % i)

        for t in range(nt):
            S.phase = "t%d:x" % t
            if not (skip_lastx and t == nt - 1):
                load_x(t)
            if stop_after == "x" and t == nt - 1:
                break
            S.phase = "t%d:ffn1" % t
            ffn(0)
            if stop_after == "ln1" and t == nt - 1:
                break
            S.phase = "t%d:win" % t
            if t == 0:
                DMA("sp", winglr[:], wbf["win"].rearrange("(kc p) n -> p kc n", p=128)[:, :, 3072:3088],
                    [rW[("win", 3072)]], [res("winglr")], "k_winglr")
            w_in_phase(t)
            if stop_after == "win" and t == nt - 1:
                break
            S.phase = "t%d:da" % t
            da_phase(t)
            if stop_after == "da" and t == nt - 1:
                break
            if stop_after == "gla" and t == nt - 1:
                break
            S.phase = "t%d:wout" % t
            transpose_ocat()
            proj_residual_ln(8, lambda f: hT[:, f, :], lambda f: [rhT[f]], 1.0, 1)
            if stop_after == "ln2" and t == nt - 1:
                break
            if t == 0:
                S.phase = "memkv"
                mem_kv()
            S.phase = "t%d:cross" % t
            cross_phase()
            S.phase = "t%d:wo" % t
            transpose_ocat()
            proj_residual_ln(8, lambda f: hT[:, f, :], lambda f: [rhT[f]], 1.0, 2)
            if stop_after == "ln3" and t == nt - 1:
                break
            S.phase = "t%d:ffn2" % t
            ffn(3)
            S.phase = "t%d:out" % t
            store_out(t)
        if dbg:
            DMA("sp", dbg_d, RT[:], rRT, [], "dbg")
            DMA("sp", dbg2_d, hT[:], rhT, [], "dbg2")
        if stop_after is None:
            assert ws["acq"] == len(units) == ws["rel"], (ws, len(units))
        S.finalize()
        S.emit(nc, st, final_lanes=["tok0", "tok1", "dbg", "dbg2"])
    return nc


def make_in_maps(inputs, cores):
    f = lambda a: np.ascontiguousarray(np.asarray(a, dtype=np.float32))
    g = {k: f(v) for k, v in inputs.items()}
    lnp = np.zeros((128, 64), np.float32)
    for i, nm in enumerate(["ln1_g", "ln1_b", "ln2_g", "ln2_b", "ln3_g", "ln3_b", "ln4_g", "ln4_b"]):
        lnp[:, i * 8:(i + 1) * 8] = g[nm][0].reshape(8, 128).T
    lamv = np.concatenate([g["da_lambda_q1"][0], g["da_lambda_k1"][0], g["da_lambda_q2"][0], g["da_lambda_k2"][0]])
    lamv = np.ascontiguousarray(np.broadcast_to(lamv[None, :], (128, 256)))
    dng4 = np.ascontiguousarray(np.broadcast_to(np.tile(g["da_norm_g"][0], 4)[None, :], (128, 512)))
    gng4 = np.ascontiguousarray(np.broadcast_to(np.tile(g["gla_norm_g"][0], 4)[None, :], (128, 512)))
    w2aug = np.zeros((32, 256), np.float32)
    w2aug[0:16] = g["gla_w_gate2"][0]
    w2aug[16] = g["gla_b_gate"][0]
    s_idx = np.arange(128)
    same = (s_idx[:, None] // 64) == (s_idx[None, :] // 64)
    m1 = np.where(same & (s_idx[:, None] > s_idx[None, :]), -1.0 / 16.0, 0.0).astype(np.float32)
    ind = np.zeros((128, 2), np.float32)
    ind[0:64, 0] = -1.0 / 16.0
    ind[64:128, 1] = -1.0 / 16.0
    common = {
        "ffn1_w_gate": g["ffn1_w_gate"][0], "ffn1_w_up": g["ffn1_w_up"][0], "ffn1_w_down": g["ffn1_w_down"][0],
        "w_in": g["w_in"][0], "w_out": g["w_out"][0], "cross_wq": g["cross_wq"][0], "cross_wkv": g["cross_wkv"][0],
        "cross_wo": g["cross_wo"][0], "ffn2_w_gate": g["ffn2_w_gate"][0], "ffn2_w_up": g["ffn2_w_up"][0],
        "ffn2_w_down": g["ffn2_w_down"][0],
        "lnp": lnp, "lamv": lamv, "dng4": dng4, "gng4": gng4, "w2aug": w2aug,
        "identf": np.eye(128, dtype=np.float32), "identb": np.eye(128, dtype=np.float32).astype(ml_dtypes.bfloat16),
        "m1": m1, "ind": ind, "onesb": np.full((128, 128), 1.0 / 1024.0, np.float32).astype(ml_dtypes.bfloat16),
    }
    maps = []
    for c in cores:
        m = dict(common)
        m["x"] = g["x"][c]
        m["mem"] = g["mem"][c]
        maps.append(m)
    return maps


_CACHE = {}


def kernel(**inputs):
    if "nc" not in _CACHE:
        _CACHE["nc"] = build_program()
    nc = _CACHE["nc"]
    cores = list(range(8))
    in_maps = make_in_maps(inputs, cores)
    res = run_bass_kernel_spmd(nc, in_maps, core_ids=cores)
    out = np.stack([np.asarray(r["out"], dtype=np.float32) for r in res.results], axis=0)
    return out
```

```python
import contextlib
import numpy as np
import ml_dtypes
import concourse.bass as bass
import concourse.mybir as mybir
from concourse.bass_utils import run_bass_kernel_spmd

F32 = mybir.dt.float32
BF16 = mybir.dt.bfloat16
AF = mybir.ActivationFunctionType
ALU = mybir.AluOpType
AX = mybir.AxisListType

NT = 8
TT = 512
SEQ = 4096
D = 1024
FH = 2816
NFC = 22
INW = 3088
ALPHA = 2.0 ** 0.25
EPS = 1e-5
LAMBDA_INIT = 0.8 - 0.6 * 1.0
NR = 4
SLOT = 4096


class Res:
    __slots__ = ("name", "w", "rs", "excl")

    def __init__(self, name, excl=False):
        self.name = name
        self.w = None
        self.rs = []
        self.excl = excl


class Op:
    __slots__ = ("eng", "fn", "deps", "dma", "lane", "needs", "idx", "laneval", "waits", "phase")

    def __init__(self, eng, fn, dma, lane):
        self.phase = None
        self.eng = eng
        self.fn = fn
        self.dma = dma
        self.lane = lane
        self.deps = []
        self.needs = False
        self.idx = None
        self.laneval = None
        self.waits = []


ENGS = ("pe", "act", "dve", "pool", "sp")
NROT = 4


class Sched:
    def __init__(self):
        self.ops = {e: [] for e in ENGS}
        self.lane_tok = {}
        self.phase = ""

    def add(self, eng, fn, reads=(), writes=(), lane=None):
        dma = lane is not None
        op = Op(eng, fn, dma, lane)
        op.phase = self.phase
        reads = list(reads)
        writes = list(writes)
        if dma:
            if lane not in self.lane_tok:
                self.lane_tok[lane] = Res("lane_" + str(lane))
            writes.append(self.lane_tok[lane])
        raw = set()
        other = set()
        for r in reads:
            if r.w is not None:
                raw.add(r.w)
            if r.excl:
                for q in r.rs:
                    if q.eng != eng:
                        other.add(q)
        for w in writes:
            if w.w is not None:
                other.add(w.w)
            for q in w.rs:
                other.add(q)
        deps = set()
        for d in raw | other:
            if d is op:
                continue
            if (not d.dma) and (not dma) and d.eng == eng:
                if eng == "pe":
                    continue
                if d not in raw:
                    continue
            deps.add(d)
        op.deps = list(deps)
        for d in deps:
            d.needs = True
        for r in reads:
            if r.excl:
                r.rs = [op]
                continue
            if not dma:
                r.rs = [q for q in r.rs if q.dma or q.eng != eng]
            r.rs.append(op)
        for w in writes:
            w.w = op
            w.rs = []
        self.ops[eng].append(op)
        return op

    def finalize(self):
        lanecnt = {}
        for e in ENGS:
            c = 0
            for op in self.ops[e]:
                if op.dma:
                    lanecnt[op.lane] = lanecnt.get(op.lane, 0) + 1
                    op.laneval = 16 * lanecnt[op.lane]
                elif op.needs:
                    c += 1
                    op.idx = c
        self.lanecnt = lanecnt
        for e in ENGS:
            waited = {}
            for op in self.ops[e]:
                best = {}
                for d in op.deps:
                    if d.dma:
                        key = ("lane", d.lane)
                        val = d.laneval
                    else:
                        key = ("eng", d.eng)
                        val = d.idx
                    if waited.get(key, 0) >= val:
                        continue
                    if best.get(key, 0) < val:
                        best[key] = val
                for key, val in best.items():
                    waited[key] = val
                    op.waits.append((key, val))

    def emit(self, nc, stack, final_lanes=()):
        sems = {}
        for e in ENGS:
            for r in range(NROT):
                sems[("eng", e, r)] = stack.enter_context(nc.semaphore("s_%s%d" % (e, r)))
        for lane in self.lanecnt:
            sems[("lane", lane)] = stack.enter_context(nc.semaphore("l_%s" % str(lane)))
        block = stack.enter_context(nc.Block())
        names = {"pe": "tensor", "act": "scalar", "dve": "vector", "pool": "gpsimd", "sp": "sync"}

        def run(e, eng):
            for op in self.ops[e]:
                for key, val in op.waits:
                    if key[0] == "lane":
                        eng.wait_ge(sems[key], val)
                    else:
                        j = val - 1
                        eng.wait_ge(sems[("eng", key[1], j % NROT)], j // NROT + 1)
                inst = op.fn(eng)
                if op.dma:
                    inst.then_inc(sems[("lane", op.lane)], 16)
                elif op.idx is not None:
                    j = op.idx - 1
                    inst.then_inc(sems[("eng", e, j % NROT)], 1)
            if e == "sp":
                for lane in self.lanecnt:
                    eng.wait_ge(sems[("lane", lane)], 16 * self.lanecnt[lane])

        for e in ENGS:
            getattr(block, names[e])(lambda eng, e=e: run(e, eng))


def build_program(nt=NT, dbg=False, stop_after=None, unit_limit=None, skip_lastx=False, xmode=0):
    nc = bass.Bass("TRN2", target_bir_lowering=False, dynamic_dma_scratch_size=8192)
    S = Sched()

    def din(name, shape, dt=F32):
        return nc.dram_tensor(name, list(shape), dt, kind="ExternalInput").ap()

    x_d = din("x", [SEQ, D])
    mem_d = din("mem", [256, D])
    wsrc = {
        "wg1": din("ffn1_w_gate", [D, FH]), "wu1": din("ffn1_w_up", [D, FH]), "wd1": din("ffn1_w_down", [FH, D]),
        "win": din("w_in", [D, INW]), "wout": din("w_out", [D, D]), "wq": din("cross_wq", [D, D]),
        "wkv": din("cross_wkv", [D, 2 * D]), "wo": din("cross_wo", [D, D]),
        "wg2": din("ffn2_w_gate", [D, FH]), "wu2": din("ffn2_w_up", [D, FH]), "wd2": din("ffn2_w_down", [FH, D]),
    }
    lnp_d = din("lnp", [128, 64])
    lamv_d = din("lamv", [128, 256])
    dng_d = din("dng4", [128, 512])
    gng_d = din("gng4", [128, 512])
    w2a_d = din("w2aug", [32, 256])
    identf_d = din("identf", [128, 128])
    identb_d = din("identb", [128, 128], BF16)
    m1_d = din("m1", [128, 128])
    ind_d = din("ind", [128, 2])
    onesb_d = din("onesb", [128, 128], BF16)
    out_d = nc.dram_tensor("out", [SEQ, D], F32, kind="ExternalOutput").ap()
    wbf = {k: nc.dram_tensor("bf_" + k, list(v.shape), BF16, kind="Internal").ap() for k, v in wsrc.items()}
    if dbg:
        dbg_d = nc.dram_tensor("dbg", [128, 8, 512], F32, kind="ExternalOutput").ap()
        dbg2_d = nc.dram_tensor("dbg2", [128, 8, 512], BF16, kind="ExternalOutput").ap()

    with contextlib.ExitStack() as st:
        def sb(n, s, d):
            return st.enter_context(nc.sbuf_tensor(n, list(s), d))

        Kc = sb("Kc", [128, 4, SEQ], BF16)
        Vc = sb("Vc", [128, 32, 4, 129], BF16)
        ring = [sb("ring%d" % i, [128, SLOT], BF16) for i in range(NR)]
        RT = sb("RT", [128, 8, TT], F32)
        hT = sb("hT", [128, 8, TT], BF16)
        HT = sb("HT", [128, NFC, TT], BF16)
        tokbuf = sb("tokbuf", [128, 2048], F32)
        qdz = [sb("qdz%d" % i, [128, 4, 512], BF16) for i in range(2)]
        xst = [sb("xst%d" % i, [128, 1024], F32) for i in range(2)]
        sg = [sb("sg%d" % i, [128, 512], F32) for i in range(2)]
        PT = [sb("PT%d" % i, [128, 512], BF16) for i in range(4)]
        zsq = [sb("zsq%d" % i, [128, 512], BF16) for i in range(2)]
        lnt = sb("lnt", [128, 512], F32)
        Sst = sb("Sst", [64, 512], F32)
        Sbf = [sb("Sbf%d" % i, [64, 512], BF16) for i in range(2)]
        glr = sb("glr", [32, 512], F32)
        w2a = sb("w2a", [32, 256], F32)
        winglr = sb("winglr", [128, 8, 16], BF16)
        identf = sb("identf_s", [128, 128], F32)
        identb = sb("identb_s", [128, 128], BF16)
        m1 = sb("m1_s", [128, 128], F32)
        ind = sb("ind_s", [128, 2], F32)
        onesb = sb("onesb_s", [128, 128], BF16)
        lnp = sb("lnp_s", [128, 64], F32)
        lnpa = sb("lnpa_s", [128, 64], F32)
        lamv = sb("lamv_s", [128, 256], F32)
        lamt = sb("lamt", [128, 64], F32)
        lams = sb("lams", [128, 8], F32)
        gda4 = sb("gda4", [128, 512], F32)
        gng4 = sb("gng4_s", [128, 512], F32)
        kmT = sb("kmT", [128, 8, 256], BF16)
        Vm = sb("Vm", [128, 2, 4, 257], BF16)
        la = [sb("la%d" % i, [128, 256], F32) for i in range(2)]
        ed = [sb("ed%d" % i, [128, 256], F32) for i in range(2)]
        dec = sb("dec", [64, 4, 8], F32)
        o0 = sb("o0", [128, 4, 128], F32)
        od = sb("od", [128, 4, 128], F32)
        sm = sb("sm", [128, 32], F32)
        banks = [st.enter_context(nc.psum_tensor("pb%d" % i, [128, 512], F32)) for i in range(8)]
        bres = [Res("pb%d" % i, excl=True) for i in range(8)]
        free_banks = list(range(8))

        def take():
            return free_banks.pop(0)

        def free(b):
            free_banks.append(b)

        scr = HT[:].rearrange("p a b -> p (a b)")
        def scr_bf(g0, ng, shape_str=None, **kw):
            v = scr[:, g0 * 512:(g0 + ng) * 512]
            return v.rearrange(shape_str, **kw) if shape_str else v

        def scr_f32(g0, ng, shape_str=None, **kw):
            v = scr[:, g0 * 512:(g0 + ng) * 512].bitcast(F32)
            return v.rearrange(shape_str, **kw) if shape_str else v

        qdT = scr_bf(0, 4, "p (a b) -> p a b", b=512)
        qgT = scr_bf(4, 4, "p (a b) -> p a b", b=512)
        kg_sb = scr_f32(8, 4, "p (a b) -> p a b", b=256)
        vg_sb = scr_bf(12, 4, "p (a b) -> p a b", b=512)
        gate = scr_bf(16, 4, "p (a b) -> p a b", b=512)
        kend = scr_bf(20, 2, "p (a b) -> p a b", b=256)
        qcT = scr_bf(0, 8, "p (a b) -> p a b", b=512)
        PTc = scr_bf(8, 8, "p (a b) -> p a b", b=512)
        tok = [tokbuf[:, i * 1024:(i + 1) * 1024] for i in range(2)]
        ocat = tokbuf[:].bitcast(BF16).rearrange("p (a b) -> p a b", b=1024)

        R = {}
        def res(name):
            if name not in R:
                R[name] = Res(name)
            return R[name]
        rRT = [res("RT%d" % k) for k in range(8)]
        rhT = [res("hT%d" % k) for k in range(8)]
        rG = [res("G%d" % g) for g in range(NFC)]
        rring = [res("ring%d" % i) for i in range(NR)]
        rK = [[res("K%d_%d" % (h, t)) for t in range(NT)] for h in range(4)]
        rV = [res("V%d" % k) for k in range(32)]
        rocat = [res("ocat%d" % s) for s in range(4)]
        rtok = [[rocat[0], rocat[1]], [rocat[2], rocat[3]]]
        rsg = [res("sg%d" % i) for i in range(2)]
        rqd = [[res("qd%d_%d" % (c, h)) for h in range(4)] for c in range(2)]
        rxst = [res("xst%d" % i) for i in range(2)]
        rPT = [res("PT%d" % i) for i in range(4)]
        rzsq = [res("zsq%d" % i) for i in range(2)]
        rSbf = [res("Sbf%d" % i) for i in range(2)]
        rla = [res("la%d" % i) for i in range(2)]
        red = [res("ed%d" % i) for i in range(2)]

        def gr(g0, ng):
            return rG[g0:g0 + ng]

        def MM(out, lhsT, rhs, start, stop, r, w, skip=False):
            if skip:
                return S.add("pe", lambda e: e.matmul(out, lhsT=lhsT, rhs=rhs, start=start, stop=stop, skip_group_check=True), r, w)
            return S.add("pe", lambda e: e.matmul(out, lhsT=lhsT, rhs=rhs, start=start, stop=stop), r, w)

        def TR(out, in_, ident, r, w):
            return S.add("pe", lambda e: e.transpose(out=out, in_=in_, identity=ident), r, w)

        def A(out, in_, func, r, w, scale=None, bias=None):
            kw = {}
            if scale is not None:
                kw["scale"] = scale
            if bias is not None:
                kw["bias"] = bias
            return S.add("act", lambda e: e.activation(out=out, in_=in_, func=func, **kw), r, w)

        def TT_(eng, out, in0, in1, op, r, w):
            return S.add(eng, lambda e: e.tensor_tensor(out=out, in0=in0, in1=in1, op=op), r, w)

        def TS(eng, out, in0, s1, s2, op0, op1, r, w):
            if op1 is None:
                return S.add(eng, lambda e: e.tensor_scalar(out=out, in0=in0, scalar1=s1, scalar2=None, op0=op0), r, w)
            return S.add(eng, lambda e: e.tensor_scalar(out=out, in0=in0, scalar1=s1, scalar2=s2, op0=op0, op1=op1), r, w)

        def STT(out, in0, scalar, in1, op0, op1, r, w):
            return S.add("dve", lambda e: e.scalar_tensor_tensor(out=out, in0=in0, scalar=scalar, in1=in1, op0=op0, op1=op1), r, w)

        def CP(eng, out, in_, r, w):
            return S.add(eng, lambda e: e.tensor_copy(out=out, in_=in_), r, w)

        def RECIP(out, in_, r, w):
            return S.add("dve", lambda e: e.reciprocal(out=out, in_=in_), r, w)

        def RED(out, in_, r, w):
            return S.add("dve", lambda e: e.tensor_reduce(out=out, in_=in_, axis=AX.X, op=ALU.add), r, w)

        def MEMSET(eng, ap, val, r, w):
            return S.add(eng, lambda e: e.memset(ap, val), r, w)

        def DMA(eng, out, in_, r, w, lane):
            return S.add(eng, lambda e: e.dma_start(out=out, in_=in_), r, w, lane=lane)

        cnt = {"lane": 0, "cast": 0}

        def const_load(dst, src, r):
            cnt["lane"] += 1
            DMA("act", dst, src, [], [r], "k%d" % cnt["lane"])

        const_load(identf[:], identf_d, res("identf"))
        const_load(identb[:], identb_d, res("identb"))
        const_load(onesb[:], onesb_d, res("onesb"))
        const_load(lnp[:], lnp_d, res("lnp"))
        const_load(lamv[:], lamv_d, res("lamv"))
        const_load(gda4[:], dng_d, res("gda4"))
        const_load(gng4[:], gng_d, res("gng4"))
        const_load(w2a[:], w2a_d, res("w2a"))
        const_load(m1[:], m1_d, res("m1"))
        const_load(ind[:], ind_d, res("ind"))

        rW = {}

        def cast(name, c0, c1):
            key = (name, c0)
            rW[key] = Res("w_%s_%d" % key)
            nrows = wsrc[name].shape[0]
            nsplit = 2 if nrows > 1024 else 1
            rstep = nrows // nsplit
            for rs_ in range(nsplit):
                lane = "cast%d" % (cnt["cast"] % 4)
                cnt["cast"] += 1
                DMA("pool", wbf[name][rs_ * rstep:(rs_ + 1) * rstep, c0:c1], wsrc[name][rs_ * rstep:(rs_ + 1) * rstep, c0:c1], [], [rW[key]], lane)

        def cast_cols(name, ncols, step=512):
            for c0 in range(0, ncols, step):
                cast(name, c0, min(ncols, c0 + step))

        def cast_ffn(i):
            for c0 in range(0, FH, 512):
                cast("wg%d" % i, c0, min(FH, c0 + 512))
                cast("wu%d" % i, c0, min(FH, c0 + 512))
            cast_cols("wd%d" % i, D)

        cast_ffn(1)
        cast_cols("win", 3072)
        cast("win", 3072, 3088)
        cast_cols("wout", D)
        cast_cols("wkv", 2 * D)
        cast_cols("wq", D)
        cast_cols("wo", D)
        cast_ffn(2)

        units = []

        def u_k8(name, c0, ncols):
            units.append((name, c0, ncols, 0, 8))

        def u_ffn(i):
            for c0 in range(0, FH, 512):
                u_k8("wg%d" % i, c0, min(512, FH - c0))
                u_k8("wu%d" % i, c0, min(512, FH - c0))
            for grp in range(2):
                for part in range(3):
                    nf = 8 if part < 2 else 6
                    units.append(("wd%d" % i, grp * 512, 512, part * 8, nf))

        for t in range(nt):
            u_ffn(1)
            for c0 in range(0, 3072, 512):
                u_k8("win", c0, 512)
            u_k8("wout", 0, 512)
            u_k8("wout", 512, 512)
            if t == 0:
                for c0 in range(0, 2 * D, 512):
                    u_k8("wkv", c0, 512)
            for nm in ("wq", "wo"):
                u_k8(nm, 0, 512)
                u_k8(nm, 512, 512)
            u_ffn(2)

        ws = {"issued": 0, "acq": 0, "rel": 0}

        def ws_issue():
            i = ws["issued"]
            if i >= len(units) or (unit_limit is not None and i >= unit_limit):
                return
            name, c0, ncols, f0, nf = units[i]
            slot = i % NR
            src = wbf[name].rearrange("(kc p) n -> p kc n", p=128)[:, f0:f0 + nf, c0:c0 + ncols]
            dst = ring[slot][:, 0:nf * ncols].rearrange("p (a b) -> p a b", b=ncols)
            rr = [rW[(name, c)] for c in range((c0 // 512) * 512, c0 + ncols, 512)]
            DMA("sp", dst, src, rr, [rring[slot]], "ring%d" % slot)
            ws["issued"] += 1

        def acquire():
            i = ws["acq"]
            assert i < ws["issued"], "ring underflow"
            name, c0, ncols, f0, nf = units[i]
            slot = i % NR
            ws["acq"] += 1
            view = ring[slot][:, 0:nf * ncols].rearrange("p (a b) -> p a b", b=ncols)
            return view, rring[slot], units[i]

        def release():
            ws["rel"] += 1
            ws_issue()

        for _ in range(NR):
            ws_issue()


        rlnpa = res("lnpa")
        TS("dve", lnpa[:, 0:48], lnp[:, 0:48], ALPHA, None, ALU.mult, None, [res("lnp")], [rlnpa])
        CP("dve", lnpa[:, 48:64], lnp[:, 48:64], [res("lnp")], [rlnpa])
        rlam = res("lam")
        TT_("dve", lamt[:, 0:64], lamv[:, 0:64], lamv[:, 64:128], ALU.mult, [res("lamv")], [res("lamt")])
        RED(lams[:, 0:1], lamt[:, 0:64], [res("lamt")], [res("lams0")])
        TT_("dve", lamt[:, 0:64], lamv[:, 128:192], lamv[:, 192:256], ALU.mult, [res("lamv"), res("lams0")], [res("lamt")])
        RED(lams[:, 1:2], lamt[:, 0:64], [res("lamt")], [res("lams1")])
        A(lams[:, 2:4], lams[:, 0:2], AF.Exp, [res("lams0"), res("lams1")], [res("lams2")])
        TT_("dve", lams[:, 4:5], lams[:, 3:4], lams[:, 2:3], ALU.subtract, [res("lams2")], [res("lams4")])
        TS("dve", lams[:, 5:6], lams[:, 4:5], -LAMBDA_INIT, None, ALU.add, None, [res("lams4")], [rlam])
        neglam = lams[:, 5:6]
        TS("dve", gda4[:], gda4[:], 1.0 - LAMBDA_INIT, None, ALU.mult, None, [res("gda4")], [res("gda4")])
        MEMSET("pool", Vc[:, :, :, 128:129], 1.0, [], rV)
        MEMSET("pool", Vm[:, :, :, 256:257], 1.0, [], [res("Vm")])
        MEMSET("pool", glr[:], 1.0, [], [res("glr")])
        MEMSET("pool", qdz[0][64:128, :, :], 0.0, [], rqd[0])
        MEMSET("pool", qdz[1][0:64, :, :], 0.0, [], rqd[1])
        rSst = [res("Sst%d" % h) for h in range(4)]
        MEMSET("pool", Sst[:], 0.0, [], rSst)

        def transpose_in(src_tok, r_src, sub, scale_rt, xm=0):
            cs = slice(sub * 128, (sub + 1) * 128)
            for g in range(2):
                b = take()
                for j in range(4):
                    kc = g * 4 + j
                    TR(banks[b][:, j * 128:(j + 1) * 128], src_tok[:, kc * 128:(kc + 1) * 128], identf[:],
                       r_src + [res("identf")], [bres[b]])
                bv = banks[b][:].rearrange("p (a c) -> p a c", c=128)
                if scale_rt is not None:
                    A(RT[:, g * 4:(g + 1) * 4, cs], bv, AF.Copy, [bres[b]], rRT[g * 4:(g + 1) * 4], scale=scale_rt)
                    TS("dve", hT[:, g * 4:(g + 1) * 4, cs], RT[:, g * 4:(g + 1) * 4, cs], 1.0 / scale_rt, None, ALU.mult, None,
                       rRT[g * 4:(g + 1) * 4], rhT[g * 4:(g + 1) * 4])
                else:
                    CP("dve", hT[:, g * 4:(g + 1) * 4, cs], bv, [bres[b]], rhT[g * 4:(g + 1) * 4])
                free(b)

        def transpose_ocat():
            for sub in range(4):
                cs = slice(sub * 128, (sub + 1) * 128)
                for g in range(2):
                    b = take()
                    bvb = banks[b][:].bitcast(BF16)
                    for j in range(4):
                        kc = g * 4 + j
                        TR(bvb[:, j * 128:(j + 1) * 128], ocat[:, sub, kc * 128:(kc + 1) * 128], identb[:],
                           [rocat[sub], res("identb")], [bres[b]])
                    bv = bvb[:, 0:512].rearrange("p (a c) -> p a c", c=128)
                    if (sub + g) % 2 == 0:
                        CP("dve", hT[:, g * 4:(g + 1) * 4, cs], bv, [bres[b]], rhT[g * 4:(g + 1) * 4])
                    else:
                        A(hT[:, g * 4:(g + 1) * 4, cs], bv, AF.Copy, [bres[b]], rhT[g * 4:(g + 1) * 4])
                    free(b)

        st_ = {"zi": 0}

        def evac_residual(b, kc, factor):
            STT(RT[:, kc, :], banks[b][:], factor, RT[:, kc, :], ALU.mult, ALU.add, [bres[b], rRT[kc]], [rRT[kc]])
            CP("dve", hT[:, kc, :], RT[:, kc, :], [rRT[kc]], [rhT[kc]])
            zi = st_["zi"] % 2
            st_["zi"] += 1
            A(zsq[zi][:], RT[:, kc, :], AF.Square, [rRT[kc]], [rzsq[zi]])
            return zi

        def stats_mm(kc, zi, bM, bQ):
            MM(banks[bM][:], onesb[:], hT[:, kc, :], kc == 0, kc == 7, [res("onesb"), rhT[kc]], [bres[bM]])
            MM(banks[bQ][:], onesb[:], zsq[zi][:], kc == 0, kc == 7, [res("onesb"), rzsq[zi]], [bres[bQ]])

        def layer_norm(li, bM, bQ, final=False):
            rl = res("lnt")
            A(lnt[:], banks[bM][:], AF.Square, [bres[bM]], [rl])
            TT_("dve", lnt[:], banks[bQ][:], lnt[:], ALU.subtract, [bres[bQ], rl], [rl])
            A(lnt[:], lnt[:], AF.Ln, [rl], [rl], bias=EPS)
            A(banks[bQ][:], lnt[:], AF.Exp, [rl], [bres[bQ]], scale=-0.5)
            gi, bi = 2 * li, 2 * li + 1
            for kc in range(8):
                TT_("dve", RT[:, kc, :], RT[:, kc, :], banks[bM][:], ALU.subtract, [rRT[kc], bres[bM]], [rRT[kc]])
                TT_("dve", RT[:, kc, :], RT[:, kc, :], banks[bQ][:], ALU.mult, [rRT[kc], bres[bQ]], [rRT[kc]])
                if not final:
                    A(hT[:, kc, :], RT[:, kc, :], AF.Identity, [rRT[kc], res("lnp")], [rhT[kc]],
                      scale=lnp[:, gi * 8 + kc:gi * 8 + kc + 1], bias=lnp[:, bi * 8 + kc:bi * 8 + kc + 1])
                A(RT[:, kc, :], RT[:, kc, :], AF.Identity, [rRT[kc], rlnpa], [rRT[kc]],
                  scale=lnpa[:, gi * 8 + kc:gi * 8 + kc + 1], bias=lnpa[:, bi * 8 + kc:bi * 8 + kc + 1])
            free(bQ)
            free(bM)

        def proj_residual_ln(n_k, rhs_fn, r_rhs_fn, factor, li, final=False, wd=False):
            grp_banks = []
            for grp in range(2):
                bs = [take() for _ in range(4)]
                grp_banks.append(bs)
                nparts = 3 if wd else 1
                for part in range(nparts):
                    view, rs, u = acquire()
                    nf = u[4]
                    for dmc in range(4):
                        for fl in range(nf):
                            f = part * 8 + fl
                            MM(banks[bs[dmc]][:], view[:, fl, dmc * 128:(dmc + 1) * 128], rhs_fn(f), f == 0, f == n_k - 1,
                               [rs] + r_rhs_fn(f), [bres[bs[dmc]]])
                    release()
            bM = bQ = None
            pend = []
            for grp in range(2):
                for dmc in range(4):
                    kc = grp * 4 + dmc
                    zi = evac_residual(grp_banks[grp][dmc], kc, factor)
                    free(grp_banks[grp][dmc])
                    pend.append((kc, zi))
                    if kc == 0:
                        continue
                    if bM is None:
                        bM = take()
                        bQ = take()
                    for (k2, z2) in pend:
                        stats_mm(k2, z2, bM, bQ)
                    pend = []
            layer_norm(li, bM, bQ, final)

        def ffn(li):
            si = 0
            for fg in range(6):
                vg_, rg_, ug_ = acquire()
                vu_, ru_, uu_ = acquire()
                ncols = ug_[2]
                for j in range(ncols // 128):
                    fc = fg * 4 + j
                    bG = take()
                    bU = take()
                    for kc in range(8):
                        MM(banks[bG][:], vg_[:, kc, j * 128:(j + 1) * 128], hT[:, kc, :], kc == 0, kc == 7, [rg_, rhT[kc]], [bres[bG]])
                    for kc in range(8):
                        MM(banks[bU][:], vu_[:, kc, j * 128:(j + 1) * 128], hT[:, kc, :], kc == 0, kc == 7, [ru_, rhT[kc]], [bres[bU]])
                    i = si % 2
                    si += 1
                    A(sg[i][:], banks[bG][:], AF.Silu, [bres[bG]], [rsg[i]])
                    TT_("dve", HT[:, fc, :], sg[i][:], banks[bU][:], ALU.mult, [rsg[i], bres[bU]], [rG[fc]])
                    free(bG)
                    free(bU)
                release()
                release()
            proj_residual_ln(NFC, lambda f: HT[:, f, :], lambda f: [rG[f]], 0.5, li, final=(li == 3), wd=True)

        def proj_fm(n_chunks, lhs_cols, evac):
            view, rs, u = acquire()
            for c in range(n_chunks):
                b = take()
                for kc in range(8):
                    MM(banks[b][0:lhs_cols, :], view[:, kc, c * lhs_cols:(c + 1) * lhs_cols], hT[:, kc, :], kc == 0, kc == 7,
                       [rs, rhT[kc]], [bres[b]])
                evac(c, b)
                free(b)
            return view, rs

        def proj_tm(view, rs, c0, ncols, evac):
            for sub in range(4):
                b = take()
                for kc in range(8):
                    MM(banks[b][:, 0:ncols], hT[:, kc, sub * 128:(sub + 1) * 128], view[:, kc, c0:c0 + ncols], kc == 0, kc == 7,
                       [rs, rhT[kc]], [bres[b]])
                evac(sub, b)
                free(b)

        def rms_rs(ss_ap, r_in, n):
            A(sm[:, 16:20], ss_ap, AF.Ln, r_in, [res("sm_ln")], scale=1.0 / n, bias=EPS)
            A(sm[:, 20:24], sm[:, 16:20], AF.Exp, [res("sm_ln")], [res("sm_rs")], scale=-0.5)

        memT = scr_bf(16, 4, "p (a b) -> p a b", b=256)

        def mem_kv():
            for ms in range(2):
                DMA("sp", tok[ms], mem_d[ms * 128:(ms + 1) * 128, :], [], rtok[ms], "tok%d" % ms)
            for ms in range(2):
                cs = slice(ms * 128, (ms + 1) * 128)
                for g in range(2):
                    b = take()
                    for j in range(4):
                        kc = g * 4 + j
                        TR(banks[b][:, j * 128:(j + 1) * 128], tok[ms][:, kc * 128:(kc + 1) * 128], identf[:],
                           rtok[ms] + [res("identf")], [bres[b]])
                    CP("dve", memT[:, g * 4:(g + 1) * 4, cs], banks[b][:].rearrange("p (a c) -> p a c", c=128), [bres[b]], gr(16, 4))
                    free(b)
            for uu in range(2):
                view, rs, u = acquire()
                for c in range(4):
                    ch = uu * 4 + c
                    b = take()
                    for kc in range(8):
                        MM(banks[b][:, 0:256], view[:, kc, c * 128:(c + 1) * 128], memT[:, kc, :], kc == 0, kc == 7, [rs] + gr(16, 4), [bres[b]])
                    A(kmT[:, ch, :], banks[b][:, 0:256], AF.Copy, [bres[b]], [res("kmT")])
                    free(b)
                release()
            for uu in range(2):
                view, rs, u = acquire()
                for ms in range(2):
                    b = take()
                    for kc in range(8):
                        MM(banks[b][:], memT[:, kc, ms * 128:(ms + 1) * 128], view[:, kc, :], kc == 0, kc == 7, [rs] + gr(16, 4), [bres[b]])
                    CP("dve", Vm[:, ms, uu * 2:(uu + 1) * 2, 0:256], banks[b][:].rearrange("p (a c) -> p a c", c=256), [bres[b]], [res("Vm")])
                    free(b)
                release()

        def x_dma(t, sub):
            i = sub % 2
            r0 = t * TT + sub * 128
            DMA("sp", xst[i][:], x_d[r0:r0 + 128, :], [], [rxst[i]], "xst%d" % i)

        RTn = scr_f32(0, 16, "p (a b) -> p a b", b=512)

        def load_x(t, pre=False):
            if t == 0:
                x_dma(0, 0)
                x_dma(0, 1)
            for sub in range(4):
                i = sub % 2
                cs = slice(sub * 128, (sub + 1) * 128)
                for g in range(2):
                    b = take()
                    for j in range(4):
                        kc = g * 4 + j
                        TR(banks[b][:, j * 128:(j + 1) * 128], xst[i][:, kc * 128:(kc + 1) * 128], identf[:], [rxst[i], res("identf")], [bres[b]])
                    bv = banks[b][:].rearrange("p (a c) -> p a c", c=128)
                    if pre:
                        dst, rdst = RTn[:, g * 4:(g + 1) * 4, cs], gr(8 * g, 8)
                    else:
                        dst, rdst = RT[:, g * 4:(g + 1) * 4, cs], rRT[g * 4:(g + 1) * 4]
                    A(dst, bv, AF.Copy, [bres[b]], rdst, scale=ALPHA)
                    TS("dve", hT[:, g * 4:(g + 1) * 4, cs], dst, 1.0 / ALPHA, None, ALU.mult, None, rdst, rhT[g * 4:(g + 1) * 4])
                    free(b)
                if sub + 2 < 4:
                    x_dma(t, sub + 2)
                elif t + 1 < nt:
                    x_dma(t + 1, sub - 2)

        def adopt_rt():
            for kc in range(8):
                CP("pool", RT[:, kc, :], RTn[:, kc, :], gr(2 * kc, 2), [rRT[kc]])

        def w_in_phase(t):
            def ev_q(h, b):
                A(qdz[0][0:64, h, :], banks[b][0:64, :], AF.Copy, [bres[b]], [rqd[0][h]])
                A(qdz[1][64:128, h, :], banks[b][64:128, :], AF.Copy, [bres[b]], [rqd[1][h]])
            proj_fm(4, 128, ev_q)
            release()
            proj_fm(4, 128, lambda h, b: CP("dve", Kc[:, h, t * TT:(t + 1) * TT], banks[b][:], [bres[b]], [rK[h][t]]))
            release()
            view, rs, u = acquire()
            proj_tm(view, rs, 0, 512, lambda sub, b: A(Vc[:, 4 * t + sub, :, 0:128], banks[b][:].rearrange("p (a c) -> p a c", c=128),
                                                       AF.Copy, [bres[b]], [rV[4 * t + sub]]))
            release()
            view, rs = proj_fm(4, 64, lambda h, b: A(qgT[0:64, h, :], banks[b][0:64, :], AF.Copy, [bres[b]], [rG[4 + h]], scale=0.125))
            proj_tm(view, rs, 256, 256, lambda sub, b: CP("dve", kg_sb[:, sub, :], banks[b][:, 0:256], [bres[b]], [rG[8 + sub]]))
            b = take()
            for kc in range(8):
                MM(banks[b][0:16, :], winglr[:, kc, :], hT[:, kc, :], kc == 0, kc == 7, [res("winglr"), rhT[kc]], [bres[b]])
            CP("dve", glr[0:16, :], banks[b][0:16, :], [bres[b]], [res("glr")])
            free(b)
            release()
            view, rs, u = acquire()
            proj_tm(view, rs, 0, 512, lambda sub, b: CP("dve", vg_sb[:, sub, :], banks[b][:], [bres[b]], [rG[12 + sub]]))
            release()
            view, rs, u = acquire()

            def ev_r(sub, b):
                i = sub % 2
                A(sg[i][:], banks[b][:], AF.Silu, [bres[b]], [rsg[i]])
                TT_("pool", gate[:, sub, :], sg[i][:], gng4[:], ALU.mult, [rsg[i], res("gng4")], [rG[16 + sub]])
            proj_tm(view, rs, 0, 512, ev_r)
            release()

        pti = [0]

        def da_phase(t):
            nkt = 4 * t + 4
            gen = gla_gen(t)
            state = {"done": False, "n": 0}
            per = max(1, (8 * nkt) // 30)

            def pump():
                state["n"] += 1
                if state["done"] or state["n"] % per != 0:
                    return
                try:
                    next(gen)
                except StopIteration:
                    state["done"] = True

            def qk(h, c, kt):
                j = kt - 4 * t
                q0 = 128 * j if j >= 0 else 0
                b = take()
                MM(banks[b][:, q0:512], Kc[:, h, kt * 128:(kt + 1) * 128], qdz[c][:, h, q0:512], True, True,
                   [rK[h][kt // 4], rqd[c][h]], [bres[b]])
                i = pti[0] % 4
                pti[0] += 1
                A(PT[i][:, q0:512], banks[b][:, q0:512], AF.Exp, [bres[b]], [rPT[i]], scale=0.125)
                if j >= 0:
                    MEMSET("pool", PT[i][64:128, q0:q0 + 64], 0.0, [], [rPT[i]])
                free(b)
                return i, j

            def acc_ap(accb, s):
                return banks[accb[s // 2]][:, (s % 2) * 256:(s % 2) * 256 + 129]

            def pv(h, kt, i, j, accb):
                for s in range(max(j, 0), 4):
                    MM(acc_ap(accb, s), PT[i][:, s * 128:(s + 1) * 128], Vc[:, kt, h, :], (kt == 0 and s % 2 == 0), kt == 4 * t + s,
                       [rPT[i], rV[kt]], [bres[accb[s // 2]]], skip=True)

            def epilogue(h, c, accb):
                rb = [bres[accb[0]], bres[accb[1]]]
                if c == 0:
                    for s in range(4):
                        RECIP(sm[:, s:s + 1], acc_ap(accb, s)[:, 128:129], [rb[s // 2]], [res("sm_r0_%d" % s)])
                        TS("dve", o0[:, s, :], acc_ap(accb, s)[:, 0:128], sm[:, s:s + 1], None, ALU.mult, None,
                           [rb[s // 2], res("sm_r0_%d" % s)], [res("o0")])
                else:
                    for s in range(4):
                        RECIP(sm[:, 4 + s:5 + s], acc_ap(accb, s)[:, 128:129], [rb[s // 2]], [res("sm_r1")])
                    TS("dve", sm[:, 8:12], sm[:, 4:8], neglam, None, ALU.mult, None, [res("sm_r1"), rlam], [res("sm_r1l")])
                    for s in range(4):
                        STT(od[:, s, :], acc_ap(accb, s)[:, 0:128], sm[:, 8 + s:9 + s], o0[:, s, :], ALU.mult, ALU.add,
                            [rb[s // 2], res("sm_r1l"), res("o0")], [res("od")])
                    A(sg[0][:], od[:].rearrange("p a b -> p (a b)"), AF.Square, [res("od")], [rsg[0]])
                    RED(sm[:, 12:16], sg[0][:].rearrange("p (a b) -> p a b", b=128), [rsg[0]], [res("sm_ss")])
                    rms_rs(sm[:, 12:16], [res("sm_ss")], 128.0)
                    TT_("dve", od[:], od[:], sm[:, 20:24].unsqueeze(2).to_broadcast([128, 4, 128]), ALU.mult,
                        [res("od"), res("sm_rs")], [res("od")])
                    TT_("pool", ocat[:, :, h * 128:(h + 1) * 128], od[:], gda4[:].rearrange("p (a b) -> p a b", b=128), ALU.mult,
                        [res("od"), res("gda4")], rocat)
                free(accb[0])
                free(accb[1])

            steps = [(h, c, kt) for h in range(4) for c in range(2) for kt in range(nkt)]
            DEPTH = 2
            pend = [qk(*steps[k]) for k in range(min(DEPTH, len(steps)))]
            accb = None
            for n, (h, c, kt) in enumerate(steps):
                cur = pend.pop(0)
                if n + DEPTH < len(steps):
                    pend.append(qk(*steps[n + DEPTH]))
                if kt == 0:
                    accb = [take(), take()]
                pv(h, kt, cur[0], cur[1], accb)
                if kt == nkt - 1:
                    epilogue(h, c, accb)
                pump()
            if not state["done"]:
                for _ in gen:
                    pass

        sbi = [0]

        def gla_sub_gen(t, sub):
            i = sub % 2
            cs = slice(sub * 128, (sub + 1) * 128)
            bz = take()
            MM(banks[bz][:, 0:256], glr[0:32, cs], w2a[0:32, :], True, True, [res("glr"), res("w2a")], [bres[bz]])
            A(la[i][:], banks[bz][:, 0:256], AF.Exp, [bres[bz]], [rla[i]], scale=-1.0)
            A(la[i][:], la[i][:], AF.Ln, [rla[i]], [rla[i]], bias=1.0)
            free(bz)
            yield
            bD = take()
            MM(banks[bD][:, 0:256], m1[:], la[i][:], True, True, [res("m1"), rla[i]], [bres[bD]])
            for h in range(4):
                MM(banks[bD][0:64, 256 + 2 * h:258 + 2 * h], la[i][:, h * 64:(h + 1) * 64], ind[:], True, True,
                   [res("ind"), rla[i]], [bres[bD]], skip=True)
            A(ed[i][:], banks[bD][:, 0:256], AF.Exp, [bres[bD]], [red[i]])
            A(dec[:, sub, :], banks[bD][0:64, 256:264], AF.Exp, [bres[bD]], [res("dec")])
            TT_("dve", kend[:, sub, :], kg_sb[:, sub, :], ed[i][:], ALU.mult, [rG[8 + sub], red[i]], [rG[20 + sub // 2]])
            free(bD)
            yield
            bO = take()
            for c in range(2):
                ps = slice(c * 64, (c + 1) * 64)
                bS = take()
                for h in range(4):
                    MM(banks[bS][0:64, h * 128:(h + 1) * 128], kend[ps, sub, h * 64:(h + 1) * 64], vg_sb[ps, sub, h * 128:(h + 1) * 128],
                       True, True, [rG[20 + sub // 2], rG[12 + sub]], [bres[bS]], skip=True)
                for h in range(4):
                    STT(Sst[:, h * 128:(h + 1) * 128], Sst[:, h * 128:(h + 1) * 128], dec[:, sub, 2 * h + c:2 * h + c + 1],
                        banks[bS][0:64, h * 128:(h + 1) * 128], ALU.mult, ALU.add, [rSst[h], res("dec"), bres[bS]], [rSst[h]])
                k = sbi[0] % 2
                sbi[0] += 1
                CP("pool", Sbf[k][:], Sst[:], rSst, [rSbf[k]])
                free(bS)
                yield
                for h in range(4):
                    MM(banks[bO][ps, h * 128:(h + 1) * 128], qgT[0:64, h, sub * 128 + c * 64:sub * 128 + (c + 1) * 64],
                       Sbf[k][:, h * 128:(h + 1) * 128], True, True, [rG[4 + h], rSbf[k]], [bres[bO]], skip=True)
                yield
            A(sg[1][:], banks[bO][:], AF.Square, [bres[bO]], [rsg[1]])
            RED(sm[:, 12:16], sg[1][:].rearrange("p (a b) -> p a b", b=128), [rsg[1]], [res("sm_ss")])
            rms_rs(sm[:, 12:16], [res("sm_ss")], 128.0)
            TT_("dve", sg[1][:].rearrange("p (a b) -> p a b", b=128), banks[bO][:].rearrange("p (a b) -> p a b", b=128),
                sm[:, 20:24].unsqueeze(2).to_broadcast([128, 4, 128]), ALU.mult, [bres[bO], res("sm_rs"), rsg[1]], [rsg[1]])
            TT_("pool", ocat[:, sub, 512:1024], sg[1][:], gate[:, sub, :], ALU.mult, [rsg[1], rG[16 + sub]], [rocat[sub]])
            free(bO)

        def gla_gen(t):
            for sub in range(4):
                yield from gla_sub_gen(t, sub)

        def gla_phase(t):
            for _ in gla_gen(t):
                pass

        rci = [0]

        def cross_phase():
            for uu in range(2):
                proj_fm(4, 128, lambda c, b, uu=uu: A(qcT[:, uu * 4 + c, :], banks[b][:], AF.Copy, [bres[b]], [rG[uu * 4 + c]]))
                release()
            def scores(hd):
                for ms in range(2):
                    b = take()
                    for j in range(2):
                        MM(banks[b][:], kmT[:, 2 * hd + j, ms * 128:(ms + 1) * 128], qcT[:, 2 * hd + j, :], j == 0, j == 1,
                           [res("kmT"), rG[2 * hd + j]], [bres[b]])
                    A(PTc[:, hd * 2 + ms, :], banks[b][:], AF.Exp, [bres[b]], [rG[8 + hd * 2 + ms]], scale=1.0 / 16.0)
                    free(b)

            def pvx(hd):
                for sub in range(4):
                    b = take()
                    for ms in range(2):
                        MM(banks[b][:, 0:257], PTc[:, hd * 2 + ms, sub * 128:(sub + 1) * 128], Vm[:, ms, hd, :], ms == 0, ms == 1,
                           [rG[8 + hd * 2 + ms], res("Vm")], [bres[b]])
                    k = 24 + (rci[0] % 8)
                    rci[0] += 1
                    RECIP(sm[:, k:k + 1], banks[b][:, 256:257], [bres[b]], [res("sm_rc%d" % k)])
                    A(ocat[:, sub, hd * 256:(hd + 1) * 256], banks[b][:, 0:256], AF.Identity, [bres[b], res("sm_rc%d" % k)], [rocat[sub]],
                      scale=sm[:, k:k + 1])
                    free(b)

            scores(0)
            scores(1)
            for hd in range(4):
                pvx(hd)
                if hd + 2 < 4:
                    scores(hd + 2)

        def store_out(t):
            for sub in range(4):
                i = sub % 2
                for g in range(2):
                    b = take()
                    for j in range(4):
                        kc = g * 4 + j
                        TR(banks[b][:, j * 128:(j + 1) * 128], RT[:, kc, sub * 128:(sub + 1) * 128], identf[:], [rRT[kc], res("identf")], [bres[b]])
                    if g == 0:
                        A(tok[i][:, g * 512:(g + 1) * 512], banks[b][:], AF.Copy, [bres[b]], rtok[i])
                    else:
                        CP("dve", tok[i][:, g * 512:(g + 1) * 512], banks[b][:], [bres[b]], rtok[i])
                    free(b)
                r0 = t * TT + sub * 128
                DMA("sp", out_d[r0:r0 + 128, :], tok[i], rtok[i], [], "tok%d" % i)

        for t in range(nt):
            S.phase = "t%d:x" % t
            if t == 0:
                load_x(0)
            if stop_after == "x" and t == nt - 1:
                break
            S.phase = "t%d:ffn1" % t
            ffn(0)
            if stop_after == "ln1" and t == nt - 1:
                break
            S.phase = "t%d:win" % t
            if t == 0:
                DMA("sp", winglr[:], wbf["win"].rearrange("(kc p) n -> p kc n", p=128)[:, :, 3072:3088],
                    [rW[("win", 3072)]], [res("winglr")], "k_winglr")
            w_in_phase(t)
            if stop_after == "win" and t == nt - 1:
                break
            S.phase = "t%d:da" % t
            da_phase(t)
            if stop_after == "da" and t == nt - 1:
                break
            if stop_after == "gla" and t == nt - 1:
                break
            S.phase = "t%d:wout" % t
            transpose_ocat()
            proj_residual_ln(8, lambda f: hT[:, f, :], lambda f: [rhT[f]], 1.0, 1)
            if stop_after == "ln2" and t == nt - 1:
                break
            if t == 0:
                S.phase = "memkv"
                mem_kv()
            S.phase = "t%d:cross" % t
            cross_phase()
            S.phase = "t%d:wo" % t
            transpose_ocat()
            proj_residual_ln(8, lambda f: hT[:, f, :], lambda f: [rhT[f]], 1.0, 2)
            if stop_after == "ln3" and t == nt - 1:
                break
            S.phase = "t%d:ffn2" % t
            ffn(3)
            if t + 1 < nt:
                S.phase = "t%d:x" % (t + 1)
                load_x(t + 1, pre=True)
            S.phase = "t%d:out" % t
            store_out(t)
            if t + 1 < nt:
                adopt_rt()
        if dbg:
            DMA("sp", dbg_d, RT[:], rRT, [], "dbg")
            DMA("sp", dbg2_d, hT[:], rhT, [], "dbg2")
        if stop_after is None:
            assert ws["acq"] == len(units) == ws["rel"], (ws, len(units))
        S.finalize()
        S.emit(nc, st, final_lanes=["tok0", "tok1", "dbg", "dbg2"])
    return nc


def make_in_maps(inputs, cores):
    f = lambda a: np.ascontiguousarray(np.asarray(a, dtype=np.float32))
    g = {k: f(v) for k, v in inputs.items()}
    lnp = np.zeros((128, 64), np.float32)
    for i, nm in enumerate(["ln1_g", "ln1_b", "ln2_g", "ln2_b", "ln3_g", "ln3_b", "ln4_g", "ln4_b"]):
        lnp[:, i * 8:(i + 1) * 8] = g[nm][0].reshape(8, 128).T
    lamv = np.concatenate([g["da_lambda_q1"][0], g["da_lambda_k1"][0], g["da_lambda_q2"][0], g["da_lambda_k2"][0]])
    lamv = np.ascontiguousarray(np.broadcast_to(lamv[None, :], (128, 256)))
    dng4 = np.ascontiguousarray(np.broadcast_to(np.tile(g["da_norm_g"][0], 4)[None, :], (128, 512)))
    gng4 = np.ascontiguousarray(np.broadcast_to(np.tile(g["gla_norm_g"][0], 4)[None, :], (128, 512)))
    w2aug = np.zeros((32, 256), np.float32)
    w2aug[0:16] = g["gla_w_gate2"][0]
    w2aug[16] = g["gla_b_gate"][0]
    s_idx = np.arange(128)
    same = (s_idx[:, None] // 64) == (s_idx[None, :] // 64)
    m1 = np.where(same & (s_idx[:, None] > s_idx[None, :]), -1.0 / 16.0, 0.0).astype(np.float32)
    ind = np.zeros((128, 2), np.float32)
    ind[0:64, 0] = -1.0 / 16.0
    ind[64:128, 1] = -1.0 / 16.0
    common = {
        "ffn1_w_gate": g["ffn1_w_gate"][0], "ffn1_w_up": g["ffn1_w_up"][0], "ffn1_w_down": g["ffn1_w_down"][0],
        "w_in": g["w_in"][0], "w_out": g["w_out"][0], "cross_wq": g["cross_wq"][0], "cross_wkv": g["cross_wkv"][0],
        "cross_wo": g["cross_wo"][0], "ffn2_w_gate": g["ffn2_w_gate"][0], "ffn2_w_up": g["ffn2_w_up"][0],
        "ffn2_w_down": g["ffn2_w_down"][0],
        "lnp": lnp, "lamv": lamv, "dng4": dng4, "gng4": gng4, "w2aug": w2aug,
        "identf": np.eye(128, dtype=np.float32), "identb": np.eye(128, dtype=np.float32).astype(ml_dtypes.bfloat16),
        "m1": m1, "ind": ind, "onesb": np.full((128, 128), 1.0 / 1024.0, np.float32).astype(ml_dtypes.bfloat16),
    }
    maps = []
    for c in cores:
        m = dict(common)
        m["x"] = g["x"][c]
        m["mem"] = g["mem"][c]
        maps.append(m)
    return maps


_CACHE = {}


def kernel(**inputs):
    if "nc" not in _CACHE:
        _CACHE["nc"] = build_program()
    nc = _CACHE["nc"]
    cores = list(range(8))
    in_maps = make_in_maps(inputs, cores)
    res = run_bass_kernel_spmd(nc, in_maps, core_ids=cores)
    out = np.stack([np.asarray(r["out"], dtype=np.float32) for r in res.results], axis=0)
    return out
```
